# Optimizing a Trainium2 kernel written in Bass

```python
import math
import jax, jax.numpy as jnp
from jax import lax
import numpy as np

D_MODEL = 2048
BATCH = 4
SEQ = 2048
DEPTH = 1

MEM_LEN = 256
DA_HEADS = 4
DA_HEAD_DIM = 128
DA_V_DIM = 2 * DA_HEAD_DIM
RET_HEADS = 4
RET_QK_DIM = 128
RET_V_DIM = 256
MIX_WIDTH = DA_HEADS * DA_V_DIM + RET_HEADS * RET_V_DIM
COL_SIZES = (
    DA_HEADS * 2 * DA_HEAD_DIM,
    DA_HEADS * 2 * DA_HEAD_DIM,
    DA_HEADS * DA_V_DIM,
    RET_HEADS * RET_QK_DIM,
    RET_HEADS * RET_QK_DIM,
    RET_HEADS * RET_V_DIM,
    RET_HEADS * RET_V_DIM,
)
IN_COLS = sum(COL_SIZES)
XATTN_HEADS = 4
XATTN_HEAD_DIM = D_MODEL // XATTN_HEADS
D_FF = ((8 * D_MODEL + 3 * 256 - 1) // (3 * 256)) * 256
BLOCK = 128
CHUNK = 128
NORM_EPS = 1e-6
NEG_INF = -1e30

kernel_name = "hybrid_diffattn_retention_block"


def rmsnorm(x, g=None, eps=NORM_EPS):
    x32 = x.astype(jnp.float32)
    y = x32 * lax.rsqrt(jnp.mean(x32 * x32, axis=-1, keepdims=True) + eps)
    if g is not None:
        y = y * g.astype(jnp.float32)
    return y.astype(x.dtype)


def alibi_slopes(n_heads):
    return jnp.asarray(2.0 ** (-8.0 * np.arange(1, n_heads + 1) / n_heads), dtype=jnp.float32)


def retention_log_gammas(n_heads):
    return jnp.asarray(np.log(1.0 - 2.0 ** (-5.0 - np.arange(n_heads))), dtype=jnp.float32)


def split_cols(p):
    out, start = [], 0
    for size in COL_SIZES:
        out.append(p[..., start:start + size])
        start += size
    return out


def diff_attention(q, k, v, lam, slopes):
    b, s, h, _, d = q.shape
    nb = s // BLOCK
    scale = d ** -0.5
    qb = q.reshape(b, nb, BLOCK, h, 2, d).transpose(1, 0, 2, 3, 4, 5)
    kpos = jnp.arange(s)
    lam32 = lam.astype(jnp.float32)

    def one_block(args):
        qi, i = args
        qpos = i * BLOCK + jnp.arange(BLOCK)
        sc = jnp.einsum('bqhcd,bkhcd->bhcqk', qi, k).astype(jnp.float32) * scale
        dist = (qpos[:, None] - kpos[None, :]).astype(jnp.float32)
        bias = -slopes[:, None, None, None] * dist
        sc = jnp.where(dist >= 0, sc + bias, NEG_INF)
        p = jax.nn.softmax(sc, axis=-1)
        a = p[:, :, 0] - lam32 * p[:, :, 1]
        return jnp.einsum('bhqk,bkhe->bqhe', a.astype(v.dtype), v)

    out = lax.map(one_block, (qb, jnp.arange(nb)))
    return out.transpose(1, 0, 2, 3, 4).reshape(b, s, h, v.shape[-1])


def retention(q, k, v, log_gamma):
    b, s, h, dk = q.shape
    dv = v.shape[-1]
    n = s // CHUNK
    f32 = jnp.float32
    qc = q.astype(f32).reshape(b, n, CHUNK, h, dk).transpose(1, 0, 2, 3, 4)
    kc = (k.astype(f32) * dk ** -0.5).reshape(b, n, CHUNK, h, dk).transpose(1, 0, 2, 3, 4)
    vc = v.astype(f32).reshape(b, n, CHUNK, h, dv).transpose(1, 0, 2, 3, 4)
    idx = jnp.arange(CHUNK, dtype=f32)
    diff = idx[:, None] - idx[None, :]
    intra = jnp.where(diff >= 0, jnp.exp(log_gamma[:, None, None] * jnp.maximum(diff, 0.0)), 0.0)
    q_decay = jnp.exp(log_gamma[None, :] * (idx[:, None] + 1.0))[None, :, :, None]
    k_decay = jnp.exp(log_gamma[None, :] * (CHUNK - 1.0 - idx[:, None]))[None, :, :, None]
    chunk_decay = jnp.exp(log_gamma * CHUNK)[None, :, None, None]

    def step(state, inp):
        qi, ki, vi = inp
        scores = jnp.einsum('bihd,bjhd->bhij', qi, ki) * intra
        inner = jnp.einsum('bhij,bjhe->bihe', scores, vi)
        cross = jnp.einsum('bihd,bhde->bihe', qi, state) * q_decay
        new_state = state * chunk_decay + jnp.einsum('bjhd,bjhe->bhde', ki * k_decay, vi)
        return new_state, inner + cross

    state0 = jnp.zeros((b, h, dk, dv), f32)
    _, out = lax.scan(step, state0, (qc, kc, vc))
    return out.transpose(1, 0, 2, 3, 4).reshape(b, s, h, dv).astype(q.dtype)


def setup_inputs(seed: int = 0) -> dict:
    key = jax.random.key(seed)
    ks = jax.random.split(key, 24)
    f32 = jnp.float32

    def w(k, shape, fan_in):
        return jax.random.normal(k, shape, f32) * fan_in ** -0.5

    def gain(k, shape):
        return 1.0 + 0.02 * jax.random.normal(k, shape, f32)

    return {
        "x": jax.random.normal(ks[0], (BATCH, SEQ, D_MODEL), f32),
        "mem": jax.random.normal(ks[1], (BATCH, MEM_LEN, D_MODEL), f32),
        "norm_mix_g": gain(ks[2], (DEPTH, D_MODEL)),
        "w_in": w(ks[3], (DEPTH, D_MODEL, IN_COLS), D_MODEL),
        "lambda_q1": 0.1 * jax.random.normal(ks[4], (DEPTH, DA_HEAD_DIM), f32),
        "lambda_k1": 0.1 * jax.random.normal(ks[5], (DEPTH, DA_HEAD_DIM), f32),
        "lambda_q2": 0.1 * jax.random.normal(ks[6], (DEPTH, DA_HEAD_DIM), f32),
        "lambda_k2": 0.1 * jax.random.normal(ks[7], (DEPTH, DA_HEAD_DIM), f32),
        "da_subln_g": gain(ks[8], (DEPTH, DA_V_DIM)),
        "w_o": w(ks[9], (DEPTH, MIX_WIDTH, D_MODEL), MIX_WIDTH),
        "norm_x_g": gain(ks[10], (DEPTH, D_MODEL)),
        "norm_mem_g": gain(ks[11], (DEPTH, D_MODEL)),
        "w_xq": w(ks[12], (DEPTH, D_MODEL, D_MODEL), D_MODEL),
        "w_xk": w(ks[13], (DEPTH, D_MODEL, D_MODEL), D_MODEL),
        "w_xv": w(ks[14], (DEPTH, D_MODEL, D_MODEL), D_MODEL),
        "w_xo": w(ks[15], (DEPTH, D_MODEL, D_MODEL), D_MODEL),
        "norm_ffn_g": gain(ks[16], (DEPTH, D_MODEL)),
        "w_gate": w(ks[17], (DEPTH, D_MODEL, D_FF), D_MODEL),
        "w_up": w(ks[18], (DEPTH, D_MODEL, D_FF), D_MODEL),
        "w_down": w(ks[19], (DEPTH, D_FF, D_MODEL), D_FF),
        "norm_f_g": gain(ks[20], (D_MODEL,)),
    }


def reference(x, mem, norm_mix_g, w_in, lambda_q1, lambda_k1, lambda_q2, lambda_k2,
              da_subln_g, w_o, norm_x_g, norm_mem_g, w_xq, w_xk, w_xv, w_xo,
              norm_ffn_g, w_gate, w_up, w_down, norm_f_g):
    b, s, _ = x.shape
    slopes = alibi_slopes(DA_HEADS)
    log_gamma = retention_log_gammas(RET_HEADS)
    for l in range(DEPTH):
        lam_init = 0.8 - 0.6 * math.exp(-0.3 * l)
        h = rmsnorm(x, norm_mix_g[l])
        proj = h @ w_in[l]
        dq, dk, dv, rq, rk, rv, rg = split_cols(proj)
        lam = (jnp.exp(jnp.sum(lambda_q1[l] * lambda_k1[l]))
               - jnp.exp(jnp.sum(lambda_q2[l] * lambda_k2[l])) + lam_init)
        da = diff_attention(dq.reshape(b, s, DA_HEADS, 2, DA_HEAD_DIM),
                            dk.reshape(b, s, DA_HEADS, 2, DA_HEAD_DIM),
                            dv.reshape(b, s, DA_HEADS, DA_V_DIM), lam, slopes)
        da = rmsnorm(da, da_subln_g[l]) * (1.0 - lam_init)
        ret = retention(rq.reshape(b, s, RET_HEADS, RET_QK_DIM),
                        rk.reshape(b, s, RET_HEADS, RET_QK_DIM),
                        rv.reshape(b, s, RET_HEADS, RET_V_DIM), log_gamma)
        ret = rmsnorm(ret).reshape(b, s, RET_HEADS * RET_V_DIM) * jax.nn.silu(rg)
        mixed = jnp.concatenate([da.reshape(b, s, DA_HEADS * DA_V_DIM), ret], axis=-1)
        x = x + mixed @ w_o[l]
        hx = rmsnorm(x, norm_x_g[l])
        hm = rmsnorm(mem, norm_mem_g[l])
        xq = (hx @ w_xq[l]).reshape(b, s, XATTN_HEADS, XATTN_HEAD_DIM)
        xk = (hm @ w_xk[l]).reshape(b, MEM_LEN, XATTN_HEADS, XATTN_HEAD_DIM)
        xv = (hm @ w_xv[l]).reshape(b, MEM_LEN, XATTN_HEADS, XATTN_HEAD_DIM)
        sc = jnp.einsum('bqhd,bkhd->bhqk', xq, xk).astype(jnp.float32) * XATTN_HEAD_DIM ** -0.5
        p = jax.nn.softmax(sc, axis=-1).astype(x.dtype)
        xo = jnp.einsum('bhqk,bkhd->bqhd', p, xv).reshape(b, s, D_MODEL)
        x = x + xo @ w_xo[l]
        hf = rmsnorm(x, norm_ffn_g[l])
        x = x + (jax.nn.silu(hf @ w_gate[l]) * (hf @ w_up[l])) @ w_down[l]
    return rmsnorm(x, norm_f_g)
```

```python
import math
from contextlib import ExitStack
import numpy as np
import concourse.bass as bass
import concourse.mybir as mybir
from concourse.bass_utils import run_bass_kernel_spmd

F32 = mybir.dt.float32
BF16 = mybir.dt.bfloat16
AF = mybir.ActivationFunctionType
ALU = mybir.AluOpType
ENGS = ('pe', 'act', 'dve', 'pool', 'sp')

EPS = 1e-6
NSLOT = 130
RING = 4
LAM_INIT = 0.8 - 0.6 * math.exp(0.0)

_c = 0
def _col(n):
    global _c
    s = _c
    _c += n
    return s
C_ID = _col(128)
C_ONE = _col(128)
C_GMIX = _col(16)
C_GX = _col(16)
C_GMEM = _col(16)
C_GFFN = _col(16)
C_GF = _col(16)
C_GSUB = _col(2)
C_LAM = _col(4)
C_ZERO = _col(1)
C_EPS = _col(1)
C_ABO = _col(4 * 19)
C_ABC = _col(4 * 19)
C_KFAC = _col(4 * 16)
C_QDEC = _col(4 * 512)
C_MASK = _col(512)
NCST = _c


class Op:
    __slots__ = ('eng', 'fn', 'deps', 'dma_key', 'dma_val', 'signal', 'sig_ord', 'is_output')

    def __init__(self, eng, fn):
        self.eng = eng
        self.fn = fn
        self.deps = []
        self.dma_key = None
        self.dma_val = 0
        self.signal = False
        self.sig_ord = 0
        self.is_output = False


class Sched:
    def __init__(self, nc):
        self.nc = nc
        self.ops = {e: [] for e in ENGS}
        self.bufs = {}
        self.dma_cnt = {}
        self.last_dma = {}
        self.pending = {e: [] for e in ENGS}
        self.stack = ExitStack()

    def sb(self, name, shape, dtype):
        return self.stack.enter_context(self.nc.sbuf_tensor(name, shape, dtype))

    def ps(self, name, shape, dtype):
        return self.stack.enter_context(self.nc.psum_tensor(name, shape, dtype))

    def _track(self, op, reads, writes):
        deps = list(self.pending[op.eng])
        self.pending[op.eng] = []
        for r in reads:
            b = self.bufs.get(r)
            if b is not None and b['w'] is not None:
                deps.append(b['w'])
        for w in writes:
            b = self.bufs.get(w)
            if b is not None:
                if b['w'] is not None:
                    deps.append(b['w'])
                deps.extend(b['r'].values())
        rk = op.eng if op.dma_key is None else ('dma', op.dma_key)
        for r in reads:
            b = self.bufs.setdefault(r, {'w': None, 'r': {}})
            b['r'][rk] = op
        for w in writes:
            b = self.bufs.setdefault(w, {'w': None, 'r': {}})
            b['w'] = op
            b['r'] = {}
        seen = set()
        for d in deps:
            if d is op or id(d) in seen:
                continue
            seen.add(id(d))
            if d.dma_key is None:
                if d.eng == 'pe' and op.eng == 'pe' and op.dma_key is None:
                    continue
                d.signal = True
            op.deps.append(d)

    def op(self, eng, fn, reads=(), writes=()):
        o = Op(eng, fn)
        self._track(o, reads, writes)
        self.ops[eng].append(o)
        return o

    def dma(self, eng, out, in_, reads=(), writes=(), sem=None, is_output=False, **kw):
        o = Op(eng, lambda e: e.dma_start(out=out, in_=in_, **kw))
        o.dma_key = sem
        self.dma_cnt[sem] = self.dma_cnt.get(sem, 0) + 16
        o.dma_val = self.dma_cnt[sem]
        o.is_output = is_output
        self._track(o, reads, writes)
        self.ops[eng].append(o)
        self.last_dma[sem] = o
        return o

    def barrier(self):
        deps = []
        for e in ENGS:
            for o in reversed(self.ops[e]):
                if o.dma_key is None:
                    deps.append(o)
                    break
        deps.extend(self.last_dma.values())
        for e in ENGS:
            self.pending[e] = list(deps)

    def emit(self):
        nc = self.nc
        for e in ENGS:
            k = 0
            for o in self.ops[e]:
                if o.dma_key is None and o.signal:
                    k += 1
                    o.sig_ord = k
        st = self.stack
        esem = {e: st.enter_context(nc.semaphore(f"s_{e}")) for e in ('pe', 'act', 'dve', 'pool')}
        dsem = {k: st.enter_context(nc.semaphore(f"d_{k}")) for k in self.dma_cnt}
        out_keys = set()
        for e in ENGS:
            for o in self.ops[e]:
                if o.is_output:
                    out_keys.add(o.dma_key)

        def run(eng_name, eng):
            waited = {}
            for o in self.ops[eng_name]:
                for d in o.deps:
                    if d.dma_key is not None:
                        key, sem, val = ('d', d.dma_key), dsem[d.dma_key], d.dma_val
                    else:
                        key, sem, val = ('e', d.eng), esem[d.eng], d.sig_ord
                    if waited.get(key, 0) < val:
                        eng.wait_ge(sem, val)
                        waited[key] = val
                inst = o.fn(eng)
                if o.dma_key is not None:
                    inst.then_inc(dsem[o.dma_key], 16)
                elif o.signal:
                    inst.then_inc(esem[eng_name], 1)
            if eng_name == 'sp':
                for k in sorted(out_keys):
                    eng.wait_ge(dsem[k], self.dma_cnt[k])

        with nc.Block() as block:
            @block.sync
            def _(e):
                run('sp', e)

            @block.tensor
            def _(e):
                run('pe', e)

            @block.scalar
            def _(e):
                run('act', e)

            @block.vector
            def _(e):
                run('dve', e)

            @block.gpsimd
            def _(e):
                run('pool', e)
        st.close()


def build(stage=99, dbg=None):
    nc = bass.Bass("TRN2", target_bir_lowering=False)
    xall = nc.dram_tensor("xall", [16, 128, 2048], F32, kind="ExternalInput").ap()
    memb = nc.dram_tensor("memb", [2, 128, 2048], F32, kind="ExternalInput").ap()
    cst_d = nc.dram_tensor("cst", [128, NCST], F32, kind="ExternalInput").ap()
    wst = nc.dram_tensor("wst", [NSLOT, 128, 4096], F32, kind="ExternalInput").ap()
    out_d = nc.dram_tensor("out", [8, 128, 2048], F32, kind="ExternalOutput").ap()
    dbg_d = {}
    if dbg:
        for name, shp in dbg.items():
            dbg_d[name] = nc.dram_tensor("dbg_" + name, list(shp), F32, kind="ExternalOutput").ap()

    S = Sched(nc)
    cst = S.sb("cst_sb", [128, NCST], F32)
    onesb = S.sb("onesb", [128, 128], BF16)
    maskb = S.sb("maskb", [128, 512], BF16)
    ring = [S.sb(f"ring{i}", [128, 4096], BF16) for i in range(RING)]
    X = S.sb("X", [128, 16384], F32)
    Y = S.sb("Y", [128, 16, 1024], BF16)
    Z = S.sb("Z", [128, 8192], F32)
    sqr = [S.sb(f"sq{i}", [128, 16, 128], BF16) for i in range(2)]
    rsb = [S.sb(f"rs{i}", [128, 128], F32) for i in range(2)]
    ptr = [S.sb(f"pt{i}", [128, 512], BF16) for i in range(4)]
    rcp = [S.sb(f"rcp{i}", [128, 512], F32) for i in range(2)]
    At = S.sb("At", [128, 4, 512], F32)
    T2 = [S.sb(f"t2{i}", [128, 512], F32) for i in range(2)]
    rs512 = S.sb("rs512", [128, 512], F32)
    sq512 = S.sb("sq512", [128, 2, 512], BF16)
    sm = S.sb("sm", [128, 16], F32)
    PB = [S.ps(f"pb{i}", [128, 512], F32) for i in range(8)]

    def zf(a, b):
        return Z[:, a * 256:b * 256]

    def zb(a, b):
        return Z[:, a * 256:b * 256].bitcast(BF16)

    ident = cst[:, C_ID:C_ID + 128]
    onesf = cst[:, C_ONE:C_ONE + 128]
    zero_c = cst[:, C_ZERO:C_ZERO + 1]

    hA = X[:, :].bitcast(BF16).rearrange("p (c t) -> p c t", c=16)
    xT = X[:, :].rearrange("p (c t) -> p c t", c=16)

    wplan = []
    wstate = {'n': 0}

    def wload(desc):
        n = wstate['n']
        wstate['n'] += 1
        wplan.append(desc)
        s = n % RING
        S.dma('pool', ring[s][:, :], wst[n], writes=[('w', s)], sem=f"w{s}", max_dma_last_dim=2048)
        return s

    def w_fm(s):
        return ring[s][:, :].rearrange("p (i k n) -> p i k n", i=2, k=16)

    def w_tm(s):
        return ring[s][:, :].rearrange("p (k n) -> p k n", k=16)

    def w_rows(s):
        return ring[s][:, :].rearrange("p (i n) -> p i n", i=2)

    rot = {}

    def bank(group, banks):
        i = rot.get(group, 0)
        rot[group] = i + 1
        return banks[i % len(banks)]

    evac_flip = {'i': 0}

    def evac_copy(out_ap, in_ap, reads, writes, eng=None):
        if eng is None:
            evac_flip['i'] += 1
            eng = 'dve' if evac_flip['i'] % 3 else 'act'
        if eng == 'dve':
            S.op('dve', lambda e: e.tensor_copy(out_ap, in_ap), reads=reads, writes=writes)
        else:
            S.op('act', lambda e: e.activation(out_ap, in_ap, AF.Copy, scale=1.0), reads=reads, writes=writes)

    S.dma('sp', cst[:, :], cst_d, writes=['cst'], sem='cst')
    S.op('dve', lambda e: e.tensor_copy(onesb[:, :], cst[:, C_ONE:C_ONE + 128]), reads=['cst'], writes=['onesb'])
    S.op('dve', lambda e: e.tensor_copy(maskb[:, :], cst[:, C_MASK:C_MASK + 512]), reads=['cst'], writes=['maskb'])
    S.op('dve', lambda e: e.tensor_tensor(sm[:, 0:1], cst[:, C_LAM:C_LAM + 1], cst[:, C_LAM + 1:C_LAM + 2], ALU.mult),
         reads=['cst'], writes=['sm01'])
    S.op('dve', lambda e: e.tensor_tensor(sm[:, 1:2], cst[:, C_LAM + 2:C_LAM + 3], cst[:, C_LAM + 3:C_LAM + 4], ALU.mult),
         reads=['cst'], writes=['sm01'])
    sp3 = S.sb("sp3", [128, 3, 2], BF16)
    S.op('dve', lambda e: e.tensor_copy(sp3[:, 0, :], sm[:, 0:2]), reads=['sm01'], writes=['sp3a'])
    S.op('dve', lambda e: e.tensor_tensor(sm[:, 8:10], sm[:, 0:2], sp3[:, 0, :], ALU.subtract), reads=['sm01', 'sp3a'], writes=['smr1'])
    S.op('dve', lambda e: e.tensor_copy(sp3[:, 1, :], sm[:, 8:10]), reads=['smr1'], writes=['sp3b'])
    S.op('dve', lambda e: e.tensor_tensor(sm[:, 10:12], sm[:, 8:10], sp3[:, 1, :], ALU.subtract), reads=['smr1', 'sp3b'], writes=['smr2'])
    S.op('dve', lambda e: e.tensor_copy(sp3[:, 2, :], sm[:, 10:12]), reads=['smr2'], writes=['sp3c'])
    for q in range(3):
        S.op('pe', lambda e, q=q: e.matmul(PB[7][:, 0:2], lhsT=onesb[:, :], rhs=sp3[:, q, :], start=(q == 0), stop=(q == 2)),
             reads=['onesb', 'sp3a', 'sp3b', 'sp3c'], writes=[('ps', 7)])
    S.op('act', lambda e: e.activation(sm[:, 2:4], PB[7][:, 0:2], AF.Exp, bias=zero_c, scale=1.0),
         reads=[('ps', 7), 'cst'], writes=['sm23'])
    S.op('dve', lambda e: e.tensor_tensor(sm[:, 4:5], sm[:, 2:3], sm[:, 3:4], ALU.subtract), reads=['sm23'], writes=['sm4'])
    S.op('dve', lambda e: e.tensor_scalar(sm[:, 5:6], sm[:, 4:5], -1.0, -LAM_INIT, ALU.mult, ALU.add), reads=['sm4'], writes=['neglam'])
    S.op('dve', lambda e: e.tensor_scalar(sm[:, 6:8], cst[:, C_GSUB:C_GSUB + 2], 1.0 - LAM_INIT, None, ALU.mult), reads=['cst'], writes=['gsubs'])
    neglam = sm[:, 5:6]

    nb = {'i': 0}

    def norm_block(src, src_keys, gc, dst, dst_keys, D=2048.0):
        i = nb['i'] % 2
        nb['i'] += 1
        sq, rs = sqr[i], rsb[i]
        ss = bank('ss', [7, 6])
        S.op('act', lambda e: e.activation(sq[:, :, :], src, AF.Square, bias=zero_c, scale=1.0),
             reads=list(src_keys) + ['cst'], writes=[('sq', i)])
        for c in range(16):
            S.op('pe', lambda e, c=c: e.matmul(PB[ss][:, 0:128], lhsT=onesb[:, :], rhs=sq[:, c, :], start=(c == 0), stop=(c == 15)),
                 reads=[('sq', i), 'onesb'], writes=[('ps', ss)])
        S.op('act', lambda e: e.activation(rs[:, :], PB[ss][:, 0:128], AF.Sqrt, bias=cst[:, C_EPS:C_EPS + 1], scale=1.0 / D),
             reads=[('ps', ss), 'cst'], writes=[('rs', i)])
        S.op('dve', lambda e: e.reciprocal(rs[:, :], rs[:, :]), reads=[('rs', i)], writes=[('rs', i)])
        for c in range(16):
            S.op('dve', lambda e, c=c: e.scalar_tensor_tensor(dst[:, c, :], src[:, c, :], cst[:, gc + c:gc + c + 1], rs[:, :], ALU.mult, ALU.mult),
                 reads=list(src_keys) + [('rs', i), 'cst'], writes=list(dst_keys))

    def load_T(src_d, stg, stg_key, dst, dst_keys):
        S.dma('sp', stg, src_d, writes=[stg_key], sem=f"ld_{stg_key[1]}")
        for g in range(4):
            b = bank('tp', [0, 1])
            for j in range(4):
                c = 4 * g + j
                S.op('pe', lambda e, c=c, j=j, b=b: e.transpose(PB[b][:, j * 128:(j + 1) * 128], stg[:, c * 128:(c + 1) * 128], ident),
                     reads=[stg_key, 'cst'], writes=[('ps', b)])
            evac_copy(dst[:, 4 * g:4 * g + 4, :], PB[b][:, :].rearrange("p (a t) -> p a t", a=4), reads=[('ps', b)], writes=list(dst_keys))

    def dump(name, ap, reads):
        if name in dbg_d:
            S.dma('sp', dbg_d[name], ap, reads=reads, sem='dbg_' + name, is_output=True)

    stg = [zf(0, 8), zf(8, 16)]
    xTf = [zf(16, 24).rearrange("p (c t) -> p c t", c=16), zf(24, 32).rearrange("p (c t) -> p c t", c=16)]
    for blk in range(16):
        i = blk % 2
        load_T(xall[blk], stg[i], ('stg', i), xTf[i], [('xTf', i)])
        norm_block(xTf[i], [('xTf', i)], C_GMIX, hA[:, :, blk * 128:(blk + 1) * 128], [('hA', blk)])
    if 'hA' in dbg_d:
        S.barrier()
        S.op('dve', lambda e: e.tensor_copy(Z[:, 0:4096].rearrange("p (c t) -> p c t", c=2), hA[:, 0:2, :]), reads=[('hA', b) for b in range(16)], writes=['zdbg'])
        dump('hA', Z[:, 0:4096], ['zdbg'])
    S.barrier()
    if stage < 1:
        S.emit()
        return nc, wplan

    B1 = zb(0, 4).rearrange("p (m t) -> p m t", m=2)
    B2 = zb(4, 12)
    B3 = zb(12, 20).rearrange("p (b e) -> p b e", b=16)
    hkeys_tc = lambda tc: [('hA', 4 * tc + q) for q in range(4)]
    SC128 = 128.0 ** -0.5

    def proj_fm(wv_, tcs, evac):
        for tc in tcs:
            b = bank('pj', [0, 1])
            for kc in range(16):
                S.op('pe', lambda e, wv_=wv_, kc=kc, tc=tc, b=b: e.matmul(PB[b][:, :], lhsT=wv_[0][:, kc, :], rhs=hA[:, kc, tc * 512:(tc + 1) * 512],
                                                                start=(kc == 0), stop=(kc == 15)),
                     reads=[wv_[1]] + hkeys_tc(tc), writes=[('ps', b)])
            evac(tc, b)

    def proj_v(s):
        wv_ = w_tm(s)
        for bp in range(8):
            b = bank('pj', [0, 1])
            for i in range(2):
                blk = 2 * bp + i
                for kc in range(16):
                    S.op('pe', lambda e, kc=kc, blk=blk, b=b, i=i: e.matmul(PB[b][:, i * 256:(i + 1) * 256], lhsT=hA[:, kc, blk * 128:(blk + 1) * 128],
                                                                        rhs=wv_[:, kc, :], start=(kc == 0), stop=(kc == 15)),
                         reads=[('w', s), ('hA', blk)], writes=[('ps', b)])
            evac_copy(B3[:, 2 * bp:2 * bp + 2, :], PB[b][:, :].rearrange("p (a t) -> p a t", a=2), reads=[('ps', b)],
                      writes=[('B3', 2 * bp), ('B3', 2 * bp + 1)])

    def attn_units(c, kT_of, q_of, evac_pt, nsum, pv_banks, sm_bank):
        last = 8 + 4 * c + 3
        for j in range(last + 1):
            r = max(0, j - (8 + 4 * c))
            n0 = 128 * r
            N = 512 - n0
            sb_ = bank('st', [2, 3])
            kap, kkey = kT_of(j)
            qap, qkey = q_of(c, n0)
            S.op('pe', lambda e, sb_=sb_, kap=kap, qap=qap, N=N: e.matmul(PB[sb_][:, 0:N], lhsT=kap, rhs=qap, start=True, stop=True),
                 reads=[kkey, qkey], writes=[('ps', sb_)])
            pi = bank('pt', [0, 1, 2, 3])
            pt = ptr[pi]
            evac_pt(c, j, r, n0, N, sb_, pt, pi)
            if j >= 8 + 4 * c:
                S.op('dve', lambda e, pt=pt, n0=n0: e.tensor_tensor(pt[:, n0:n0 + 128], pt[:, n0:n0 + 128], maskb[:, 0:128], ALU.mult),
                     reads=[('pt', pi), 'maskb'], writes=[('pt', pi)])
            for e_ in range(2):
                S.op('pe', lambda e, e_=e_, j=j, pt=pt, n0=n0: e.matmul(PB[pv_banks[e_]][:, n0:512], lhsT=B3[:, j, e_ * 128:(e_ + 1) * 128], rhs=pt[:, n0:512],
                                                                    start=(j == 0), stop=(j == last)),
                     reads=[('B3', j), ('pt', pi)], writes=[('ps', pv_banks[e_])])
            if nsum:
                S.op('pe', lambda e, pt=pt, n0=n0, j=j: e.matmul(PB[sm_bank][:, n0:512], lhsT=onesb[:, :], rhs=pt[:, n0:512], start=(j == 0), stop=(j == last)),
                     reads=['onesb', ('pt', pi)], writes=[('ps', sm_bank)])

    def subln_stats(nchunk, D):
        for e_ in range(nchunk):
            S.op('dve', lambda e, e_=e_: e.tensor_tensor(sq512[:, e_, :], At[:, e_, :], At[:, e_, :], ALU.mult), reads=[('At', e_)], writes=[('sq512', e_)])
        for e_ in range(nchunk):
            S.op('pe', lambda e, e_=e_: e.matmul(PB[7][:, :], lhsT=onesb[:, :], rhs=sq512[:, e_, :], start=(e_ == 0), stop=(e_ == nchunk - 1)),
                 reads=['onesb', ('sq512', e_)], writes=[('ps', 7)])
        S.op('act', lambda e: e.activation(rs512[:, :], PB[7][:, :], AF.Sqrt, bias=cst[:, C_EPS:C_EPS + 1], scale=1.0 / D),
             reads=[('ps', 7), 'cst'], writes=['rs512'])
        S.op('dve', lambda e: e.reciprocal(rs512[:, :], rs512[:, :]), reads=['rs512'], writes=['rs512'])

    for h in range(4):
        sq_, sk_, sv_ = None, None, None
        sq_ = wload(('fm2', ('w_in', h * 256), ('w_in', h * 256 + 128)))
        wq = w_fm(sq_)
        for m in range(2):
            proj_fm((wq[:, m], ('w', sq_)), [2, 3],
                    lambda tc, b, m=m: evac_copy(B1[:, m, (tc - 2) * 512:(tc - 1) * 512], PB[b][:, :], reads=[('ps', b)], writes=[('B1', m * 2 + tc - 2)]))
        sk_ = wload(('fm2', ('w_in', 1024 + h * 256), ('w_in', 1024 + h * 256 + 128)))
        wk = w_fm(sk_)
        for m in range(2):
            proj_fm((wk[:, m], ('w', sk_)), [0, 1, 2, 3],
                    lambda tc, b, m=m: evac_copy(B2[:, m * 2048 + tc * 512:m * 2048 + (tc + 1) * 512], PB[b][:, :], reads=[('ps', b)], writes=[('B2', m * 4 + tc)]))
        sv_ = wload(('tm', 'w_in', 2048 + h * 256))
        proj_v(sv_)
        split = (2.0 ** (-2.0 * (h + 1))) * 511 > 60.0
        for c in range(2):
            for m in range(2):
                def evac_exp(c, j, r, n0, N, sb_, pt, pi, m=m, h=h):
                    D = 8 + 4 * c - j
                    tab = C_ABC if j < 8 else C_ABO
                    if split:
                        for rr in range(r, 4):
                            col = tab + h * 19 + (D + rr) + 3
                            S.op('act', lambda e, rr=rr, col=col: e.activation(pt[:, rr * 128:(rr + 1) * 128], PB[sb_][:, (rr - r) * 128:(rr - r + 1) * 128],
                                                                               AF.Exp, bias=cst[:, col:col + 1], scale=SC128),
                                 reads=[('ps', sb_), 'cst'], writes=[('pt', pi)])
                    else:
                        col = tab + h * 19 + D + 3
                        S.op('act', lambda e, col=col: e.activation(pt[:, n0:512], PB[sb_][:, 0:N], AF.Exp, bias=cst[:, col:col + 1], scale=SC128),
                             reads=[('ps', sb_), 'cst'], writes=[('pt', pi)])
                attn_units(c,
                           lambda j, m=m: (B2[:, m * 2048 + j * 128:m * 2048 + (j + 1) * 128], ('B2', m * 4 + j // 4)),
                           lambda c, n0, m=m: (B1[:, m, c * 512 + n0:(c + 1) * 512], ('B1', m * 2 + c)),
                           evac_exp, True, [4, 5], 6)
                rc = rcp[m]
                S.op('dve', lambda e, rc=rc: e.reciprocal(rc[:, :], PB[6][:, :]), reads=[('ps', 6)], writes=[('rcp', m)])
                for e_ in range(2):
                    if m == 0:
                        S.op('dve', lambda e, e_=e_, rc=rc: e.tensor_tensor(At[:, e_, :], PB[4 + e_][:, :], rc[:, :], ALU.mult),
                             reads=[('ps', 4 + e_), ('rcp', m)], writes=[('At', e_)])
                    else:
                        S.op('dve', lambda e, e_=e_, rc=rc: e.tensor_tensor(T2[e_][:, :], PB[4 + e_][:, :], rc[:, :], ALU.mult),
                             reads=[('ps', 4 + e_), ('rcp', m)], writes=[('T2', e_)])
                        S.op('dve', lambda e, e_=e_: e.scalar_tensor_tensor(At[:, e_, :], T2[e_][:, :], neglam, At[:, e_, :], ALU.mult, ALU.add),
                             reads=[('T2', e_), ('At', e_), 'neglam'], writes=[('At', e_)])
            subln_stats(2, 256.0)
            for e_ in range(2):
                S.op('dve', lambda e, e_=e_, c=c, h=h: e.scalar_tensor_tensor(Y[:, h * 2 + e_, c * 512:(c + 1) * 512], At[:, e_, :], sm[:, 6 + e_:7 + e_], rs512[:, :],
                                                                            ALU.mult, ALU.mult),
                     reads=[('At', e_), 'gsubs', 'rs512'], writes=[('Y', h * 2 + e_, c)])

        sqk = wload(('fm2', ('w_in', 3072 + h * 128), ('w_in', 3584 + h * 128)))
        wqk = w_fm(sqk)
        qd = cst[:, C_QDEC + h * 512:C_QDEC + (h + 1) * 512]
        proj_fm((wqk[:, 0], ('w', sqk)), [2, 3],
                lambda tc, b, qd=qd: S.op('dve', lambda e: e.tensor_tensor(B2[:, 2048 + (tc - 2) * 512:2048 + (tc - 1) * 512], PB[b][:, :], qd, ALU.mult),
                                   reads=[('ps', b), 'cst'], writes=[('B2', 4 + tc - 2)]))
        proj_fm((wqk[:, 1], ('w', sqk)), [0, 1, 2, 3],
                lambda tc, b: evac_copy(B2[:, tc * 512:(tc + 1) * 512], PB[b][:, :], reads=[('ps', b)], writes=[('B2', tc)]))
        sg_ = wload(('fm2', ('w_in', 5120 + h * 256), ('w_in', 5120 + h * 256 + 128)))
        wg = w_fm(sg_)
        for e_ in range(2):
            proj_fm((wg[:, e_], ('w', sg_)), [2, 3],
                    lambda tc, b, e_=e_: S.op('act', lambda e: e.activation(B1[:, e_, (tc - 2) * 512:(tc - 1) * 512], PB[b][:, :], AF.Silu, bias=zero_c, scale=1.0),
                                              reads=[('ps', b), 'cst'], writes=[('B1', e_ * 2 + tc - 2)]))
        srv = wload(('tm', 'w_in', 4096 + h * 256))
        proj_v(srv)
        for c in range(2):
            def evac_ret(c, j, r, n0, N, sb_, pt, pi, h=h):
                D = 8 + 4 * c - j
                col = C_KFAC + h * 16 + D + 3
                S.op('dve', lambda e: e.tensor_scalar(pt[:, n0:512], PB[sb_][:, 0:N], cst[:, col:col + 1], None, ALU.mult),
                     reads=[('ps', sb_), 'cst'], writes=[('pt', pi)])
            attn_units(c,
                       lambda j: (B2[:, j * 128:(j + 1) * 128], ('B2', j // 4)),
                       lambda c, n0: (B2[:, 2048 + c * 512 + n0:2048 + (c + 1) * 512], ('B2', 4 + c)),
                       evac_ret, False, [4, 5], 6)
            for e_ in range(2):
                evac_copy(At[:, e_, :], PB[4 + e_][:, :], reads=[('ps', 4 + e_)], writes=[('At', e_)])
            subln_stats(2, 256.0)
            for e_ in range(2):
                S.op('dve', lambda e, e_=e_: e.tensor_tensor(T2[e_][:, :], At[:, e_, :], rs512[:, :], ALU.mult),
                     reads=[('At', e_), 'rs512'], writes=[('T2', e_)])
                S.op('dve', lambda e, e_=e_, c=c, h=h: e.tensor_tensor(Y[:, 8 + h * 2 + e_, c * 512:(c + 1) * 512], T2[e_][:, :], B1[:, e_, c * 512:(c + 1) * 512], ALU.mult),
                     reads=[('T2', e_), ('B1', e_ * 2 + c)], writes=[('Y', 8 + h * 2 + e_, c)])
    if 'Y' in dbg_d:
        S.barrier()
        S.op('dve', lambda e: e.tensor_copy(Z[:, :].rearrange("p (c t) -> p c t", c=16), Y[:, :, 0:512]), reads=[('Y', c, 0) for c in range(16)], writes=['zdbg'])
        dump('Y', Z[:, :], ['zdbg'])
    S.barrier()
    if stage < 2:
        S.emit()
        return nc, wplan

    xkeys_blk = lambda blk: [('xT', c, blk // 4) for c in range(16)]
    for blk in range(8):
        i = blk % 2
        load_T(xall[8 + blk], stg[i], ('stg', i), xT[:, :, blk * 128:(blk + 1) * 128], xkeys_blk(blk))

    def add_evac(c, tc, b):
        S.op('dve', lambda e: e.tensor_tensor(xT[:, c, tc * 512:(tc + 1) * 512], PB[b][:, :], xT[:, c, tc * 512:(tc + 1) * 512], ALU.add),
             reads=[('ps', b), ('xT', c, tc)], writes=[('xT', c, tc)])

    for sp_ in range(8):
        s = wload(('fm2', ('w_o', sp_ * 256), ('w_o', sp_ * 256 + 128)))
        wv_ = w_fm(s)
        for i in range(2):
            c = 2 * sp_ + i
            for tc in range(2):
                b = bank('pj', [0, 1])
                for kc in range(16):
                    S.op('pe', lambda e, wv_=wv_, kc=kc, tc=tc, b=b, i=i: e.matmul(PB[b][:, :], lhsT=wv_[:, i, kc, :], rhs=Y[:, kc, tc * 512:(tc + 1) * 512],
                                                                        start=(kc == 0), stop=(kc == 15)),
                         reads=[('w', s), ('Y', kc, tc)], writes=[('ps', b)])
                add_evac(c, tc, b)

    def norm_x_to_Y(gc):
        for blk in range(8):
            norm_block(xT[:, :, blk * 128:(blk + 1) * 128], xkeys_blk(blk), gc, Y[:, :, blk * 128:(blk + 1) * 128], [('Y', c, blk // 4) for c in range(16)])

    if 'x1' in dbg_d:
        S.barrier()
        dump('x1', X[:, 0:8192], [('xT', c, tc) for c in range(16) for tc in range(2)])
    if stage < 3:
        S.barrier()
        S.emit()
        return nc, wplan

    norm_x_to_Y(C_GX)
    S.barrier()
    hmT = zb(0, 8).rearrange("p (c t) -> p c t", c=16)
    stg_m = zf(8, 16)
    xTf_m = zf(16, 24).rearrange("p (c t) -> p c t", c=16)
    for mb in range(2):
        load_T(memb[mb], stg_m, ('stg', 0), xTf_m, [('xTf', 0)])
        norm_block(xTf_m, [('xTf', 0)], C_GMEM, hmT[:, :, mb * 128:(mb + 1) * 128], [('hm', mb)])
    S.barrier()
    xq = zb(8, 16).rearrange("p (c t) -> p c t", c=4)
    xk = zb(16, 18).rearrange("p (c t) -> p c t", c=4)
    xv = zb(18, 20).rearrange("p (b e) -> p b e", b=2)
    xo = zb(20, 28).rearrange("p (c t) -> p c t", c=4)
    SC512 = 512.0 ** -0.5
    for h in range(4):
        for half in range(2):
            s = wload(('fm2', ('w_xk', h * 512 + half * 256), ('w_xk', h * 512 + half * 256 + 128)))
            wv_ = w_fm(s)
            for i in range(2):
                ch = half * 2 + i
                b = bank('pj', [0, 1])
                for kc in range(16):
                    S.op('pe', lambda e, wv_=wv_, kc=kc, b=b, i=i: e.matmul(PB[b][:, 0:256], lhsT=wv_[:, i, kc, :], rhs=hmT[:, kc, :], start=(kc == 0), stop=(kc == 15)),
                         reads=[('w', s), ('hm', 0), ('hm', 1)], writes=[('ps', b)])
                evac_copy(xk[:, ch, :], PB[b][:, 0:256], reads=[('ps', b)], writes=[('xk', ch)])
        for half in range(2):
            s = wload(('tm', 'w_xv', h * 512 + half * 256))
            wv_ = w_tm(s)
            for mb in range(2):
                b = bank('pj', [0, 1])
                for kc in range(16):
                    S.op('pe', lambda e, wv_=wv_, kc=kc, b=b, mb=mb: e.matmul(PB[b][:, 0:256], lhsT=hmT[:, kc, mb * 128:(mb + 1) * 128], rhs=wv_[:, kc, :], start=(kc == 0), stop=(kc == 15)),
                         reads=[('w', s), ('hm', mb)], writes=[('ps', b)])
                evac_copy(xv[:, mb, half * 256:(half + 1) * 256], PB[b][:, 0:256], reads=[('ps', b)], writes=[('xv', mb)])
        for half in range(2):
            s = wload(('fm2', ('w_xq', h * 512 + half * 256), ('w_xq', h * 512 + half * 256 + 128)))
            wv_ = w_fm(s)
            for i in range(2):
                ch = half * 2 + i
                for tc in range(2):
                    b = bank('pj', [0, 1])
                    for kc in range(16):
                        S.op('pe', lambda e, wv_=wv_, kc=kc, b=b, i=i, tc=tc: e.matmul(PB[b][:, :], lhsT=wv_[:, i, kc, :], rhs=Y[:, kc, tc * 512:(tc + 1) * 512], start=(kc == 0), stop=(kc == 15)),
                             reads=[('w', s), ('Y', kc, tc)], writes=[('ps', b)])
                    evac_copy(xq[:, ch, tc * 512:(tc + 1) * 512], PB[b][:, :], reads=[('ps', b)], writes=[('xq', ch, tc)])
        for tc in range(2):
            for mb in range(2):
                sb_ = bank('st', [2, 3])
                for ch in range(4):
                    S.op('pe', lambda e, ch=ch, sb_=sb_, mb=mb, tc=tc: e.matmul(PB[sb_][:, :], lhsT=xk[:, ch, mb * 128:(mb + 1) * 128], rhs=xq[:, ch, tc * 512:(tc + 1) * 512],
                                                                          start=(ch == 0), stop=(ch == 3)),
                         reads=[('xk', ch), ('xq', ch, tc)], writes=[('ps', sb_)])
                pi = bank('pt', [0, 1, 2, 3])
                pt = ptr[pi]
                S.op('act', lambda e, pt=pt, sb_=sb_: e.activation(pt[:, :], PB[sb_][:, :], AF.Exp, bias=zero_c, scale=SC512),
                     reads=[('ps', sb_), 'cst'], writes=[('pt', pi)])
                for e_ in range(4):
                    S.op('pe', lambda e, e_=e_, pt=pt, mb=mb: e.matmul(PB[4 + e_][:, :], lhsT=xv[:, mb, e_ * 128:(e_ + 1) * 128], rhs=pt[:, :], start=(mb == 0), stop=(mb == 1)),
                         reads=[('xv', mb), ('pt', pi)], writes=[('ps', 4 + e_)])
                S.op('pe', lambda e, pt=pt, mb=mb: e.matmul(PB[1][:, :], lhsT=onesb[:, :], rhs=pt[:, :], start=(mb == 0), stop=(mb == 1)),
                     reads=['onesb', ('pt', pi)], writes=[('ps', 1)])
            rc = rcp[tc]
            S.op('dve', lambda e, rc=rc: e.reciprocal(rc[:, :], PB[1][:, :]), reads=[('ps', 1)], writes=[('rcp', tc)])
            for e_ in range(4):
                S.op('dve', lambda e, e_=e_, rc=rc, tc=tc: e.tensor_tensor(xo[:, e_, tc * 512:(tc + 1) * 512], PB[4 + e_][:, :], rc[:, :], ALU.mult),
                     reads=[('ps', 4 + e_), ('rcp', tc)], writes=[('xo', e_, tc)])
        so = [wload(('rows2', 'w_xo', 4 * h + 2 * q)) for q in range(2)]
        wo = [w_rows(s) for s in so]
        for c in range(16):
            for tc in range(2):
                b = bank('pj', [0, 1])
                for e_ in range(4):
                    S.op('pe', lambda e, wo=wo, e_=e_, c=c, tc=tc, b=b: e.matmul(PB[b][:, :], lhsT=wo[e_ // 2][:, e_ % 2, c * 128:(c + 1) * 128], rhs=xo[:, e_, tc * 512:(tc + 1) * 512],
                                                                        start=(e_ == 0), stop=(e_ == 3)),
                         reads=[('w', so[e_ // 2]), ('xo', e_, tc)], writes=[('ps', b)])
                add_evac(c, tc, b)
    if 'x2' in dbg_d:
        S.barrier()
        dump('x2', X[:, 0:8192], [('xT', c, tc) for c in range(16) for tc in range(2)])
    S.barrier()
    if stage < 4:
        S.emit()
        return nc, wplan

    norm_x_to_Y(C_GFFN)
    act = [zb(0, 8).rearrange("p (f t) -> p f t", f=4), zb(8, 16).rearrange("p (f t) -> p f t", f=4)]
    for g in range(11):
        ab = act[g % 2]
        for fi in range(4):
            f = 4 * g + fi
            s = wload(('fm2', ('w_gate', f * 128), ('w_up', f * 128)))
            wv_ = w_fm(s)
            for tc in range(2):
                bg = bank('gu', [0, 1, 2, 3])
                bu = bank('gu', [0, 1, 2, 3])
                for (i, b) in ((0, bg), (1, bu)):
                    for kc in range(16):
                        S.op('pe', lambda e, wv_=wv_, kc=kc, b=b, i=i, tc=tc: e.matmul(PB[b][:, :], lhsT=wv_[:, i, kc, :], rhs=Y[:, kc, tc * 512:(tc + 1) * 512], start=(kc == 0), stop=(kc == 15)),
                             reads=[('w', s), ('Y', kc, tc)], writes=[('ps', b)])
                ti = bank('t2', [0, 1])
                S.op('act', lambda e, ti=ti, bg=bg: e.activation(T2[ti][:, :], PB[bg][:, :], AF.Silu, bias=zero_c, scale=1.0),
                     reads=[('ps', bg), 'cst'], writes=[('T2', ti)])
                S.op('dve', lambda e, ti=ti, bu=bu, fi=fi, tc=tc, ab=ab: e.tensor_tensor(ab[:, fi, tc * 512:(tc + 1) * 512], PB[bu][:, :], T2[ti][:, :], ALU.mult),
                     reads=[('ps', bu), ('T2', ti)], writes=[('act', g % 2, fi, tc)])
        sd = [wload(('rows2', 'w_down', 4 * g + 2 * q)) for q in range(2)]
        wd = [w_rows(s) for s in sd]
        for c in range(16):
            for tc in range(2):
                b = bank('dn', [4, 5, 6, 7])
                for fi in range(4):
                    S.op('pe', lambda e, wd=wd, fi=fi, c=c, tc=tc, b=b, ab=ab: e.matmul(PB[b][:, :], lhsT=wd[fi // 2][:, fi % 2, c * 128:(c + 1) * 128], rhs=ab[:, fi, tc * 512:(tc + 1) * 512],
                                                                              start=(fi == 0), stop=(fi == 3)),
                         reads=[('w', sd[fi // 2]), ('act', g % 2, fi, tc)], writes=[('ps', b)])
                add_evac(c, tc, b)
    S.barrier()

    yTf = [zf(0, 8).rearrange("p (c t) -> p c t", c=16), zf(8, 16).rearrange("p (c t) -> p c t", c=16)]
    ostg = [zf(16, 24), zf(24, 32)]
    for blk in range(8):
        i = blk % 2
        norm_block(xT[:, :, blk * 128:(blk + 1) * 128], xkeys_blk(blk), C_GF, yTf[i], [('yTf', i)])
        for g in range(4):
            b = bank('tp', [0, 1])
            for j in range(4):
                c = 4 * g + j
                S.op('pe', lambda e, c=c, j=j, b=b, i=i: e.transpose(PB[b][:, j * 128:(j + 1) * 128], yTf[i][:, c, :], ident),
                     reads=[('yTf', i), 'cst'], writes=[('ps', b)])
            evac_copy(ostg[i][:, g * 512:(g + 1) * 512], PB[b][:, :], reads=[('ps', b)], writes=[('ostg', i)])
        S.dma('sp', out_d[blk], ostg[i], reads=[('ostg', i)], sem=f"st{i}", is_output=True)
    S.emit()
    assert wstate['n'] == NSLOT, wstate['n']
    return nc, wplan


def _pack_weights(wplan, W):
    arr = np.zeros((NSLOT, 128, 4096), np.float32)
    for n, d in enumerate(wplan):
        if d[0] == 'fm2':
            v = arr[n].reshape(128, 2, 16, 128)
            for i in range(2):
                name, c0 = d[1 + i]
                v[:, i] = W[name][:, c0:c0 + 128].reshape(16, 128, 128).transpose(1, 0, 2)
        elif d[0] == 'tm':
            _, name, c0 = d
            arr[n].reshape(128, 16, 256)[:] = W[name][:, c0:c0 + 256].reshape(16, 128, 256).transpose(1, 0, 2)
        elif d[0] == 'rows2':
            _, name, rc = d
            arr[n].reshape(128, 2, 2048)[:] = W[name][rc * 128:(rc + 2) * 128, :].reshape(2, 128, 2048).transpose(1, 0, 2)
        else:
            raise ValueError(d)
    return arr


def _col16(g):
    return np.ascontiguousarray(np.asarray(g, np.float32).reshape(-1, 128).T)


def _consts(inp, s):
    c = np.zeros((128, NCST), np.float32)
    p = np.arange(128, dtype=np.float64)
    c[:, C_ID:C_ID + 128] = np.eye(128, dtype=np.float32)
    c[:, C_ONE:C_ONE + 128] = 1.0
    c[:, C_GMIX:C_GMIX + 16] = _col16(inp['norm_mix_g'][0])
    c[:, C_GX:C_GX + 16] = _col16(inp['norm_x_g'][0])
    c[:, C_GMEM:C_GMEM + 16] = _col16(inp['norm_mem_g'][0])
    c[:, C_GFFN:C_GFFN + 16] = _col16(inp['norm_ffn_g'][0])
    c[:, C_GF:C_GF + 16] = _col16(inp['norm_f_g'])
    c[:, C_GSUB:C_GSUB + 2] = _col16(inp['da_subln_g'][0])
    for i, k in enumerate(('lambda_q1', 'lambda_k1', 'lambda_q2', 'lambda_k2')):
        c[:, C_LAM + i] = np.asarray(inp[k][0], np.float32)
    c[:, C_EPS] = EPS
    for h in range(4):
        slope = 2.0 ** (-8.0 * (h + 1) / 4)
        lg = math.log(1.0 - 2.0 ** (-5.0 - h))
        for idx in range(19):
            dd = idx - 3
            v = slope * (p - 128.0 * dd)
            c[:, C_ABO + h * 19 + idx] = v
            c[:, C_ABC + h * 19 + idx] = v + (0.0 if s == 1 else -30000.0)
        for idx in range(16):
            dd = idx - 3
            c[:, C_KFAC + h * 16 + idx] = np.exp(lg * (128.0 * dd - p))
        c[:, C_QDEC + h * 512:C_QDEC + (h + 1) * 512] = (np.exp(lg * np.arange(512, dtype=np.float64)) * 128.0 ** -0.5)[None, :]
    m = np.ones((128, 512), np.float32)
    m[:, 0:128] = (np.arange(128)[None, :] >= np.arange(128)[:, None]).astype(np.float32)
    c[:, C_MASK:C_MASK + 512] = m
    return c


_CACHE = {}


def _get_program():
    if 'nc' not in _CACHE:
        _CACHE['nc'], _CACHE['wplan'] = build()
    return _CACHE['nc'], _CACHE['wplan']


def make_in_maps(inp, wplan, cores=range(8)):
    W = {k: np.asarray(inp[k][0], np.float32) for k in ('w_in', 'w_o', 'w_xq', 'w_xk', 'w_xv', 'w_xo', 'w_gate', 'w_up', 'w_down')}
    wst = _pack_weights(wplan, W)
    x = np.asarray(inp['x'], np.float32)
    mem = np.asarray(inp['mem'], np.float32)
    maps = []
    for core in cores:
        b, s = core // 2, core % 2
        xa = np.zeros((16, 128, 2048), np.float32)
        if s == 1:
            xa[:] = x[b].reshape(16, 128, 2048)
        else:
            xa[8:] = x[b, :1024].reshape(8, 128, 2048)
        maps.append({"xall": xa, "memb": np.ascontiguousarray(mem[b].reshape(2, 128, 2048)), "cst": _consts(inp, s), "wst": wst})
    return maps


def kernel(**inputs):
    nc, wplan = _get_program()
    in_maps = make_in_maps(inputs, wplan)
    res = run_bass_kernel_spmd(nc, in_maps, core_ids=list(range(8)))
    out = np.empty((4, 2048, 2048), np.float32)
    for core in range(8):
        b, s = core // 2, core % 2
        out[b, s * 1024:(s + 1) * 1024] = np.asarray(res.results[core]["out"], np.float32).reshape(1024, 2048)
    return out
```

```python
import math
from contextlib import ExitStack
import numpy as np
import concourse.bass as bass
import concourse.mybir as mybir
from concourse.bass_utils import run_bass_kernel_spmd

F32 = mybir.dt.float32
BF16 = mybir.dt.bfloat16
AF = mybir.ActivationFunctionType
ALU = mybir.AluOpType
ENGS = ('pe', 'act', 'dve', 'pool', 'sp')

EPS = 1e-6
NSLOT = 130
RING = 4
LAM_INIT = 0.8 - 0.6 * math.exp(0.0)

_c = 0
def _col(n):
    global _c
    s = _c
    _c += n
    return s
C_ID = _col(128)
C_ONE = _col(128)
C_GMIX = _col(16)
C_GX = _col(16)
C_GMEM = _col(16)
C_GFFN = _col(16)
C_GF = _col(16)
C_GSUB = _col(2)
C_LAM = _col(4)
C_ZERO = _col(1)
C_EPS = _col(1)
C_ABO = _col(4 * 19)
C_ABC = _col(4 * 19)
C_KFAC = _col(4 * 16)
C_QDEC = _col(4 * 512)
C_MASK = _col(512)
NCST = _c


class Op:
    __slots__ = ('eng', 'fn', 'deps', 'dma_key', 'dma_val', 'signal', 'sig_ord', 'is_output')

    def __init__(self, eng, fn):
        self.eng = eng
        self.fn = fn
        self.deps = []
        self.dma_key = None
        self.dma_val = 0
        self.signal = False
        self.sig_ord = 0
        self.is_output = False


class Sched:
    def __init__(self, nc):
        self.nc = nc
        self.ops = {e: [] for e in ENGS}
        self.bufs = {}
        self.dma_cnt = {}
        self.last_dma = {}
        self.pending = {e: [] for e in ENGS}
        self.stack = ExitStack()

    def sb(self, name, shape, dtype):
        return self.stack.enter_context(self.nc.sbuf_tensor(name, shape, dtype))

    def ps(self, name, shape, dtype):
        return self.stack.enter_context(self.nc.psum_tensor(name, shape, dtype))

    def _track(self, op, reads, writes):
        deps = list(self.pending[op.eng])
        self.pending[op.eng] = []
        for r in reads:
            b = self.bufs.get(r)
            if b is not None and b['w'] is not None:
                deps.append(b['w'])
        for w in writes:
            b = self.bufs.get(w)
            if b is not None:
                if b['w'] is not None:
                    deps.append(b['w'])
                deps.extend(b['r'].values())
        rk = op.eng if op.dma_key is None else ('dma', op.dma_key)
        for r in reads:
            b = self.bufs.setdefault(r, {'w': None, 'r': {}})
            b['r'][rk] = op
        for w in writes:
            b = self.bufs.setdefault(w, {'w': None, 'r': {}})
            b['w'] = op
            b['r'] = {}
        seen = set()
        for d in deps:
            if d is op or id(d) in seen:
                continue
            seen.add(id(d))
            if d.dma_key is None:
                if d.eng == 'pe' and op.eng == 'pe' and op.dma_key is None:
                    continue
                d.signal = True
            op.deps.append(d)

    def op(self, eng, fn, reads=(), writes=()):
        o = Op(eng, fn)
        self._track(o, reads, writes)
        self.ops[eng].append(o)
        return o

    def dma(self, eng, out, in_, reads=(), writes=(), sem=None, is_output=False, **kw):
        o = Op(eng, lambda e: e.dma_start(out=out, in_=in_, **kw))
        o.dma_key = sem
        self.dma_cnt[sem] = self.dma_cnt.get(sem, 0) + 16
        o.dma_val = self.dma_cnt[sem]
        o.is_output = is_output
        self._track(o, reads, writes)
        self.ops[eng].append(o)
        self.last_dma[sem] = o
        return o

    def barrier(self):
        deps = []
        for e in ENGS:
            for o in reversed(self.ops[e]):
                if o.dma_key is None:
                    deps.append(o)
                    break
        deps.extend(self.last_dma.values())
        for e in ENGS:
            self.pending[e] = list(deps)

    def emit(self):
        nc = self.nc
        for e in ENGS:
            k = 0
            for o in self.ops[e]:
                if o.dma_key is None and o.signal:
                    k += 1
                    o.sig_ord = k
        st = self.stack
        esem = {e: st.enter_context(nc.semaphore(f"s_{e}")) for e in ('pe', 'act', 'dve', 'pool')}
        dsem = {k: st.enter_context(nc.semaphore(f"d_{k}")) for k in self.dma_cnt}
        out_keys = set()
        for e in ENGS:
            for o in self.ops[e]:
                if o.is_output:
                    out_keys.add(o.dma_key)

        def run(eng_name, eng):
            waited = {}
            for o in self.ops[eng_name]:
                for d in o.deps:
                    if d.dma_key is not None:
                        key, sem, val = ('d', d.dma_key), dsem[d.dma_key], d.dma_val
                    else:
                        key, sem, val = ('e', d.eng), esem[d.eng], d.sig_ord
                    if waited.get(key, 0) < val:
                        eng.wait_ge(sem, val)
                        waited[key] = val
                inst = o.fn(eng)
                if o.dma_key is not None:
                    inst.then_inc(dsem[o.dma_key], 16)
                elif o.signal:
                    inst.then_inc(esem[eng_name], 1)
            if eng_name == 'sp':
                for k in sorted(out_keys):
                    eng.wait_ge(dsem[k], self.dma_cnt[k])

        with nc.Block() as block:
            @block.sync
            def _(e):
                run('sp', e)

            @block.tensor
            def _(e):
                run('pe', e)

            @block.scalar
            def _(e):
                run('act', e)

            @block.vector
            def _(e):
                run('dve', e)

            @block.gpsimd
            def _(e):
                run('pool', e)
        st.close()


def build(stage=99, dbg=None):
    nc = bass.Bass("TRN2", target_bir_lowering=False)
    xall = nc.dram_tensor("xall", [16, 128, 2048], F32, kind="ExternalInput").ap()
    memb = nc.dram_tensor("memb", [2, 128, 2048], F32, kind="ExternalInput").ap()
    cst_d = nc.dram_tensor("cst", [128, NCST], F32, kind="ExternalInput").ap()
    wst = nc.dram_tensor("wst", [NSLOT, 128, 4096], F32, kind="ExternalInput").ap()
    out_d = nc.dram_tensor("out", [8, 128, 2048], F32, kind="ExternalOutput").ap()
    dbg_d = {}
    if dbg:
        for name, shp in dbg.items():
            dbg_d[name] = nc.dram_tensor("dbg_" + name, list(shp), F32, kind="ExternalOutput").ap()

    S = Sched(nc)
    cst = S.sb("cst_sb", [128, NCST], F32)
    onesb = S.sb("onesb", [128, 128], BF16)
    maskb = S.sb("maskb", [128, 512], BF16)
    ring = [S.sb(f"ring{i}", [128, 4096], BF16) for i in range(RING)]
    X = S.sb("X", [128, 16384], F32)
    Y = S.sb("Y", [128, 16, 1024], BF16)
    Z = S.sb("Z", [128, 8192], F32)
    sqr = [S.sb(f"sq{i}", [128, 16, 128], BF16) for i in range(2)]
    rsb = [S.sb(f"rs{i}", [128, 128], F32) for i in range(2)]
    ptr = [S.sb(f"pt{i}", [128, 512], BF16) for i in range(4)]
    rcp = [S.sb(f"rcp{i}", [128, 512], F32) for i in range(2)]
    At = S.sb("At", [128, 4, 512], F32)
    T2 = [S.sb(f"t2{i}", [128, 512], F32) for i in range(2)]
    rs512 = S.sb("rs512", [128, 512], F32)
    sq512 = S.sb("sq512", [128, 2, 512], BF16)
    sm = S.sb("sm", [128, 16], F32)
    PB = [S.ps(f"pb{i}", [128, 512], F32) for i in range(8)]

    def zf(a, b):
        return Z[:, a * 256:b * 256]

    def zb(a, b):
        return Z[:, a * 256:b * 256].bitcast(BF16)

    ident = cst[:, C_ID:C_ID + 128]
    onesf = cst[:, C_ONE:C_ONE + 128]
    zero_c = cst[:, C_ZERO:C_ZERO + 1]

    hA = X[:, :].bitcast(BF16).rearrange("p (c t) -> p c t", c=16)
    xT = X[:, :].rearrange("p (c t) -> p c t", c=16)

    wplan = []
    wstate = {'n': 0}

    def wload(desc):
        n = wstate['n']
        wstate['n'] += 1
        wplan.append(desc)
        s = n % RING
        S.dma('pool', ring[s][:, :], wst[n], writes=[('w', s)], sem=f"w{s}", max_dma_last_dim=2048)
        return s

    def w_fm(s):
        return ring[s][:, :].rearrange("p (i k n) -> p i k n", i=2, k=16)

    def w_tm(s):
        return ring[s][:, :].rearrange("p (k n) -> p k n", k=16)

    def w_rows(s):
        return ring[s][:, :].rearrange("p (i n) -> p i n", i=2)

    rot = {}

    def bank(group, banks):
        i = rot.get(group, 0)
        rot[group] = i + 1
        return banks[i % len(banks)]

    evac_flip = {'i': 0}

    def evac_copy(out_ap, in_ap, reads, writes, eng=None):
        if eng is None:
            evac_flip['i'] += 1
            eng = 'dve' if evac_flip['i'] % 3 else 'act'
        if eng == 'dve':
            S.op('dve', lambda e: e.tensor_copy(out_ap, in_ap), reads=reads, writes=writes)
        else:
            S.op('act', lambda e: e.activation(out_ap, in_ap, AF.Copy, scale=1.0), reads=reads, writes=writes)

    S.dma('sp', cst[:, :], cst_d, writes=['cst'], sem='cst')
    S.op('dve', lambda e: e.tensor_copy(onesb[:, :], cst[:, C_ONE:C_ONE + 128]), reads=['cst'], writes=['onesb'])
    S.op('dve', lambda e: e.tensor_copy(maskb[:, :], cst[:, C_MASK:C_MASK + 512]), reads=['cst'], writes=['maskb'])
    S.op('dve', lambda e: e.tensor_tensor(sm[:, 0:1], cst[:, C_LAM:C_LAM + 1], cst[:, C_LAM + 1:C_LAM + 2], ALU.mult),
         reads=['cst'], writes=['sm01'])
    S.op('dve', lambda e: e.tensor_tensor(sm[:, 1:2], cst[:, C_LAM + 2:C_LAM + 3], cst[:, C_LAM + 3:C_LAM + 4], ALU.mult),
         reads=['cst'], writes=['sm01'])
    sp3 = S.sb("sp3", [128, 3, 2], BF16)
    S.op('dve', lambda e: e.tensor_copy(sp3[:, 0, :], sm[:, 0:2]), reads=['sm01'], writes=['sp3a'])
    S.op('dve', lambda e: e.tensor_tensor(sm[:, 8:10], sm[:, 0:2], sp3[:, 0, :], ALU.subtract), reads=['sm01', 'sp3a'], writes=['smr1'])
    S.op('dve', lambda e: e.tensor_copy(sp3[:, 1, :], sm[:, 8:10]), reads=['smr1'], writes=['sp3b'])
    S.op('dve', lambda e: e.tensor_tensor(sm[:, 10:12], sm[:, 8:10], sp3[:, 1, :], ALU.subtract), reads=['smr1', 'sp3b'], writes=['smr2'])
    S.op('dve', lambda e: e.tensor_copy(sp3[:, 2, :], sm[:, 10:12]), reads=['smr2'], writes=['sp3c'])
    for q in range(3):
        S.op('pe', lambda e, q=q: e.matmul(PB[7][:, 0:2], lhsT=onesb[:, :], rhs=sp3[:, q, :], start=(q == 0), stop=(q == 2)),
             reads=['onesb', 'sp3a', 'sp3b', 'sp3c'], writes=[('ps', 7)])
    S.op('act', lambda e: e.activation(sm[:, 2:4], PB[7][:, 0:2], AF.Exp, bias=zero_c, scale=1.0),
         reads=[('ps', 7), 'cst'], writes=['sm23'])
    S.op('dve', lambda e: e.tensor_tensor(sm[:, 4:5], sm[:, 2:3], sm[:, 3:4], ALU.subtract), reads=['sm23'], writes=['sm4'])
    S.op('dve', lambda e: e.tensor_scalar(sm[:, 5:6], sm[:, 4:5], -1.0, -LAM_INIT, ALU.mult, ALU.add), reads=['sm4'], writes=['neglam'])
    S.op('dve', lambda e: e.tensor_scalar(sm[:, 6:8], cst[:, C_GSUB:C_GSUB + 2], 1.0 - LAM_INIT, None, ALU.mult), reads=['cst'], writes=['gsubs'])
    neglam = sm[:, 5:6]

    nb = {'i': 0}

    def norm_block(src, src_keys, gc, dst, dst_keys, D=2048.0):
        i = nb['i'] % 2
        nb['i'] += 1
        sq, rs = sqr[i], rsb[i]
        ss = bank('ss', [7, 6])
        S.op('act', lambda e: e.activation(sq[:, :, :], src, AF.Square, bias=zero_c, scale=1.0),
             reads=list(src_keys) + ['cst'], writes=[('sq', i)])
        for c in range(16):
            S.op('pe', lambda e, c=c: e.matmul(PB[ss][:, 0:128], lhsT=onesb[:, :], rhs=sq[:, c, :], start=(c == 0), stop=(c == 15)),
                 reads=[('sq', i), 'onesb'], writes=[('ps', ss)])
        S.op('act', lambda e: e.activation(rs[:, :], PB[ss][:, 0:128], AF.Sqrt, bias=cst[:, C_EPS:C_EPS + 1], scale=1.0 / D),
             reads=[('ps', ss), 'cst'], writes=[('rs', i)])
        S.op('dve', lambda e: e.reciprocal(rs[:, :], rs[:, :]), reads=[('rs', i)], writes=[('rs', i)])
        for c in range(16):
            S.op('dve', lambda e, c=c: e.scalar_tensor_tensor(dst[:, c, :], src[:, c, :], cst[:, gc + c:gc + c + 1], rs[:, :], ALU.mult, ALU.mult),
                 reads=list(src_keys) + [('rs', i), 'cst'], writes=list(dst_keys))

    def load_T(src_d, stg, stg_key, dst, dst_keys):
        S.dma('sp', stg, src_d, writes=[stg_key], sem=f"ld_{stg_key[1]}")
        for g in range(4):
            b = bank('tp', [0, 1])
            for j in range(4):
                c = 4 * g + j
                S.op('pe', lambda e, c=c, j=j, b=b: e.transpose(PB[b][:, j * 128:(j + 1) * 128], stg[:, c * 128:(c + 1) * 128], ident),
                     reads=[stg_key, 'cst'], writes=[('ps', b)])
            evac_copy(dst[:, 4 * g:4 * g + 4, :], PB[b][:, :].rearrange("p (a t) -> p a t", a=4), reads=[('ps', b)], writes=list(dst_keys))

    def dump(name, ap, reads):
        if name in dbg_d:
            S.dma('sp', dbg_d[name], ap, reads=reads, sem='dbg_' + name, is_output=True)

    stg = [zf(0, 8), zf(8, 16)]
    xTf = [zf(16, 24).rearrange("p (c t) -> p c t", c=16), zf(24, 32).rearrange("p (c t) -> p c t", c=16)]
    load_T(xall[0], stg[0], ('stg', 0), xTf[0], [('xTf', 0)])
    for blk in range(16):
        i = blk % 2
        if blk + 1 < 16:
            load_T(xall[blk + 1], stg[1 - i], ('stg', 1 - i), xTf[1 - i], [('xTf', 1 - i)])
        norm_block(xTf[i], [('xTf', i)], C_GMIX, hA[:, :, blk * 128:(blk + 1) * 128], [('hA', blk)])
    if 'hA' in dbg_d:
        S.barrier()
        S.op('dve', lambda e: e.tensor_copy(Z[:, 0:4096].rearrange("p (c t) -> p c t", c=2), hA[:, 0:2, :]), reads=[('hA', b) for b in range(16)], writes=['zdbg'])
        dump('hA', Z[:, 0:4096], ['zdbg'])
    S.barrier()
    if stage < 1:
        S.emit()
        return nc, wplan

    B1 = zb(0, 4).rearrange("p (m t) -> p m t", m=2)
    B2 = zb(4, 12)
    B3 = zb(12, 20).rearrange("p (b e) -> p b e", b=16)
    hkeys_tc = lambda tc: [('hA', 4 * tc + q) for q in range(4)]
    SC128 = 128.0 ** -0.5

    def proj_fm(wv_, tcs, evac):
        for tc in tcs:
            b = bank('pj', [0, 1])
            for kc in range(16):
                S.op('pe', lambda e, wv_=wv_, kc=kc, tc=tc, b=b: e.matmul(PB[b][:, :], lhsT=wv_[0][:, kc, :], rhs=hA[:, kc, tc * 512:(tc + 1) * 512],
                                                                start=(kc == 0), stop=(kc == 15)),
                     reads=[wv_[1]] + hkeys_tc(tc), writes=[('ps', b)])
            evac(tc, b)

    def proj_v(s):
        wv_ = w_tm(s)
        for bp in range(8):
            b = bank('pj', [0, 1])
            for i in range(2):
                blk = 2 * bp + i
                for kc in range(16):
                    S.op('pe', lambda e, kc=kc, blk=blk, b=b, i=i: e.matmul(PB[b][:, i * 256:(i + 1) * 256], lhsT=hA[:, kc, blk * 128:(blk + 1) * 128],
                                                                        rhs=wv_[:, kc, :], start=(kc == 0), stop=(kc == 15)),
                         reads=[('w', s), ('hA', blk)], writes=[('ps', b)])
            evac_copy(B3[:, 2 * bp:2 * bp + 2, :], PB[b][:, :].rearrange("p (a t) -> p a t", a=2), reads=[('ps', b)],
                      writes=[('B3', 2 * bp), ('B3', 2 * bp + 1)])

    def run_attention(groups):
        units = [(g, j) for g in groups for j in range(8 + 4 * g['c'] + 4)]
        state = {}

        def qk(g, j):
            c = g['c']
            r = max(0, j - (8 + 4 * c))
            n0 = 128 * r
            N = 512 - n0
            sb_ = bank('st', [2, 3])
            kap, kkey = g['kT_of'](j)
            qap, qkey = g['q_of'](c, n0)
            S.op('pe', lambda e, sb_=sb_, kap=kap, qap=qap, N=N: e.matmul(PB[sb_][:, 0:N], lhsT=kap, rhs=qap, start=True, stop=True),
                 reads=[kkey, qkey], writes=[('ps', sb_)])
            pi = bank('pt', [0, 1, 2, 3])
            pt = ptr[pi]
            g['evac_pt'](c, j, r, n0, N, sb_, pt, pi)
            if j >= 8 + 4 * c:
                S.op('dve', lambda e, pt=pt, n0=n0: e.tensor_tensor(pt[:, n0:n0 + 128], pt[:, n0:n0 + 128], maskb[:, 0:128], ALU.mult),
                     reads=[('pt', pi), 'maskb'], writes=[('pt', pi)])
            state[(id(g), j)] = (pt, pi, n0)

        def pv(g, j):
            pt, pi, n0 = state.pop((id(g), j))
            last = 8 + 4 * g['c'] + 3
            o0, o1, smb = g['banks']
            for e_ in range(2):
                ob = (o0, o1)[e_]
                S.op('pe', lambda e, e_=e_, j=j, pt=pt, n0=n0, ob=ob: e.matmul(PB[ob][:, n0:512], lhsT=B3[:, j, e_ * 128:(e_ + 1) * 128], rhs=pt[:, n0:512],
                                                                           start=(j == 0), stop=(j == last)),
                     reads=[('B3', j), ('pt', pi)], writes=[('ps', ob)])
            if g['nsum']:
                S.op('pe', lambda e, pt=pt, n0=n0, j=j, smb=smb: e.matmul(PB[smb][:, n0:512], lhsT=onesb[:, :], rhs=pt[:, n0:512], start=(j == 0), stop=(j == last)),
                     reads=['onesb', ('pt', pi)], writes=[('ps', smb)])

        deferred = []
        qk(*units[0])
        for i, (g, j) in enumerate(units):
            if i + 1 < len(units):
                qk(*units[i + 1])
            pv(g, j)
            for d in deferred:
                d[0] -= 1
            while deferred and deferred[0][0] <= 0:
                deferred.pop(0)[1]()
            if j == 8 + 4 * g['c'] + 3:
                g['after']()
                if g.get('deferred') is not None:
                    deferred.append([3, g['deferred']])
        while deferred:
            deferred.pop(0)[1]()

    def subln_stats(ab, D):
        for e_ in range(2):
            S.op('dve', lambda e, e_=e_: e.tensor_tensor(sq512[:, e_, :], At[:, ab + e_, :], At[:, ab + e_, :], ALU.mult), reads=[('At', ab + e_)], writes=[('sq512', e_)])
        sb_ = bank('st', [2, 3])
        for e_ in range(2):
            S.op('pe', lambda e, e_=e_, sb_=sb_: e.matmul(PB[sb_][:, :], lhsT=onesb[:, :], rhs=sq512[:, e_, :], start=(e_ == 0), stop=(e_ == 1)),
                 reads=['onesb', ('sq512', e_)], writes=[('ps', sb_)])
        S.op('act', lambda e, sb_=sb_: e.activation(rs512[:, :], PB[sb_][:, :], AF.Sqrt, bias=cst[:, C_EPS:C_EPS + 1], scale=1.0 / D),
             reads=[('ps', sb_), 'cst'], writes=['rs512'])
        S.op('dve', lambda e: e.reciprocal(rs512[:, :], rs512[:, :]), reads=['rs512'], writes=['rs512'])

    PSETS = [(4, 5, 6), (0, 1, 7)]

    for h in range(4):
        sq_, sk_, sv_ = None, None, None
        sq_ = wload(('fm2', ('w_in', h * 256), ('w_in', h * 256 + 128)))
        wq = w_fm(sq_)
        for m in range(2):
            proj_fm((wq[:, m], ('w', sq_)), [2, 3],
                    lambda tc, b, m=m: evac_copy(B1[:, m, (tc - 2) * 512:(tc - 1) * 512], PB[b][:, :], reads=[('ps', b)], writes=[('B1', m * 2 + tc - 2)]))
        sk_ = wload(('fm2', ('w_in', 1024 + h * 256), ('w_in', 1024 + h * 256 + 128)))
        wk = w_fm(sk_)
        for m in range(2):
            proj_fm((wk[:, m], ('w', sk_)), [0, 1, 2, 3],
                    lambda tc, b, m=m: evac_copy(B2[:, m * 2048 + tc * 512:m * 2048 + (tc + 1) * 512], PB[b][:, :], reads=[('ps', b)], writes=[('B2', m * 4 + tc)]))
        sv_ = wload(('tm', 'w_in', 2048 + h * 256))
        proj_v(sv_)
        split = (2.0 ** (-2.0 * (h + 1))) * 511 > 60.0
        groups = []
        for c in range(2):
            for m in range(2):
                def evac_exp(c, j, r, n0, N, sb_, pt, pi, m=m, h=h, split=split):
                    D = 8 + 4 * c - j
                    tab = C_ABC if j < 8 else C_ABO
                    if split:
                        for rr in range(r, 4):
                            col = tab + h * 19 + (D + rr) + 3
                            S.op('act', lambda e, rr=rr, col=col: e.activation(pt[:, rr * 128:(rr + 1) * 128], PB[sb_][:, (rr - r) * 128:(rr - r + 1) * 128],
                                                                               AF.Exp, bias=cst[:, col:col + 1], scale=SC128),
                                 reads=[('ps', sb_), 'cst'], writes=[('pt', pi)])
                    else:
                        col = tab + h * 19 + D + 3
                        S.op('act', lambda e, col=col: e.activation(pt[:, n0:512], PB[sb_][:, 0:N], AF.Exp, bias=cst[:, col:col + 1], scale=SC128),
                             reads=[('ps', sb_), 'cst'], writes=[('pt', pi)])

                pset = PSETS[(c * 2 + m) % 2]
                ab = 2 * (c % 2)

                def after(c=c, m=m, pset=pset, ab=ab):
                    rc = rcp[m]
                    S.op('dve', lambda e: e.reciprocal(rc[:, :], PB[pset[2]][:, :]), reads=[('ps', pset[2])], writes=[('rcp', m)])
                    for e_ in range(2):
                        if m == 0:
                            S.op('dve', lambda e, e_=e_: e.tensor_tensor(At[:, ab + e_, :], PB[pset[e_]][:, :], rc[:, :], ALU.mult),
                                 reads=[('ps', pset[e_]), ('rcp', m)], writes=[('At', ab + e_)])
                        else:
                            S.op('dve', lambda e, e_=e_: e.tensor_tensor(T2[e_][:, :], PB[pset[e_]][:, :], rc[:, :], ALU.mult),
                                 reads=[('ps', pset[e_]), ('rcp', m)], writes=[('T2', e_)])
                            S.op('dve', lambda e, e_=e_: e.scalar_tensor_tensor(At[:, ab + e_, :], T2[e_][:, :], neglam, At[:, ab + e_, :], ALU.mult, ALU.add),
                                 reads=[('T2', e_), ('At', ab + e_), 'neglam'], writes=[('At', ab + e_)])

                def fin(c=c, h=h, ab=ab):
                    subln_stats(ab, 256.0)
                    for e_ in range(2):
                        S.op('dve', lambda e, e_=e_: e.scalar_tensor_tensor(Y[:, h * 2 + e_, c * 512:(c + 1) * 512], At[:, ab + e_, :], sm[:, 6 + e_:7 + e_], rs512[:, :],
                                                                            ALU.mult, ALU.mult),
                             reads=[('At', ab + e_), 'gsubs', 'rs512'], writes=[('Y', h * 2 + e_, c)])

                groups.append(dict(c=c, nsum=True, banks=pset, evac_pt=evac_exp, after=after, deferred=(fin if m == 1 else None),
                                   kT_of=(lambda j, m=m: (B2[:, m * 2048 + j * 128:m * 2048 + (j + 1) * 128], ('B2', m * 4 + j // 4))),
                                   q_of=(lambda c, n0, m=m: (B1[:, m, c * 512 + n0:(c + 1) * 512], ('B1', m * 2 + c)))))
        run_attention(groups)

        sqk = wload(('fm2', ('w_in', 3072 + h * 128), ('w_in', 3584 + h * 128)))
        wqk = w_fm(sqk)
        qd = cst[:, C_QDEC + h * 512:C_QDEC + (h + 1) * 512]
        proj_fm((wqk[:, 0], ('w', sqk)), [2, 3],
                lambda tc, b, qd=qd: S.op('dve', lambda e: e.tensor_tensor(B2[:, 2048 + (tc - 2) * 512:2048 + (tc - 1) * 512], PB[b][:, :], qd, ALU.mult),
                                   reads=[('ps', b), 'cst'], writes=[('B2', 4 + tc - 2)]))
        proj_fm((wqk[:, 1], ('w', sqk)), [0, 1, 2, 3],
                lambda tc, b: evac_copy(B2[:, tc * 512:(tc + 1) * 512], PB[b][:, :], reads=[('ps', b)], writes=[('B2', tc)]))
        sg_ = wload(('fm2', ('w_in', 5120 + h * 256), ('w_in', 5120 + h * 256 + 128)))
        wg = w_fm(sg_)
        for e_ in range(2):
            proj_fm((wg[:, e_], ('w', sg_)), [2, 3],
                    lambda tc, b, e_=e_: S.op('act', lambda e: e.activation(B1[:, e_, (tc - 2) * 512:(tc - 1) * 512], PB[b][:, :], AF.Silu, bias=zero_c, scale=1.0),
                                              reads=[('ps', b), 'cst'], writes=[('B1', e_ * 2 + tc - 2)]))
        srv = wload(('tm', 'w_in', 4096 + h * 256))
        proj_v(srv)
        groups = []
        for c in range(2):
            def evac_ret(c, j, r, n0, N, sb_, pt, pi, h=h):
                D = 8 + 4 * c - j
                col = C_KFAC + h * 16 + D + 3
                S.op('dve', lambda e: e.tensor_scalar(pt[:, n0:512], PB[sb_][:, 0:N], cst[:, col:col + 1], None, ALU.mult),
                     reads=[('ps', sb_), 'cst'], writes=[('pt', pi)])

            pset = PSETS[c % 2]
            ab = 2 * (c % 2)

            def after(pset=pset, ab=ab):
                for e_ in range(2):
                    evac_copy(At[:, ab + e_, :], PB[pset[e_]][:, :], reads=[('ps', pset[e_])], writes=[('At', ab + e_)])

            def fin(c=c, h=h, ab=ab):
                subln_stats(ab, 256.0)
                for e_ in range(2):
                    S.op('dve', lambda e, e_=e_: e.tensor_tensor(T2[e_][:, :], At[:, ab + e_, :], rs512[:, :], ALU.mult),
                         reads=[('At', ab + e_), 'rs512'], writes=[('T2', e_)])
                    S.op('dve', lambda e, e_=e_: e.tensor_tensor(Y[:, 8 + h * 2 + e_, c * 512:(c + 1) * 512], T2[e_][:, :], B1[:, e_, c * 512:(c + 1) * 512], ALU.mult),
                         reads=[('T2', e_), ('B1', e_ * 2 + c)], writes=[('Y', 8 + h * 2 + e_, c)])

            groups.append(dict(c=c, nsum=False, banks=pset, evac_pt=evac_ret, after=after, deferred=fin,
                               kT_of=(lambda j: (B2[:, j * 128:(j + 1) * 128], ('B2', j // 4))),
                               q_of=(lambda c, n0: (B2[:, 2048 + c * 512 + n0:2048 + (c + 1) * 512], ('B2', 4 + c)))))
        run_attention(groups)
    if 'Y' in dbg_d:
        S.barrier()
        S.op('dve', lambda e: e.tensor_copy(Z[:, :].rearrange("p (c t) -> p c t", c=16), Y[:, :, 0:512]), reads=[('Y', c, 0) for c in range(16)], writes=['zdbg'])
        dump('Y', Z[:, :], ['zdbg'])
    S.barrier()
    if stage < 2:
        S.emit()
        return nc, wplan

    xkeys_blk = lambda blk: [('xT', c, blk // 4) for c in range(16)]
    for blk in range(8):
        i = blk % 2
        load_T(xall[8 + blk], stg[i], ('stg', i), xT[:, :, blk * 128:(blk + 1) * 128], xkeys_blk(blk))

    def add_evac(c, tc, b):
        S.op('dve', lambda e: e.tensor_tensor(xT[:, c, tc * 512:(tc + 1) * 512], PB[b][:, :], xT[:, c, tc * 512:(tc + 1) * 512], ALU.add),
             reads=[('ps', b), ('xT', c, tc)], writes=[('xT', c, tc)])

    for sp_ in range(8):
        s = wload(('fm2', ('w_o', sp_ * 256), ('w_o', sp_ * 256 + 128)))
        wv_ = w_fm(s)
        for i in range(2):
            c = 2 * sp_ + i
            for tc in range(2):
                b = bank('pj', [0, 1])
                for kc in range(16):
                    S.op('pe', lambda e, wv_=wv_, kc=kc, tc=tc, b=b, i=i: e.matmul(PB[b][:, :], lhsT=wv_[:, i, kc, :], rhs=Y[:, kc, tc * 512:(tc + 1) * 512],
                                                                        start=(kc == 0), stop=(kc == 15)),
                         reads=[('w', s), ('Y', kc, tc)], writes=[('ps', b)])
                add_evac(c, tc, b)

    def norm_x_to_Y(gc):
        for blk in range(8):
            norm_block(xT[:, :, blk * 128:(blk + 1) * 128], xkeys_blk(blk), gc, Y[:, :, blk * 128:(blk + 1) * 128], [('Y', c, blk // 4) for c in range(16)])

    if 'x1' in dbg_d:
        S.barrier()
        dump('x1', X[:, 0:8192], [('xT', c, tc) for c in range(16) for tc in range(2)])
    if stage < 3:
        S.barrier()
        S.emit()
        return nc, wplan

    norm_x_to_Y(C_GX)
    S.barrier()
    hmT = zb(0, 8).rearrange("p (c t) -> p c t", c=16)
    stg_m = zf(8, 16)
    xTf_m = zf(16, 24).rearrange("p (c t) -> p c t", c=16)
    for mb in range(2):
        load_T(memb[mb], stg_m, ('stg', 0), xTf_m, [('xTf', 0)])
        norm_block(xTf_m, [('xTf', 0)], C_GMEM, hmT[:, :, mb * 128:(mb + 1) * 128], [('hm', mb)])
    S.barrier()
    xq = zb(8, 16).rearrange("p (c t) -> p c t", c=4)
    xk = zb(16, 18).rearrange("p (c t) -> p c t", c=4)
    xv = zb(18, 20).rearrange("p (b e) -> p b e", b=2)
    xo = zb(20, 28).rearrange("p (c t) -> p c t", c=4)
    SC512 = 512.0 ** -0.5
    for h in range(4):
        for half in range(2):
            s = wload(('fm2', ('w_xk', h * 512 + half * 256), ('w_xk', h * 512 + half * 256 + 128)))
            wv_ = w_fm(s)
            for i in range(2):
                ch = half * 2 + i
                b = bank('pj', [0, 1])
                for kc in range(16):
                    S.op('pe', lambda e, wv_=wv_, kc=kc, b=b, i=i: e.matmul(PB[b][:, 0:256], lhsT=wv_[:, i, kc, :], rhs=hmT[:, kc, :], start=(kc == 0), stop=(kc == 15)),
                         reads=[('w', s), ('hm', 0), ('hm', 1)], writes=[('ps', b)])
                evac_copy(xk[:, ch, :], PB[b][:, 0:256], reads=[('ps', b)], writes=[('xk', ch)])
        for half in range(2):
            s = wload(('tm', 'w_xv', h * 512 + half * 256))
            wv_ = w_tm(s)
            for mb in range(2):
                b = bank('pj', [0, 1])
                for kc in range(16):
                    S.op('pe', lambda e, wv_=wv_, kc=kc, b=b, mb=mb: e.matmul(PB[b][:, 0:256], lhsT=hmT[:, kc, mb * 128:(mb + 1) * 128], rhs=wv_[:, kc, :], start=(kc == 0), stop=(kc == 15)),
                         reads=[('w', s), ('hm', mb)], writes=[('ps', b)])
                evac_copy(xv[:, mb, half * 256:(half + 1) * 256], PB[b][:, 0:256], reads=[('ps', b)], writes=[('xv', mb)])
        for half in range(2):
            s = wload(('fm2', ('w_xq', h * 512 + half * 256), ('w_xq', h * 512 + half * 256 + 128)))
            wv_ = w_fm(s)
            for i in range(2):
                ch = half * 2 + i
                for tc in range(2):
                    b = bank('pj', [0, 1])
                    for kc in range(16):
                        S.op('pe', lambda e, wv_=wv_, kc=kc, b=b, i=i, tc=tc: e.matmul(PB[b][:, :], lhsT=wv_[:, i, kc, :], rhs=Y[:, kc, tc * 512:(tc + 1) * 512], start=(kc == 0), stop=(kc == 15)),
                             reads=[('w', s), ('Y', kc, tc)], writes=[('ps', b)])
                    evac_copy(xq[:, ch, tc * 512:(tc + 1) * 512], PB[b][:, :], reads=[('ps', b)], writes=[('xq', ch, tc)])
        for tc in range(2):
            for mb in range(2):
                sb_ = bank('st', [2, 3])
                for ch in range(4):
                    S.op('pe', lambda e, ch=ch, sb_=sb_, mb=mb, tc=tc: e.matmul(PB[sb_][:, :], lhsT=xk[:, ch, mb * 128:(mb + 1) * 128], rhs=xq[:, ch, tc * 512:(tc + 1) * 512],
                                                                          start=(ch == 0), stop=(ch == 3)),
                         reads=[('xk', ch), ('xq', ch, tc)], writes=[('ps', sb_)])
                pi = bank('pt', [0, 1, 2, 3])
                pt = ptr[pi]
                S.op('act', lambda e, pt=pt, sb_=sb_: e.activation(pt[:, :], PB[sb_][:, :], AF.Exp, bias=zero_c, scale=SC512),
                     reads=[('ps', sb_), 'cst'], writes=[('pt', pi)])
                for e_ in range(4):
                    S.op('pe', lambda e, e_=e_, pt=pt, mb=mb: e.matmul(PB[4 + e_][:, :], lhsT=xv[:, mb, e_ * 128:(e_ + 1) * 128], rhs=pt[:, :], start=(mb == 0), stop=(mb == 1)),
                         reads=[('xv', mb), ('pt', pi)], writes=[('ps', 4 + e_)])
                S.op('pe', lambda e, pt=pt, mb=mb: e.matmul(PB[1][:, :], lhsT=onesb[:, :], rhs=pt[:, :], start=(mb == 0), stop=(mb == 1)),
                     reads=['onesb', ('pt', pi)], writes=[('ps', 1)])
            rc = rcp[tc]
            S.op('dve', lambda e, rc=rc: e.reciprocal(rc[:, :], PB[1][:, :]), reads=[('ps', 1)], writes=[('rcp', tc)])
            for e_ in range(4):
                S.op('dve', lambda e, e_=e_, rc=rc, tc=tc: e.tensor_tensor(xo[:, e_, tc * 512:(tc + 1) * 512], PB[4 + e_][:, :], rc[:, :], ALU.mult),
                     reads=[('ps', 4 + e_), ('rcp', tc)], writes=[('xo', e_, tc)])
        so = [wload(('rows2', 'w_xo', 4 * h + 2 * q)) for q in range(2)]
        wo = [w_rows(s) for s in so]
        for c in range(16):
            for tc in range(2):
                b = bank('pj', [0, 1])
                for e_ in range(4):
                    S.op('pe', lambda e, wo=wo, e_=e_, c=c, tc=tc, b=b: e.matmul(PB[b][:, :], lhsT=wo[e_ // 2][:, e_ % 2, c * 128:(c + 1) * 128], rhs=xo[:, e_, tc * 512:(tc + 1) * 512],
                                                                        start=(e_ == 0), stop=(e_ == 3)),
                         reads=[('w', so[e_ // 2]), ('xo', e_, tc)], writes=[('ps', b)])
                add_evac(c, tc, b)
    if 'x2' in dbg_d:
        S.barrier()
        dump('x2', X[:, 0:8192], [('xT', c, tc) for c in range(16) for tc in range(2)])
    S.barrier()
    if stage < 4:
        S.emit()
        return nc, wplan

    norm_x_to_Y(C_GFFN)
    act = [zb(0, 8).rearrange("p (f t) -> p f t", f=4), zb(8, 16).rearrange("p (f t) -> p f t", f=4)]
    for g in range(11):
        ab = act[g % 2]
        for fi in range(4):
            f = 4 * g + fi
            s = wload(('fm2', ('w_gate', f * 128), ('w_up', f * 128)))
            wv_ = w_fm(s)
            for tc in range(2):
                bg = bank('gu', [0, 1, 2, 3])
                bu = bank('gu', [0, 1, 2, 3])
                for (i, b) in ((0, bg), (1, bu)):
                    for kc in range(16):
                        S.op('pe', lambda e, wv_=wv_, kc=kc, b=b, i=i, tc=tc: e.matmul(PB[b][:, :], lhsT=wv_[:, i, kc, :], rhs=Y[:, kc, tc * 512:(tc + 1) * 512], start=(kc == 0), stop=(kc == 15)),
                             reads=[('w', s), ('Y', kc, tc)], writes=[('ps', b)])
                ti = bank('t2', [0, 1])
                S.op('act', lambda e, ti=ti, bg=bg: e.activation(T2[ti][:, :], PB[bg][:, :], AF.Silu, bias=zero_c, scale=1.0),
                     reads=[('ps', bg), 'cst'], writes=[('T2', ti)])
                S.op('dve', lambda e, ti=ti, bu=bu, fi=fi, tc=tc, ab=ab: e.tensor_tensor(ab[:, fi, tc * 512:(tc + 1) * 512], PB[bu][:, :], T2[ti][:, :], ALU.mult),
                     reads=[('ps', bu), ('T2', ti)], writes=[('act', g % 2, fi, tc)])
        sd = [wload(('rows2', 'w_down', 4 * g + 2 * q)) for q in range(2)]
        wd = [w_rows(s) for s in sd]
        for c in range(16):
            for tc in range(2):
                b = bank('dn', [4, 5, 6, 7])
                for fi in range(4):
                    S.op('pe', lambda e, wd=wd, fi=fi, c=c, tc=tc, b=b, ab=ab: e.matmul(PB[b][:, :], lhsT=wd[fi // 2][:, fi % 2, c * 128:(c + 1) * 128], rhs=ab[:, fi, tc * 512:(tc + 1) * 512],
                                                                              start=(fi == 0), stop=(fi == 3)),
                         reads=[('w', sd[fi // 2]), ('act', g % 2, fi, tc)], writes=[('ps', b)])
                add_evac(c, tc, b)
    S.barrier()

    yTf = [zf(0, 8).rearrange("p (c t) -> p c t", c=16), zf(8, 16).rearrange("p (c t) -> p c t", c=16)]
    ostg = [zf(16, 24), zf(24, 32)]
    for blk in range(8):
        i = blk % 2
        norm_block(xT[:, :, blk * 128:(blk + 1) * 128], xkeys_blk(blk), C_GF, yTf[i], [('yTf', i)])
        for g in range(4):
            b = bank('tp', [0, 1])
            for j in range(4):
                c = 4 * g + j
                S.op('pe', lambda e, c=c, j=j, b=b, i=i: e.transpose(PB[b][:, j * 128:(j + 1) * 128], yTf[i][:, c, :], ident),
                     reads=[('yTf', i), 'cst'], writes=[('ps', b)])
            evac_copy(ostg[i][:, g * 512:(g + 1) * 512], PB[b][:, :], reads=[('ps', b)], writes=[('ostg', i)])
        S.dma('sp', out_d[blk], ostg[i], reads=[('ostg', i)], sem=f"st{i}", is_output=True)
    S.emit()
    assert wstate['n'] == NSLOT, wstate['n']
    return nc, wplan


def _pack_weights(wplan, W):
    arr = np.zeros((NSLOT, 128, 4096), np.float32)
    for n, d in enumerate(wplan):
        if d[0] == 'fm2':
            v = arr[n].reshape(128, 2, 16, 128)
            for i in range(2):
                name, c0 = d[1 + i]
                v[:, i] = W[name][:, c0:c0 + 128].reshape(16, 128, 128).transpose(1, 0, 2)
        elif d[0] == 'tm':
            _, name, c0 = d
            arr[n].reshape(128, 16, 256)[:] = W[name][:, c0:c0 + 256].reshape(16, 128, 256).transpose(1, 0, 2)
        elif d[0] == 'rows2':
            _, name, rc = d
            arr[n].reshape(128, 2, 2048)[:] = W[name][rc * 128:(rc + 2) * 128, :].reshape(2, 128, 2048).transpose(1, 0, 2)
        else:
            raise ValueError(d)
    return arr


def _col16(g):
    return np.ascontiguousarray(np.asarray(g, np.float32).reshape(-1, 128).T)


def _consts(inp, s):
    c = np.zeros((128, NCST), np.float32)
    p = np.arange(128, dtype=np.float64)
    c[:, C_ID:C_ID + 128] = np.eye(128, dtype=np.float32)
    c[:, C_ONE:C_ONE + 128] = 1.0
    c[:, C_GMIX:C_GMIX + 16] = _col16(inp['norm_mix_g'][0])
    c[:, C_GX:C_GX + 16] = _col16(inp['norm_x_g'][0])
    c[:, C_GMEM:C_GMEM + 16] = _col16(inp['norm_mem_g'][0])
    c[:, C_GFFN:C_GFFN + 16] = _col16(inp['norm_ffn_g'][0])
    c[:, C_GF:C_GF + 16] = _col16(inp['norm_f_g'])
    c[:, C_GSUB:C_GSUB + 2] = _col16(inp['da_subln_g'][0])
    for i, k in enumerate(('lambda_q1', 'lambda_k1', 'lambda_q2', 'lambda_k2')):
        c[:, C_LAM + i] = np.asarray(inp[k][0], np.float32)
    c[:, C_EPS] = EPS
    for h in range(4):
        slope = 2.0 ** (-8.0 * (h + 1) / 4)
        lg = math.log(1.0 - 2.0 ** (-5.0 - h))
        for idx in range(19):
            dd = idx - 3
            v = slope * (p - 128.0 * dd)
            c[:, C_ABO + h * 19 + idx] = v
            c[:, C_ABC + h * 19 + idx] = v + (0.0 if s == 1 else -30000.0)
        for idx in range(16):
            dd = idx - 3
            c[:, C_KFAC + h * 16 + idx] = np.exp(lg * (128.0 * dd - p))
        c[:, C_QDEC + h * 512:C_QDEC + (h + 1) * 512] = (np.exp(lg * np.arange(512, dtype=np.float64)) * 128.0 ** -0.5)[None, :]
    m = np.ones((128, 512), np.float32)
    m[:, 0:128] = (np.arange(128)[None, :] >= np.arange(128)[:, None]).astype(np.float32)
    c[:, C_MASK:C_MASK + 512] = m
    return c


_CACHE = {}


def _get_program():
    if 'nc' not in _CACHE:
        _CACHE['nc'], _CACHE['wplan'] = build()
    return _CACHE['nc'], _CACHE['wplan']


def make_in_maps(inp, wplan, cores=range(8)):
    W = {k: np.asarray(inp[k][0], np.float32) for k in ('w_in', 'w_o', 'w_xq', 'w_xk', 'w_xv', 'w_xo', 'w_gate', 'w_up', 'w_down')}
    wst = _pack_weights(wplan, W)
    x = np.asarray(inp['x'], np.float32)
    mem = np.asarray(inp['mem'], np.float32)
    maps = []
    for core in cores:
        b, s = core // 2, core % 2
        xa = np.zeros((16, 128, 2048), np.float32)
        if s == 1:
            xa[:] = x[b].reshape(16, 128, 2048)
        else:
            xa[8:] = x[b, :1024].reshape(8, 128, 2048)
        maps.append({"xall": xa, "memb": np.ascontiguousarray(mem[b].reshape(2, 128, 2048)), "cst": _consts(inp, s), "wst": wst})
    return maps


def kernel(**inputs):
    nc, wplan = _get_program()
    in_maps = make_in_maps(inputs, wplan)
    res = run_bass_kernel_spmd(nc, in_maps, core_ids=list(range(8)))
    out = np.empty((4, 2048, 2048), np.float32)
    for core in range(8):
        b, s = core // 2, core % 2
        out[b, s * 1024:(s + 1) * 1024] = np.asarray(res.results[core]["out"], np.float32).reshape(1024, 2048)
    return out
```

```python
import math
from contextlib import ExitStack
import numpy as np
import concourse.bass as bass
import concourse.mybir as mybir
from concourse.bass_utils import run_bass_kernel_spmd

F32 = mybir.dt.float32
BF16 = mybir.dt.bfloat16
AF = mybir.ActivationFunctionType
ALU = mybir.AluOpType
ENGS = ('pe', 'act', 'dve', 'pool', 'sp')

EPS = 1e-6
NSLOT = 130
RING = 4
LAM_INIT = 0.8 - 0.6 * math.exp(0.0)

_c = 0
def _col(n):
    global _c
    s = _c
    _c += n
    return s
C_ID = _col(128)
C_ONE = _col(128)
C_GMIX = _col(16)
C_GX = _col(16)
C_GMEM = _col(16)
C_GFFN = _col(16)
C_GF = _col(16)
C_GSUB = _col(2)
C_LAM = _col(4)
C_ZERO = _col(1)
C_EPS = _col(1)
C_ABO = _col(4 * 19)
C_ABC = _col(4 * 19)
C_KFAC = _col(4 * 16)
C_QDEC = _col(4 * 512)
C_MASK = _col(512)
NCST = _c


class Op:
    __slots__ = ('eng', 'fn', 'deps', 'dma_key', 'dma_val', 'signal', 'sig_ord', 'is_output')

    def __init__(self, eng, fn):
        self.eng = eng
        self.fn = fn
        self.deps = []
        self.dma_key = None
        self.dma_val = 0
        self.signal = False
        self.sig_ord = 0
        self.is_output = False


class Sched:
    def __init__(self, nc):
        self.nc = nc
        self.ops = {e: [] for e in ENGS}
        self.bufs = {}
        self.dma_cnt = {}
        self.last_dma = {}
        self.pending = {e: [] for e in ENGS}
        self.stack = ExitStack()

    def sb(self, name, shape, dtype):
        return self.stack.enter_context(self.nc.sbuf_tensor(name, shape, dtype))

    def ps(self, name, shape, dtype):
        return self.stack.enter_context(self.nc.psum_tensor(name, shape, dtype))

    def _track(self, op, reads, writes):
        deps = list(self.pending[op.eng])
        self.pending[op.eng] = []
        for r in reads:
            b = self.bufs.get(r)
            if b is not None and b['w'] is not None:
                deps.append(b['w'])
        for w in writes:
            b = self.bufs.get(w)
            if b is not None:
                if b['w'] is not None:
                    deps.append(b['w'])
                deps.extend(b['r'].values())
        rk = op.eng if op.dma_key is None else ('dma', op.dma_key)
        for r in reads:
            b = self.bufs.setdefault(r, {'w': None, 'r': {}})
            b['r'][rk] = op
        for w in writes:
            b = self.bufs.setdefault(w, {'w': None, 'r': {}})
            b['w'] = op
            b['r'] = {}
        seen = set()
        for d in deps:
            if d is op or id(d) in seen:
                continue
            seen.add(id(d))
            if d.dma_key is None:
                if d.eng == 'pe' and op.eng == 'pe' and op.dma_key is None:
                    continue
                d.signal = True
            op.deps.append(d)

    def op(self, eng, fn, reads=(), writes=()):
        o = Op(eng, fn)
        self._track(o, reads, writes)
        self.ops[eng].append(o)
        return o

    def dma(self, eng, out, in_, reads=(), writes=(), sem=None, is_output=False, **kw):
        o = Op(eng, lambda e: e.dma_start(out=out, in_=in_, **kw))
        o.dma_key = sem
        self.dma_cnt[sem] = self.dma_cnt.get(sem, 0) + 16
        o.dma_val = self.dma_cnt[sem]
        o.is_output = is_output
        self._track(o, reads, writes)
        self.ops[eng].append(o)
        self.last_dma[sem] = o
        return o

    def barrier(self):
        deps = []
        for e in ENGS:
            for o in reversed(self.ops[e]):
                if o.dma_key is None:
                    deps.append(o)
                    break
        deps.extend(self.last_dma.values())
        for e in ENGS:
            self.pending[e] = list(deps)

    def emit(self):
        nc = self.nc
        for e in ENGS:
            k = 0
            for o in self.ops[e]:
                if o.dma_key is None and o.signal:
                    k += 1
                    o.sig_ord = k
        st = self.stack
        esem = {e: st.enter_context(nc.semaphore(f"s_{e}")) for e in ('pe', 'act', 'dve', 'pool')}
        dsem = {k: st.enter_context(nc.semaphore(f"d_{k}")) for k in self.dma_cnt}
        out_keys = set()
        for e in ENGS:
            for o in self.ops[e]:
                if o.is_output:
                    out_keys.add(o.dma_key)

        def run(eng_name, eng):
            waited = {}
            for o in self.ops[eng_name]:
                for d in o.deps:
                    if d.dma_key is not None:
                        key, sem, val = ('d', d.dma_key), dsem[d.dma_key], d.dma_val
                    else:
                        key, sem, val = ('e', d.eng), esem[d.eng], d.sig_ord
                    if waited.get(key, 0) < val:
                        eng.wait_ge(sem, val)
                        waited[key] = val
                inst = o.fn(eng)
                if o.dma_key is not None:
                    inst.then_inc(dsem[o.dma_key], 16)
                elif o.signal:
                    inst.then_inc(esem[eng_name], 1)
            if eng_name == 'sp':
                for k in sorted(out_keys):
                    eng.wait_ge(dsem[k], self.dma_cnt[k])

        with nc.Block() as block:
            @block.sync
            def _(e):
                run('sp', e)

            @block.tensor
            def _(e):
                run('pe', e)

            @block.scalar
            def _(e):
                run('act', e)

            @block.vector
            def _(e):
                run('dve', e)

            @block.gpsimd
            def _(e):
                run('pool', e)
        st.close()


def build(stage=99, dbg=None):
    nc = bass.Bass("TRN2", target_bir_lowering=False)
    xall = nc.dram_tensor("xall", [16, 128, 2048], F32, kind="ExternalInput").ap()
    memb = nc.dram_tensor("memb", [2, 128, 2048], F32, kind="ExternalInput").ap()
    cst_d = nc.dram_tensor("cst", [128, NCST], F32, kind="ExternalInput").ap()
    wst = nc.dram_tensor("wst", [NSLOT, 128, 4096], F32, kind="ExternalInput").ap()
    gfb_d = nc.dram_tensor("gfb", [128, 2048], F32, kind="ExternalInput").ap()
    out_d = nc.dram_tensor("out", [8, 128, 2048], F32, kind="ExternalOutput").ap()
    dbg_d = {}
    if dbg:
        for name, shp in dbg.items():
            dbg_d[name] = nc.dram_tensor("dbg_" + name, list(shp), F32, kind="ExternalOutput").ap()

    S = Sched(nc)
    cst = S.sb("cst_sb", [128, NCST], F32)
    onesb = S.sb("onesb", [128, 128], BF16)
    maskb = S.sb("maskb", [128, 512], BF16)
    ring = [S.sb(f"ring{i}", [128, 4096], BF16) for i in range(RING)]
    X = S.sb("X", [128, 16384], F32)
    Y = S.sb("Y", [128, 16, 1024], BF16)
    Z = S.sb("Z", [128, 8192], F32)
    sqr = [S.sb(f"sq{i}", [128, 16, 128], BF16) for i in range(2)]
    rsb = [S.sb(f"rs{i}", [128, 128], F32) for i in range(2)]
    ptr = [S.sb(f"pt{i}", [128, 512], BF16) for i in range(4)]
    rcp = [S.sb(f"rcp{i}", [128, 512], F32) for i in range(2)]
    At = S.sb("At", [128, 4, 512], F32)
    T2 = [S.sb(f"t2{i}", [128, 512], F32) for i in range(2)]
    rs512 = S.sb("rs512", [128, 512], F32)
    sq512 = S.sb("sq512", [128, 2, 512], BF16)
    sm = S.sb("sm", [128, 16], F32)
    PB = [S.ps(f"pb{i}", [128, 512], F32) for i in range(8)]

    def zf(a, b):
        return Z[:, a * 256:b * 256]

    def zb(a, b):
        return Z[:, a * 256:b * 256].bitcast(BF16)

    ident = cst[:, C_ID:C_ID + 128]
    onesf = cst[:, C_ONE:C_ONE + 128]
    zero_c = cst[:, C_ZERO:C_ZERO + 1]

    hA = X[:, :].bitcast(BF16).rearrange("p (c t) -> p c t", c=16)
    xT = X[:, :].rearrange("p (c t) -> p c t", c=16)

    wplan = []
    wstate = {'n': 0}

    def wload(desc):
        n = wstate['n']
        wstate['n'] += 1
        wplan.append(desc)
        s = n % RING
        S.dma('pool', ring[s][:, :], wst[n], writes=[('w', s)], sem=f"w{s}", max_dma_last_dim=2048)
        return s

    def w_fm(s):
        return ring[s][:, :].rearrange("p (i k n) -> p i k n", i=2, k=16)

    def w_tm(s):
        return ring[s][:, :].rearrange("p (k n) -> p k n", k=16)

    def w_rows(s):
        return ring[s][:, :].rearrange("p (i n) -> p i n", i=2)

    rot = {}

    def bank(group, banks):
        i = rot.get(group, 0)
        rot[group] = i + 1
        return banks[i % len(banks)]

    evac_flip = {'i': 0}

    def evac_copy(out_ap, in_ap, reads, writes, eng=None):
        if eng is None:
            evac_flip['i'] += 1
            eng = 'dve' if evac_flip['i'] % 3 else 'act'
        if eng == 'dve':
            S.op('dve', lambda e: e.tensor_copy(out_ap, in_ap), reads=reads, writes=writes)
        else:
            S.op('act', lambda e: e.activation(out_ap, in_ap, AF.Copy, scale=1.0), reads=reads, writes=writes)

    S.dma('sp', cst[:, :], cst_d, writes=['cst'], sem='cst')
    S.op('dve', lambda e: e.tensor_copy(onesb[:, :], cst[:, C_ONE:C_ONE + 128]), reads=['cst'], writes=['onesb'])
    S.op('dve', lambda e: e.tensor_copy(maskb[:, :], cst[:, C_MASK:C_MASK + 512]), reads=['cst'], writes=['maskb'])
    S.op('dve', lambda e: e.tensor_tensor(sm[:, 0:1], cst[:, C_LAM:C_LAM + 1], cst[:, C_LAM + 1:C_LAM + 2], ALU.mult),
         reads=['cst'], writes=['sm01'])
    S.op('dve', lambda e: e.tensor_tensor(sm[:, 1:2], cst[:, C_LAM + 2:C_LAM + 3], cst[:, C_LAM + 3:C_LAM + 4], ALU.mult),
         reads=['cst'], writes=['sm01'])
    sp3 = S.sb("sp3", [128, 3, 2], BF16)
    S.op('dve', lambda e: e.tensor_copy(sp3[:, 0, :], sm[:, 0:2]), reads=['sm01'], writes=['sp3a'])
    S.op('dve', lambda e: e.tensor_tensor(sm[:, 8:10], sm[:, 0:2], sp3[:, 0, :], ALU.subtract), reads=['sm01', 'sp3a'], writes=['smr1'])
    S.op('dve', lambda e: e.tensor_copy(sp3[:, 1, :], sm[:, 8:10]), reads=['smr1'], writes=['sp3b'])
    S.op('dve', lambda e: e.tensor_tensor(sm[:, 10:12], sm[:, 8:10], sp3[:, 1, :], ALU.subtract), reads=['smr1', 'sp3b'], writes=['smr2'])
    S.op('dve', lambda e: e.tensor_copy(sp3[:, 2, :], sm[:, 10:12]), reads=['smr2'], writes=['sp3c'])
    for q in range(3):
        S.op('pe', lambda e, q=q: e.matmul(PB[7][:, 0:2], lhsT=onesb[:, :], rhs=sp3[:, q, :], start=(q == 0), stop=(q == 2)),
             reads=['onesb', 'sp3a', 'sp3b', 'sp3c'], writes=[('ps', 7)])
    S.op('act', lambda e: e.activation(sm[:, 2:4], PB[7][:, 0:2], AF.Exp, bias=zero_c, scale=1.0),
         reads=[('ps', 7), 'cst'], writes=['sm23'])
    S.op('dve', lambda e: e.tensor_tensor(sm[:, 4:5], sm[:, 2:3], sm[:, 3:4], ALU.subtract), reads=['sm23'], writes=['sm4'])
    S.op('dve', lambda e: e.tensor_scalar(sm[:, 5:6], sm[:, 4:5], -1.0, -LAM_INIT, ALU.mult, ALU.add), reads=['sm4'], writes=['neglam'])
    S.op('dve', lambda e: e.tensor_scalar(sm[:, 6:8], cst[:, C_GSUB:C_GSUB + 2], 1.0 - LAM_INIT, None, ALU.mult), reads=['cst'], writes=['gsubs'])
    neglam = sm[:, 5:6]

    nb = {'i': 0}

    def norm_block(src, src_keys, gc, dst, dst_keys, D=2048.0):
        i = nb['i'] % 2
        nb['i'] += 1
        sq, rs = sqr[i], rsb[i]
        ss = bank('ss', [7, 6])
        S.op('act', lambda e: e.activation(sq[:, :, :], src, AF.Square, bias=zero_c, scale=1.0),
             reads=list(src_keys) + ['cst'], writes=[('sq', i)])
        for c in range(16):
            S.op('pe', lambda e, c=c: e.matmul(PB[ss][:, 0:128], lhsT=onesb[:, :], rhs=sq[:, c, :], start=(c == 0), stop=(c == 15)),
                 reads=[('sq', i), 'onesb'], writes=[('ps', ss)])
        S.op('act', lambda e: e.activation(rs[:, :], PB[ss][:, 0:128], AF.Sqrt, bias=cst[:, C_EPS:C_EPS + 1], scale=1.0 / D),
             reads=[('ps', ss), 'cst'], writes=[('rs', i)])
        S.op('dve', lambda e: e.reciprocal(rs[:, :], rs[:, :]), reads=[('rs', i)], writes=[('rs', i)])
        S.op('dve', lambda e: e.tensor_tensor(src, src, rs[:, :].unsqueeze(1).broadcast_to([128, 16, 128]), ALU.mult),
             reads=list(src_keys) + [('rs', i)], writes=list(src_keys))
        S.op('dve', lambda e: e.tensor_tensor(dst, src, cst[:, gc:gc + 16].unsqueeze(2).broadcast_to([128, 16, 128]), ALU.mult),
             reads=list(src_keys) + ['cst'], writes=list(dst_keys))

    def norm_tc(tc, gc, D=2048.0):
        ss = bank('ss', [7, 6])
        for q in range(4):
            i = nb['i'] % 2
            nb['i'] += 1
            sq = sqr[i][:, :, :].rearrange("p c t -> p (c t)").rearrange("p (a t) -> p a t", a=4)
            S.op('act', lambda e, sq=sq, q=q: e.activation(sq, xT[:, 4 * q:4 * q + 4, tc * 512:(tc + 1) * 512], AF.Square, bias=zero_c, scale=1.0),
                 reads=[('xT', c, tc) for c in range(4 * q, 4 * q + 4)] + ['cst'], writes=[('sq', i)])
            for a in range(4):
                S.op('pe', lambda e, sq=sq, a=a, q=q: e.matmul(PB[ss][:, :], lhsT=onesb[:, :], rhs=sq[:, a, :], start=(q == 0 and a == 0), stop=(q == 3 and a == 3)),
                     reads=[('sq', i), 'onesb'], writes=[('ps', ss)])
        S.op('act', lambda e: e.activation(rs512[:, :], PB[ss][:, :], AF.Sqrt, bias=cst[:, C_EPS:C_EPS + 1], scale=1.0 / D),
             reads=[('ps', ss), 'cst'], writes=['rs512'])
        S.op('dve', lambda e: e.reciprocal(rs512[:, :], rs512[:, :]), reads=['rs512'], writes=['rs512'])
        for c in range(16):
            S.op('dve', lambda e, c=c: e.scalar_tensor_tensor(Y[:, c, tc * 512:(tc + 1) * 512], xT[:, c, tc * 512:(tc + 1) * 512], cst[:, gc + c:gc + c + 1], rs512[:, :],
                                                              ALU.mult, ALU.mult),
                 reads=[('xT', c, tc), 'rs512', 'cst'], writes=[('Y', c, tc)])

    def load_T(src_d, stg, stg_key, dst, dst_keys):
        S.dma('sp', stg, src_d, writes=[stg_key], sem=f"ld_{stg_key[1]}")
        for g in range(4):
            b = bank('tp', [0, 1])
            for j in range(4):
                c = 4 * g + j
                S.op('pe', lambda e, c=c, j=j, b=b: e.transpose(PB[b][:, j * 128:(j + 1) * 128], stg[:, c * 128:(c + 1) * 128], ident),
                     reads=[stg_key, 'cst'], writes=[('ps', b)])
            evac_copy(dst[:, 4 * g:4 * g + 4, :], PB[b][:, :].rearrange("p (a t) -> p a t", a=4), reads=[('ps', b)], writes=list(dst_keys))

    def dump(name, ap, reads):
        if name in dbg_d:
            S.dma('sp', dbg_d[name], ap, reads=reads, sem='dbg_' + name, is_output=True)

    stg = [zf(0, 8), zf(8, 16)]
    xTf = [zf(16, 24).rearrange("p (c t) -> p c t", c=16), zf(24, 32).rearrange("p (c t) -> p c t", c=16)]
    load_T(xall[0], stg[0], ('stg', 0), xTf[0], [('xTf', 0)])
    for blk in range(16):
        i = blk % 2
        if blk + 1 < 16:
            load_T(xall[blk + 1], stg[1 - i], ('stg', 1 - i), xTf[1 - i], [('xTf', 1 - i)])
        norm_block(xTf[i], [('xTf', i)], C_GMIX, hA[:, :, blk * 128:(blk + 1) * 128], [('hA', blk)])
    if 'hA' in dbg_d:
        S.barrier()
        S.op('dve', lambda e: e.tensor_copy(Z[:, 0:4096].rearrange("p (c t) -> p c t", c=2), hA[:, 0:2, :]), reads=[('hA', b) for b in range(16)], writes=['zdbg'])
        dump('hA', Z[:, 0:4096], ['zdbg'])
    S.barrier()
    if stage < 1:
        S.emit()
        return nc, wplan

    B1 = zb(0, 4).rearrange("p (m t) -> p m t", m=2)
    B2 = zb(4, 12)
    B3 = zb(12, 20).rearrange("p (b e) -> p b e", b=16)
    hkeys_tc = lambda tc: [('hA', 4 * tc + q) for q in range(4)]
    SC128 = 128.0 ** -0.5

    def proj_fm(wv_, tcs, evac):
        for tc in tcs:
            b = bank('pj', [0, 1])
            for kc in range(16):
                S.op('pe', lambda e, wv_=wv_, kc=kc, tc=tc, b=b: e.matmul(PB[b][:, :], lhsT=wv_[0][:, kc, :], rhs=hA[:, kc, tc * 512:(tc + 1) * 512],
                                                                start=(kc == 0), stop=(kc == 15)),
                     reads=[wv_[1]] + hkeys_tc(tc), writes=[('ps', b)])
            evac(tc, b)

    def proj_v(s):
        wv_ = w_tm(s)
        for bp in range(8):
            b = bank('pj', [0, 1])
            for i in range(2):
                blk = 2 * bp + i
                for kc in range(16):
                    S.op('pe', lambda e, kc=kc, blk=blk, b=b, i=i: e.matmul(PB[b][:, i * 256:(i + 1) * 256], lhsT=hA[:, kc, blk * 128:(blk + 1) * 128],
                                                                        rhs=wv_[:, kc, :], start=(kc == 0), stop=(kc == 15)),
                         reads=[('w', s), ('hA', blk)], writes=[('ps', b)])
            evac_copy(B3[:, 2 * bp:2 * bp + 2, :], PB[b][:, :].rearrange("p (a t) -> p a t", a=2), reads=[('ps', b)],
                      writes=[('B3', 2 * bp), ('B3', 2 * bp + 1)])

    def run_attention(groups):
        units = [(g, j) for g in groups for j in range(8 + 4 * g['c'] + 4)]
        state = {}

        def qk(g, j):
            c = g['c']
            r = max(0, j - (8 + 4 * c))
            n0 = 128 * r
            N = 512 - n0
            sb_ = bank('st', [2, 3])
            kap, kkey = g['kT_of'](j)
            qap, qkey = g['q_of'](c, n0)
            S.op('pe', lambda e, sb_=sb_, kap=kap, qap=qap, N=N: e.matmul(PB[sb_][:, 0:N], lhsT=kap, rhs=qap, start=True, stop=True),
                 reads=[kkey, qkey], writes=[('ps', sb_)])
            pi = bank('pt', [0, 1, 2, 3])
            pt = ptr[pi]
            g['evac_pt'](c, j, r, n0, N, sb_, pt, pi)
            if j >= 8 + 4 * c:
                S.op('dve', lambda e, pt=pt, n0=n0: e.tensor_tensor(pt[:, n0:n0 + 128], pt[:, n0:n0 + 128], maskb[:, 0:128], ALU.mult),
                     reads=[('pt', pi), 'maskb'], writes=[('pt', pi)])
            state[(id(g), j)] = (pt, pi, n0)

        def pv(g, j):
            pt, pi, n0 = state.pop((id(g), j))
            last = 8 + 4 * g['c'] + 3
            o0, o1, smb = g['banks']
            for e_ in range(2):
                ob = (o0, o1)[e_]
                S.op('pe', lambda e, e_=e_, j=j, pt=pt, n0=n0, ob=ob: e.matmul(PB[ob][:, n0:512], lhsT=B3[:, j, e_ * 128:(e_ + 1) * 128], rhs=pt[:, n0:512],
                                                                           start=(j == 0), stop=(j == last)),
                     reads=[('B3', j), ('pt', pi)], writes=[('ps', ob)])
            if g['nsum']:
                S.op('pe', lambda e, pt=pt, n0=n0, j=j, smb=smb: e.matmul(PB[smb][:, n0:512], lhsT=onesb[:, :], rhs=pt[:, n0:512], start=(j == 0), stop=(j == last)),
                     reads=['onesb', ('pt', pi)], writes=[('ps', smb)])

        deferred = []
        qk(*units[0])
        for i, (g, j) in enumerate(units):
            if i + 1 < len(units):
                qk(*units[i + 1])
            pv(g, j)
            for d in deferred:
                d[0] -= 1
            while deferred and deferred[0][0] <= 0:
                deferred.pop(0)[1]()
            if j == 8 + 4 * g['c'] + 3:
                g['after']()
                if g.get('deferred') is not None:
                    deferred.append([3, g['deferred']])
        while deferred:
            deferred.pop(0)[1]()

    def subln_stats(ab, D):
        for e_ in range(2):
            S.op('dve', lambda e, e_=e_: e.tensor_tensor(sq512[:, e_, :], At[:, ab + e_, :], At[:, ab + e_, :], ALU.mult), reads=[('At', ab + e_)], writes=[('sq512', e_)])
        sb_ = bank('st', [2, 3])
        for e_ in range(2):
            S.op('pe', lambda e, e_=e_, sb_=sb_: e.matmul(PB[sb_][:, :], lhsT=onesb[:, :], rhs=sq512[:, e_, :], start=(e_ == 0), stop=(e_ == 1)),
                 reads=['onesb', ('sq512', e_)], writes=[('ps', sb_)])
        S.op('act', lambda e, sb_=sb_: e.activation(rs512[:, :], PB[sb_][:, :], AF.Sqrt, bias=cst[:, C_EPS:C_EPS + 1], scale=1.0 / D),
             reads=[('ps', sb_), 'cst'], writes=['rs512'])
        S.op('dve', lambda e: e.reciprocal(rs512[:, :], rs512[:, :]), reads=['rs512'], writes=['rs512'])

    PSETS = [(4, 5, 6), (0, 1, 7)]

    for h in range(4):
        sq_, sk_, sv_ = None, None, None
        sq_ = wload(('fm2', ('w_in', h * 256), ('w_in', h * 256 + 128)))
        wq = w_fm(sq_)
        for m in range(2):
            proj_fm((wq[:, m], ('w', sq_)), [2, 3],
                    lambda tc, b, m=m: evac_copy(B1[:, m, (tc - 2) * 512:(tc - 1) * 512], PB[b][:, :], reads=[('ps', b)], writes=[('B1', m * 2 + tc - 2)]))
        sk_ = wload(('fm2', ('w_in', 1024 + h * 256), ('w_in', 1024 + h * 256 + 128)))
        wk = w_fm(sk_)
        for m in range(2):
            proj_fm((wk[:, m], ('w', sk_)), [0, 1, 2, 3],
                    lambda tc, b, m=m: evac_copy(B2[:, m * 2048 + tc * 512:m * 2048 + (tc + 1) * 512], PB[b][:, :], reads=[('ps', b)], writes=[('B2', m * 4 + tc)]))
        sv_ = wload(('tm', 'w_in', 2048 + h * 256))
        proj_v(sv_)
        split = (2.0 ** (-2.0 * (h + 1))) * 511 > 60.0
        groups = []
        for c in range(2):
            for m in range(2):
                def evac_exp(c, j, r, n0, N, sb_, pt, pi, m=m, h=h, split=split):
                    D = 8 + 4 * c - j
                    tab = C_ABC if j < 8 else C_ABO
                    if split:
                        for rr in range(r, 4):
                            col = tab + h * 19 + (D + rr) + 3
                            S.op('act', lambda e, rr=rr, col=col: e.activation(pt[:, rr * 128:(rr + 1) * 128], PB[sb_][:, (rr - r) * 128:(rr - r + 1) * 128],
                                                                               AF.Exp, bias=cst[:, col:col + 1], scale=SC128),
                                 reads=[('ps', sb_), 'cst'], writes=[('pt', pi)])
                    else:
                        col = tab + h * 19 + D + 3
                        S.op('act', lambda e, col=col: e.activation(pt[:, n0:512], PB[sb_][:, 0:N], AF.Exp, bias=cst[:, col:col + 1], scale=SC128),
                             reads=[('ps', sb_), 'cst'], writes=[('pt', pi)])

                pset = PSETS[(c * 2 + m) % 2]
                ab = 2 * (c % 2)

                def after(c=c, m=m, pset=pset, ab=ab):
                    rc = rcp[m]
                    S.op('dve', lambda e: e.reciprocal(rc[:, :], PB[pset[2]][:, :]), reads=[('ps', pset[2])], writes=[('rcp', m)])
                    for e_ in range(2):
                        if m == 0:
                            S.op('dve', lambda e, e_=e_: e.tensor_tensor(At[:, ab + e_, :], PB[pset[e_]][:, :], rc[:, :], ALU.mult),
                                 reads=[('ps', pset[e_]), ('rcp', m)], writes=[('At', ab + e_)])
                        else:
                            S.op('dve', lambda e, e_=e_: e.tensor_tensor(T2[e_][:, :], PB[pset[e_]][:, :], rc[:, :], ALU.mult),
                                 reads=[('ps', pset[e_]), ('rcp', m)], writes=[('T2', e_)])
                            S.op('dve', lambda e, e_=e_: e.scalar_tensor_tensor(At[:, ab + e_, :], T2[e_][:, :], neglam, At[:, ab + e_, :], ALU.mult, ALU.add),
                                 reads=[('T2', e_), ('At', ab + e_), 'neglam'], writes=[('At', ab + e_)])

                def fin(c=c, h=h, ab=ab):
                    subln_stats(ab, 256.0)
                    for e_ in range(2):
                        S.op('dve', lambda e, e_=e_: e.scalar_tensor_tensor(Y[:, h * 2 + e_, c * 512:(c + 1) * 512], At[:, ab + e_, :], sm[:, 6 + e_:7 + e_], rs512[:, :],
                                                                            ALU.mult, ALU.mult),
                             reads=[('At', ab + e_), 'gsubs', 'rs512'], writes=[('Y', h * 2 + e_, c)])

                groups.append(dict(c=c, nsum=True, banks=pset, evac_pt=evac_exp, after=after, deferred=(fin if m == 1 else None),
                                   kT_of=(lambda j, m=m: (B2[:, m * 2048 + j * 128:m * 2048 + (j + 1) * 128], ('B2', m * 4 + j // 4))),
                                   q_of=(lambda c, n0, m=m: (B1[:, m, c * 512 + n0:(c + 1) * 512], ('B1', m * 2 + c)))))
        run_attention(groups)

        sqk = wload(('fm2', ('w_in', 3072 + h * 128), ('w_in', 3584 + h * 128)))
        wqk = w_fm(sqk)
        qd = cst[:, C_QDEC + h * 512:C_QDEC + (h + 1) * 512]
        proj_fm((wqk[:, 0], ('w', sqk)), [2, 3],
                lambda tc, b, qd=qd: S.op('dve', lambda e: e.tensor_tensor(B2[:, 2048 + (tc - 2) * 512:2048 + (tc - 1) * 512], PB[b][:, :], qd, ALU.mult),
                                   reads=[('ps', b), 'cst'], writes=[('B2', 4 + tc - 2)]))
        proj_fm((wqk[:, 1], ('w', sqk)), [0, 1, 2, 3],
                lambda tc, b: evac_copy(B2[:, tc * 512:(tc + 1) * 512], PB[b][:, :], reads=[('ps', b)], writes=[('B2', tc)]))
        sg_ = wload(('fm2', ('w_in', 5120 + h * 256), ('w_in', 5120 + h * 256 + 128)))
        wg = w_fm(sg_)
        for e_ in range(2):
            proj_fm((wg[:, e_], ('w', sg_)), [2, 3],
                    lambda tc, b, e_=e_: S.op('act', lambda e: e.activation(B1[:, e_, (tc - 2) * 512:(tc - 1) * 512], PB[b][:, :], AF.Silu, bias=zero_c, scale=1.0),
                                              reads=[('ps', b), 'cst'], writes=[('B1', e_ * 2 + tc - 2)]))
        srv = wload(('tm', 'w_in', 4096 + h * 256))
        proj_v(srv)
        groups = []
        for c in range(2):
            def evac_ret(c, j, r, n0, N, sb_, pt, pi, h=h):
                D = 8 + 4 * c - j
                col = C_KFAC + h * 16 + D + 3
                S.op('dve', lambda e: e.tensor_scalar(pt[:, n0:512], PB[sb_][:, 0:N], cst[:, col:col + 1], None, ALU.mult),
                     reads=[('ps', sb_), 'cst'], writes=[('pt', pi)])

            pset = PSETS[c % 2]
            ab = 2 * (c % 2)

            def after(pset=pset, ab=ab):
                for e_ in range(2):
                    evac_copy(At[:, ab + e_, :], PB[pset[e_]][:, :], reads=[('ps', pset[e_])], writes=[('At', ab + e_)])

            def fin(c=c, h=h, ab=ab):
                subln_stats(ab, 256.0)
                for e_ in range(2):
                    S.op('dve', lambda e, e_=e_: e.tensor_tensor(T2[e_][:, :], At[:, ab + e_, :], rs512[:, :], ALU.mult),
                         reads=[('At', ab + e_), 'rs512'], writes=[('T2', e_)])
                    S.op('dve', lambda e, e_=e_: e.tensor_tensor(Y[:, 8 + h * 2 + e_, c * 512:(c + 1) * 512], T2[e_][:, :], B1[:, e_, c * 512:(c + 1) * 512], ALU.mult),
                         reads=[('T2', e_), ('B1', e_ * 2 + c)], writes=[('Y', 8 + h * 2 + e_, c)])

            groups.append(dict(c=c, nsum=False, banks=pset, evac_pt=evac_ret, after=after, deferred=fin,
                               kT_of=(lambda j: (B2[:, j * 128:(j + 1) * 128], ('B2', j // 4))),
                               q_of=(lambda c, n0: (B2[:, 2048 + c * 512 + n0:2048 + (c + 1) * 512], ('B2', 4 + c)))))
        run_attention(groups)
    if 'Y' in dbg_d:
        S.barrier()
        S.op('dve', lambda e: e.tensor_copy(Z[:, :].rearrange("p (c t) -> p c t", c=16), Y[:, :, 0:512]), reads=[('Y', c, 0) for c in range(16)], writes=['zdbg'])
        dump('Y', Z[:, :], ['zdbg'])
    S.barrier()
    if stage < 2:
        S.emit()
        return nc, wplan

    xkeys_blk = lambda blk: [('xT', c, blk // 4) for c in range(16)]
    for blk in range(8):
        i = blk % 2
        load_T(xall[8 + blk], stg[i], ('stg', i), xT[:, :, blk * 128:(blk + 1) * 128], xkeys_blk(blk))

    def add_evac(c, tc, b):
        S.op('dve', lambda e: e.tensor_tensor(xT[:, c, tc * 512:(tc + 1) * 512], PB[b][:, :], xT[:, c, tc * 512:(tc + 1) * 512], ALU.add),
             reads=[('ps', b), ('xT', c, tc)], writes=[('xT', c, tc)])

    for sp_ in range(8):
        s = wload(('fm2', ('w_o', sp_ * 256), ('w_o', sp_ * 256 + 128)))
        wv_ = w_fm(s)
        for i in range(2):
            c = 2 * sp_ + i
            for tc in range(2):
                b = bank('pj', [0, 1])
                for kc in range(16):
                    S.op('pe', lambda e, wv_=wv_, kc=kc, tc=tc, b=b, i=i: e.matmul(PB[b][:, :], lhsT=wv_[:, i, kc, :], rhs=Y[:, kc, tc * 512:(tc + 1) * 512],
                                                                        start=(kc == 0), stop=(kc == 15)),
                         reads=[('w', s), ('Y', kc, tc)], writes=[('ps', b)])
                add_evac(c, tc, b)

    def norm_x_to_Y(gc):
        for tc in range(2):
            norm_tc(tc, gc)

    if 'x1' in dbg_d:
        S.barrier()
        dump('x1', X[:, 0:8192], [('xT', c, tc) for c in range(16) for tc in range(2)])
    if stage < 3:
        S.barrier()
        S.emit()
        return nc, wplan

    norm_x_to_Y(C_GX)
    S.barrier()
    hmT = zb(0, 8).rearrange("p (c t) -> p c t", c=16)
    stg_m = zf(8, 16)
    xTf_m = zf(16, 24).rearrange("p (c t) -> p c t", c=16)
    for mb in range(2):
        load_T(memb[mb], stg_m, ('stg', 0), xTf_m, [('xTf', 0)])
        norm_block(xTf_m, [('xTf', 0)], C_GMEM, hmT[:, :, mb * 128:(mb + 1) * 128], [('hm', mb)])
    S.barrier()
    xq = zb(8, 16).rearrange("p (c t) -> p c t", c=4)
    xk = zb(16, 18).rearrange("p (c t) -> p c t", c=4)
    xv = zb(18, 20).rearrange("p (b e) -> p b e", b=2)
    xo = zb(20, 28).rearrange("p (c t) -> p c t", c=4)
    SC512 = 512.0 ** -0.5
    for h in range(4):
        for half in range(2):
            s = wload(('fm2', ('w_xk', h * 512 + half * 256), ('w_xk', h * 512 + half * 256 + 128)))
            wv_ = w_fm(s)
            for i in range(2):
                ch = half * 2 + i
                b = bank('pj', [0, 1])
                for kc in range(16):
                    S.op('pe', lambda e, wv_=wv_, kc=kc, b=b, i=i: e.matmul(PB[b][:, 0:256], lhsT=wv_[:, i, kc, :], rhs=hmT[:, kc, :], start=(kc == 0), stop=(kc == 15)),
                         reads=[('w', s), ('hm', 0), ('hm', 1)], writes=[('ps', b)])
                evac_copy(xk[:, ch, :], PB[b][:, 0:256], reads=[('ps', b)], writes=[('xk', ch)])
        for half in range(2):
            s = wload(('tm', 'w_xv', h * 512 + half * 256))
            wv_ = w_tm(s)
            for mb in range(2):
                b = bank('pj', [0, 1])
                for kc in range(16):
                    S.op('pe', lambda e, wv_=wv_, kc=kc, b=b, mb=mb: e.matmul(PB[b][:, 0:256], lhsT=hmT[:, kc, mb * 128:(mb + 1) * 128], rhs=wv_[:, kc, :], start=(kc == 0), stop=(kc == 15)),
                         reads=[('w', s), ('hm', mb)], writes=[('ps', b)])
                evac_copy(xv[:, mb, half * 256:(half + 1) * 256], PB[b][:, 0:256], reads=[('ps', b)], writes=[('xv', mb)])
        for half in range(2):
            s = wload(('fm2', ('w_xq', h * 512 + half * 256), ('w_xq', h * 512 + half * 256 + 128)))
            wv_ = w_fm(s)
            for i in range(2):
                ch = half * 2 + i
                for tc in range(2):
                    b = bank('pj', [0, 1])
                    for kc in range(16):
                        S.op('pe', lambda e, wv_=wv_, kc=kc, b=b, i=i, tc=tc: e.matmul(PB[b][:, :], lhsT=wv_[:, i, kc, :], rhs=Y[:, kc, tc * 512:(tc + 1) * 512], start=(kc == 0), stop=(kc == 15)),
                             reads=[('w', s), ('Y', kc, tc)], writes=[('ps', b)])
                    evac_copy(xq[:, ch, tc * 512:(tc + 1) * 512], PB[b][:, :], reads=[('ps', b)], writes=[('xq', ch, tc)])
        for tc in range(2):
            for mb in range(2):
                sb_ = bank('st', [2, 3])
                for ch in range(4):
                    S.op('pe', lambda e, ch=ch, sb_=sb_, mb=mb, tc=tc: e.matmul(PB[sb_][:, :], lhsT=xk[:, ch, mb * 128:(mb + 1) * 128], rhs=xq[:, ch, tc * 512:(tc + 1) * 512],
                                                                          start=(ch == 0), stop=(ch == 3)),
                         reads=[('xk', ch), ('xq', ch, tc)], writes=[('ps', sb_)])
                pi = bank('pt', [0, 1, 2, 3])
                pt = ptr[pi]
                S.op('act', lambda e, pt=pt, sb_=sb_: e.activation(pt[:, :], PB[sb_][:, :], AF.Exp, bias=zero_c, scale=SC512),
                     reads=[('ps', sb_), 'cst'], writes=[('pt', pi)])
                for e_ in range(4):
                    S.op('pe', lambda e, e_=e_, pt=pt, mb=mb: e.matmul(PB[4 + e_][:, :], lhsT=xv[:, mb, e_ * 128:(e_ + 1) * 128], rhs=pt[:, :], start=(mb == 0), stop=(mb == 1)),
                         reads=[('xv', mb), ('pt', pi)], writes=[('ps', 4 + e_)])
                S.op('pe', lambda e, pt=pt, mb=mb: e.matmul(PB[1][:, :], lhsT=onesb[:, :], rhs=pt[:, :], start=(mb == 0), stop=(mb == 1)),
                     reads=['onesb', ('pt', pi)], writes=[('ps', 1)])
            rc = rcp[tc]
            S.op('dve', lambda e, rc=rc: e.reciprocal(rc[:, :], PB[1][:, :]), reads=[('ps', 1)], writes=[('rcp', tc)])
            for e_ in range(4):
                S.op('dve', lambda e, e_=e_, rc=rc, tc=tc: e.tensor_tensor(xo[:, e_, tc * 512:(tc + 1) * 512], PB[4 + e_][:, :], rc[:, :], ALU.mult),
                     reads=[('ps', 4 + e_), ('rcp', tc)], writes=[('xo', e_, tc)])
        so = [wload(('rows2', 'w_xo', 4 * h + 2 * q)) for q in range(2)]
        wo = [w_rows(s) for s in so]
        for c in range(16):
            for tc in range(2):
                b = bank('pj', [0, 1])
                for e_ in range(4):
                    S.op('pe', lambda e, wo=wo, e_=e_, c=c, tc=tc, b=b: e.matmul(PB[b][:, :], lhsT=wo[e_ // 2][:, e_ % 2, c * 128:(c + 1) * 128], rhs=xo[:, e_, tc * 512:(tc + 1) * 512],
                                                                        start=(e_ == 0), stop=(e_ == 3)),
                         reads=[('w', so[e_ // 2]), ('xo', e_, tc)], writes=[('ps', b)])
                add_evac(c, tc, b)
    if 'x2' in dbg_d:
        S.barrier()
        dump('x2', X[:, 0:8192], [('xT', c, tc) for c in range(16) for tc in range(2)])
    S.barrier()
    if stage < 4:
        S.emit()
        return nc, wplan

    norm_x_to_Y(C_GFFN)
    act = [zb(0, 8).rearrange("p (f t) -> p f t", f=4), zb(8, 16).rearrange("p (f t) -> p f t", f=4)]
    for g in range(11):
        ab = act[g % 2]
        for fi in range(4):
            f = 4 * g + fi
            s = wload(('fm2', ('w_gate', f * 128), ('w_up', f * 128)))
            wv_ = w_fm(s)
            for tc in range(2):
                bg = bank('gu', [0, 1, 2, 3])
                bu = bank('gu', [0, 1, 2, 3])
                for (i, b) in ((0, bg), (1, bu)):
                    for kc in range(16):
                        S.op('pe', lambda e, wv_=wv_, kc=kc, b=b, i=i, tc=tc: e.matmul(PB[b][:, :], lhsT=wv_[:, i, kc, :], rhs=Y[:, kc, tc * 512:(tc + 1) * 512], start=(kc == 0), stop=(kc == 15)),
                             reads=[('w', s), ('Y', kc, tc)], writes=[('ps', b)])
                ti = bank('t2', [0, 1])
                S.op('act', lambda e, ti=ti, bg=bg: e.activation(T2[ti][:, :], PB[bg][:, :], AF.Silu, bias=zero_c, scale=1.0),
                     reads=[('ps', bg), 'cst'], writes=[('T2', ti)])
                S.op('dve', lambda e, ti=ti, bu=bu, fi=fi, tc=tc, ab=ab: e.tensor_tensor(ab[:, fi, tc * 512:(tc + 1) * 512], PB[bu][:, :], T2[ti][:, :], ALU.mult),
                     reads=[('ps', bu), ('T2', ti)], writes=[('act', g % 2, fi, tc)])
        sd = [wload(('rows2', 'w_down', 4 * g + 2 * q)) for q in range(2)]
        wd = [w_rows(s) for s in sd]
        for c in range(16):
            for tc in range(2):
                b = bank('dn', [4, 5, 6, 7])
                for fi in range(4):
                    S.op('pe', lambda e, wd=wd, fi=fi, c=c, tc=tc, b=b, ab=ab: e.matmul(PB[b][:, :], lhsT=wd[fi // 2][:, fi % 2, c * 128:(c + 1) * 128], rhs=ab[:, fi, tc * 512:(tc + 1) * 512],
                                                                              start=(fi == 0), stop=(fi == 3)),
                         reads=[('w', sd[fi // 2]), ('act', g % 2, fi, tc)], writes=[('ps', b)])
                add_evac(c, tc, b)
    S.barrier()

    ostg = [zf(0, 8), zf(8, 16)]
    gfT = zf(16, 24)
    S.dma('sp', gfT, gfb_d, writes=['gfT'], sem='gfb')
    for blk in range(8):
        i = blk % 2
        for g in range(4):
            b = bank('tp', [0, 1])
            for j in range(4):
                c = 4 * g + j
                S.op('pe', lambda e, c=c, j=j, b=b, blk=blk: e.transpose(PB[b][:, j * 128:(j + 1) * 128], xT[:, c, blk * 128:(blk + 1) * 128], ident),
                     reads=[('xT', c, blk // 4), 'cst'], writes=[('ps', b)])
            evac_copy(ostg[i][:, g * 512:(g + 1) * 512], PB[b][:, :], reads=[('ps', b)], writes=[('ostg', i)])
        junk = sqr[i][:, :, :].rearrange("p c t -> p (c t)")
        S.op('act', lambda e, i=i, junk=junk: e.activation(junk, ostg[i], AF.Square, bias=zero_c, scale=1.0, accum_out=sm[:, 12 + i:13 + i]),
             reads=[('ostg', i), 'cst'], writes=[('sq', i), ('ssq', i)])
        S.op('act', lambda e, i=i: e.activation(sm[:, 12 + i:13 + i], sm[:, 12 + i:13 + i], AF.Sqrt, bias=cst[:, C_EPS:C_EPS + 1], scale=1.0 / 2048.0),
             reads=[('ssq', i), 'cst'], writes=[('ssq', i)])
        S.op('dve', lambda e, i=i: e.reciprocal(sm[:, 12 + i:13 + i], sm[:, 12 + i:13 + i]), reads=[('ssq', i)], writes=[('ssq', i)])
        S.op('dve', lambda e, i=i: e.scalar_tensor_tensor(ostg[i], ostg[i], sm[:, 12 + i:13 + i], gfT, ALU.mult, ALU.mult),
             reads=[('ostg', i), ('ssq', i), 'gfT'], writes=[('ostg', i)])
        S.dma('sp', out_d[blk], ostg[i], reads=[('ostg', i)], sem=f"st{i}", is_output=True)
    S.emit()
    assert wstate['n'] == NSLOT, wstate['n']
    return nc, wplan


def _pack_weights(wplan, W):
    arr = np.zeros((NSLOT, 128, 4096), np.float32)
    for n, d in enumerate(wplan):
        if d[0] == 'fm2':
            v = arr[n].reshape(128, 2, 16, 128)
            for i in range(2):
                name, c0 = d[1 + i]
                v[:, i] = W[name][:, c0:c0 + 128].reshape(16, 128, 128).transpose(1, 0, 2)
        elif d[0] == 'tm':
            _, name, c0 = d
            arr[n].reshape(128, 16, 256)[:] = W[name][:, c0:c0 + 256].reshape(16, 128, 256).transpose(1, 0, 2)
        elif d[0] == 'rows2':
            _, name, rc = d
            arr[n].reshape(128, 2, 2048)[:] = W[name][rc * 128:(rc + 2) * 128, :].reshape(2, 128, 2048).transpose(1, 0, 2)
        else:
            raise ValueError(d)
    return arr


def _col16(g):
    return np.ascontiguousarray(np.asarray(g, np.float32).reshape(-1, 128).T)


def _consts(inp, s):
    c = np.zeros((128, NCST), np.float32)
    p = np.arange(128, dtype=np.float64)
    c[:, C_ID:C_ID + 128] = np.eye(128, dtype=np.float32)
    c[:, C_ONE:C_ONE + 128] = 1.0
    c[:, C_GMIX:C_GMIX + 16] = _col16(inp['norm_mix_g'][0])
    c[:, C_GX:C_GX + 16] = _col16(inp['norm_x_g'][0])
    c[:, C_GMEM:C_GMEM + 16] = _col16(inp['norm_mem_g'][0])
    c[:, C_GFFN:C_GFFN + 16] = _col16(inp['norm_ffn_g'][0])
    c[:, C_GF:C_GF + 16] = _col16(inp['norm_f_g'])
    c[:, C_GSUB:C_GSUB + 2] = _col16(inp['da_subln_g'][0])
    for i, k in enumerate(('lambda_q1', 'lambda_k1', 'lambda_q2', 'lambda_k2')):
        c[:, C_LAM + i] = np.asarray(inp[k][0], np.float32)
    c[:, C_EPS] = EPS
    for h in range(4):
        slope = 2.0 ** (-8.0 * (h + 1) / 4)
        lg = math.log(1.0 - 2.0 ** (-5.0 - h))
        for idx in range(19):
            dd = idx - 3
            v = slope * (p - 128.0 * dd)
            c[:, C_ABO + h * 19 + idx] = v
            c[:, C_ABC + h * 19 + idx] = v + (0.0 if s == 1 else -30000.0)
        for idx in range(16):
            dd = idx - 3
            c[:, C_KFAC + h * 16 + idx] = np.exp(lg * (128.0 * dd - p))
        c[:, C_QDEC + h * 512:C_QDEC + (h + 1) * 512] = (np.exp(lg * np.arange(512, dtype=np.float64)) * 128.0 ** -0.5)[None, :]
    m = np.ones((128, 512), np.float32)
    m[:, 0:128] = (np.arange(128)[None, :] >= np.arange(128)[:, None]).astype(np.float32)
    c[:, C_MASK:C_MASK + 512] = m
    return c


_CACHE = {}


def _get_program():
    if 'nc' not in _CACHE:
        _CACHE['nc'], _CACHE['wplan'] = build()
    return _CACHE['nc'], _CACHE['wplan']


def make_in_maps(inp, wplan, cores=range(8)):
    W = {k: np.asarray(inp[k][0], np.float32) for k in ('w_in', 'w_o', 'w_xq', 'w_xk', 'w_xv', 'w_xo', 'w_gate', 'w_up', 'w_down')}
    wst = _pack_weights(wplan, W)
    x = np.asarray(inp['x'], np.float32)
    mem = np.asarray(inp['mem'], np.float32)
    gfb = np.ascontiguousarray(np.broadcast_to(np.asarray(inp['norm_f_g'], np.float32)[None, :], (128, 2048)))
    maps = []
    for core in cores:
        b, s = core // 2, core % 2
        xa = np.zeros((16, 128, 2048), np.float32)
        if s == 1:
            xa[:] = x[b].reshape(16, 128, 2048)
        else:
            xa[8:] = x[b, :1024].reshape(8, 128, 2048)
        maps.append({"gfb": gfb, "xall": xa, "memb": np.ascontiguousarray(mem[b].reshape(2, 128, 2048)), "cst": _consts(inp, s), "wst": wst})
    return maps


def kernel(**inputs):
    nc, wplan = _get_program()
    in_maps = make_in_maps(inputs, wplan)
    res = run_bass_kernel_spmd(nc, in_maps, core_ids=list(range(8)))
    out = np.empty((4, 2048, 2048), np.float32)
    for core in range(8):
        b, s = core // 2, core % 2
        out[b, s * 1024:(s + 1) * 1024] = np.asarray(res.results[core]["out"], np.float32).reshape(1024, 2048)
    return out
```

```python
import math
from contextlib import ExitStack
import numpy as np
import concourse.bass as bass
import concourse.mybir as mybir
from concourse.bass_utils import run_bass_kernel_spmd

F32 = mybir.dt.float32
BF16 = mybir.dt.bfloat16
AF = mybir.ActivationFunctionType
ALU = mybir.AluOpType
ENGS = ('pe', 'act', 'dve', 'pool', 'sp')

EPS = 1e-6
NSLOT = 130
RING = 4
LAM_INIT = 0.8 - 0.6 * math.exp(0.0)

_c = 0
def _col(n):
    global _c
    s = _c
    _c += n
    return s
C_ID = _col(128)
C_ONE = _col(128)
C_GMIX = _col(16)
C_GX = _col(16)
C_GMEM = _col(16)
C_GFFN = _col(16)
C_GF = _col(16)
C_GSUB = _col(2)
C_LAM = _col(4)
C_ZERO = _col(1)
C_EPS = _col(1)
C_ABO = _col(4 * 19)
C_ABC = _col(4 * 19)
C_KFAC = _col(4 * 16)
C_QDEC = _col(4 * 512)
C_MASK = _col(512)
NCST = _c


class Op:
    __slots__ = ('eng', 'fn', 'deps', 'dma_key', 'dma_val', 'signal', 'sig_ord', 'is_output')

    def __init__(self, eng, fn):
        self.eng = eng
        self.fn = fn
        self.deps = []
        self.dma_key = None
        self.dma_val = 0
        self.signal = False
        self.sig_ord = 0
        self.is_output = False


class Sched:
    def __init__(self, nc):
        self.nc = nc
        self.ops = {e: [] for e in ENGS}
        self.bufs = {}
        self.dma_cnt = {}
        self.last_dma = {}
        self.pending = {e: [] for e in ENGS}
        self.stack = ExitStack()

    def sb(self, name, shape, dtype):
        return self.stack.enter_context(self.nc.sbuf_tensor(name, shape, dtype))

    def ps(self, name, shape, dtype):
        return self.stack.enter_context(self.nc.psum_tensor(name, shape, dtype))

    def _track(self, op, reads, writes):
        deps = list(self.pending[op.eng])
        self.pending[op.eng] = []
        for r in reads:
            b = self.bufs.get(r)
            if b is not None and b['w'] is not None:
                deps.append(b['w'])
        for w in writes:
            b = self.bufs.get(w)
            if b is not None:
                if b['w'] is not None:
                    deps.append(b['w'])
                deps.extend(b['r'].values())
        rk = op.eng if op.dma_key is None else ('dma', op.dma_key)
        for r in reads:
            b = self.bufs.setdefault(r, {'w': None, 'r': {}})
            b['r'][rk] = op
        for w in writes:
            b = self.bufs.setdefault(w, {'w': None, 'r': {}})
            b['w'] = op
            b['r'] = {}
        seen = set()
        for d in deps:
            if d is op or id(d) in seen:
                continue
            seen.add(id(d))
            if d.dma_key is None:
                if d.eng == 'pe' and op.eng == 'pe' and op.dma_key is None:
                    continue
                d.signal = True
            op.deps.append(d)

    def op(self, eng, fn, reads=(), writes=()):
        o = Op(eng, fn)
        self._track(o, reads, writes)
        self.ops[eng].append(o)
        return o

    def dma(self, eng, out, in_, reads=(), writes=(), sem=None, is_output=False, **kw):
        o = Op(eng, lambda e: e.dma_start(out=out, in_=in_, **kw))
        o.dma_key = sem
        self.dma_cnt[sem] = self.dma_cnt.get(sem, 0) + 16
        o.dma_val = self.dma_cnt[sem]
        o.is_output = is_output
        self._track(o, reads, writes)
        self.ops[eng].append(o)
        self.last_dma[sem] = o
        return o

    def barrier(self):
        deps = []
        for e in ENGS:
            for o in reversed(self.ops[e]):
                if o.dma_key is None:
                    deps.append(o)
                    break
        deps.extend(self.last_dma.values())
        for e in ENGS:
            self.pending[e] = list(deps)

    def emit(self):
        nc = self.nc
        for e in ENGS:
            k = 0
            for o in self.ops[e]:
                if o.dma_key is None and o.signal:
                    k += 1
                    o.sig_ord = k
        st = self.stack
        esem = {e: st.enter_context(nc.semaphore(f"s_{e}")) for e in ('pe', 'act', 'dve', 'pool')}
        dsem = {k: st.enter_context(nc.semaphore(f"d_{k}")) for k in self.dma_cnt}
        out_keys = set()
        for e in ENGS:
            for o in self.ops[e]:
                if o.is_output:
                    out_keys.add(o.dma_key)

        def run(eng_name, eng):
            waited = {}
            for o in self.ops[eng_name]:
                for d in o.deps:
                    if d.dma_key is not None:
                        key, sem, val = ('d', d.dma_key), dsem[d.dma_key], d.dma_val
                    else:
                        key, sem, val = ('e', d.eng), esem[d.eng], d.sig_ord
                    if waited.get(key, 0) < val:
                        eng.wait_ge(sem, val)
                        waited[key] = val
                inst = o.fn(eng)
                if o.dma_key is not None:
                    inst.then_inc(dsem[o.dma_key], 16)
                elif o.signal:
                    inst.then_inc(esem[eng_name], 1)
            if eng_name == 'sp':
                for k in sorted(out_keys):
                    eng.wait_ge(dsem[k], self.dma_cnt[k])

        with nc.Block() as block:
            @block.sync
            def _(e):
                run('sp', e)

            @block.tensor
            def _(e):
                run('pe', e)

            @block.scalar
            def _(e):
                run('act', e)

            @block.vector
            def _(e):
                run('dve', e)

            @block.gpsimd
            def _(e):
                run('pool', e)
        st.close()


def build(stage=99, dbg=None):
    nc = bass.Bass("TRN2", target_bir_lowering=False)
    xall = nc.dram_tensor("xall", [16, 128, 2048], F32, kind="ExternalInput").ap()
    memb = nc.dram_tensor("memb", [2, 128, 2048], F32, kind="ExternalInput").ap()
    cst_d = nc.dram_tensor("cst", [128, NCST], F32, kind="ExternalInput").ap()
    wst = nc.dram_tensor("wst", [NSLOT, 128, 4096], F32, kind="ExternalInput").ap()
    gfb_d = nc.dram_tensor("gfb", [128, 2048], F32, kind="ExternalInput").ap()
    gmixb_d = nc.dram_tensor("gmixb", [128, 2048], F32, kind="ExternalInput").ap()
    gmemb_d = nc.dram_tensor("gmemb", [128, 2048], F32, kind="ExternalInput").ap()
    out_d = nc.dram_tensor("out", [8, 128, 2048], F32, kind="ExternalOutput").ap()
    dbg_d = {}
    if dbg:
        for name, shp in dbg.items():
            dbg_d[name] = nc.dram_tensor("dbg_" + name, list(shp), F32, kind="ExternalOutput").ap()

    S = Sched(nc)
    cst = S.sb("cst_sb", [128, NCST], F32)
    onesb = S.sb("onesb", [128, 128], BF16)
    maskb = S.sb("maskb", [128, 512], BF16)
    identb = S.sb("identb", [128, 128], BF16)
    ring = [S.sb(f"ring{i}", [128, 4096], BF16) for i in range(RING)]
    X = S.sb("X", [128, 16384], F32)
    Y = S.sb("Y", [128, 16, 1024], BF16)
    Z = S.sb("Z", [128, 8192], F32)
    sqr = [S.sb(f"sq{i}", [128, 16, 128], BF16) for i in range(2)]
    rsb = [S.sb(f"rs{i}", [128, 128], F32) for i in range(2)]
    ptr = [S.sb(f"pt{i}", [128, 512], BF16) for i in range(4)]
    rcp = [S.sb(f"rcp{i}", [128, 512], F32) for i in range(2)]
    At = S.sb("At", [128, 4, 512], F32)
    T2 = [S.sb(f"t2{i}", [128, 512], F32) for i in range(2)]
    rs512 = S.sb("rs512", [128, 512], F32)
    sq512 = S.sb("sq512", [128, 2, 512], BF16)
    sm = S.sb("sm", [128, 16], F32)
    PB = [S.ps(f"pb{i}", [128, 512], F32) for i in range(8)]

    def zf(a, b):
        return Z[:, a * 256:b * 256]

    def zb(a, b):
        return Z[:, a * 256:b * 256].bitcast(BF16)

    ident = cst[:, C_ID:C_ID + 128]
    onesf = cst[:, C_ONE:C_ONE + 128]
    zero_c = cst[:, C_ZERO:C_ZERO + 1]

    hA = X[:, :].bitcast(BF16).rearrange("p (c t) -> p c t", c=16)
    xT = X[:, :].rearrange("p (c t) -> p c t", c=16)

    wplan = []
    wstate = {'n': 0}

    def wload(desc):
        n = wstate['n']
        wstate['n'] += 1
        wplan.append(desc)
        s = n % RING
        S.dma('pool', ring[s][:, :], wst[n], writes=[('w', s)], sem=f"w{s}", max_dma_last_dim=2048)
        return s

    def w_fm(s):
        return ring[s][:, :].rearrange("p (i k n) -> p i k n", i=2, k=16)

    def w_tm(s):
        return ring[s][:, :].rearrange("p (k n) -> p k n", k=16)

    def w_rows(s):
        return ring[s][:, :].rearrange("p (i n) -> p i n", i=2)

    rot = {}

    def bank(group, banks):
        i = rot.get(group, 0)
        rot[group] = i + 1
        return banks[i % len(banks)]

    evac_flip = {'i': 0}

    def evac_copy(out_ap, in_ap, reads, writes, eng=None):
        if eng is None:
            evac_flip['i'] += 1
            eng = 'dve' if evac_flip['i'] % 3 else 'act'
        if eng == 'dve':
            S.op('dve', lambda e: e.tensor_copy(out_ap, in_ap), reads=reads, writes=writes)
        else:
            S.op('act', lambda e: e.activation(out_ap, in_ap, AF.Copy, scale=1.0), reads=reads, writes=writes)

    S.dma('sp', cst[:, :], cst_d, writes=['cst'], sem='cst')
    S.op('dve', lambda e: e.tensor_copy(onesb[:, :], cst[:, C_ONE:C_ONE + 128]), reads=['cst'], writes=['onesb'])
    S.op('dve', lambda e: e.tensor_copy(maskb[:, :], cst[:, C_MASK:C_MASK + 512]), reads=['cst'], writes=['maskb'])
    S.op('dve', lambda e: e.tensor_copy(identb[:, :], cst[:, C_ID:C_ID + 128]), reads=['cst'], writes=['identb'])
    S.op('dve', lambda e: e.tensor_tensor(sm[:, 0:1], cst[:, C_LAM:C_LAM + 1], cst[:, C_LAM + 1:C_LAM + 2], ALU.mult),
         reads=['cst'], writes=['sm01'])
    S.op('dve', lambda e: e.tensor_tensor(sm[:, 1:2], cst[:, C_LAM + 2:C_LAM + 3], cst[:, C_LAM + 3:C_LAM + 4], ALU.mult),
         reads=['cst'], writes=['sm01'])
    sp3 = S.sb("sp3", [128, 3, 2], BF16)
    S.op('dve', lambda e: e.tensor_copy(sp3[:, 0, :], sm[:, 0:2]), reads=['sm01'], writes=['sp3a'])
    S.op('dve', lambda e: e.tensor_tensor(sm[:, 8:10], sm[:, 0:2], sp3[:, 0, :], ALU.subtract), reads=['sm01', 'sp3a'], writes=['smr1'])
    S.op('dve', lambda e: e.tensor_copy(sp3[:, 1, :], sm[:, 8:10]), reads=['smr1'], writes=['sp3b'])
    S.op('dve', lambda e: e.tensor_tensor(sm[:, 10:12], sm[:, 8:10], sp3[:, 1, :], ALU.subtract), reads=['smr1', 'sp3b'], writes=['smr2'])
    S.op('dve', lambda e: e.tensor_copy(sp3[:, 2, :], sm[:, 10:12]), reads=['smr2'], writes=['sp3c'])
    for q in range(3):
        S.op('pe', lambda e, q=q: e.matmul(PB[7][:, 0:2], lhsT=onesb[:, :], rhs=sp3[:, q, :], start=(q == 0), stop=(q == 2)),
             reads=['onesb', 'sp3a', 'sp3b', 'sp3c'], writes=[('ps', 7)])
    S.op('act', lambda e: e.activation(sm[:, 2:4], PB[7][:, 0:2], AF.Exp, bias=zero_c, scale=1.0),
         reads=[('ps', 7), 'cst'], writes=['sm23'])
    S.op('dve', lambda e: e.tensor_tensor(sm[:, 4:5], sm[:, 2:3], sm[:, 3:4], ALU.subtract), reads=['sm23'], writes=['sm4'])
    S.op('dve', lambda e: e.tensor_scalar(sm[:, 5:6], sm[:, 4:5], -1.0, -LAM_INIT, ALU.mult, ALU.add), reads=['sm4'], writes=['neglam'])
    S.op('dve', lambda e: e.tensor_scalar(sm[:, 6:8], cst[:, C_GSUB:C_GSUB + 2], 1.0 - LAM_INIT, None, ALU.mult), reads=['cst'], writes=['gsubs'])
    neglam = sm[:, 5:6]

    nb = {'i': 0}

    def norm_block(src, src_keys, gc, dst, dst_keys, D=2048.0):
        i = nb['i'] % 2
        nb['i'] += 1
        sq, rs = sqr[i], rsb[i]
        ss = bank('ss', [7, 6])
        S.op('act', lambda e: e.activation(sq[:, :, :], src, AF.Square, bias=zero_c, scale=1.0),
             reads=list(src_keys) + ['cst'], writes=[('sq', i)])
        for c in range(16):
            S.op('pe', lambda e, c=c: e.matmul(PB[ss][:, 0:128], lhsT=onesb[:, :], rhs=sq[:, c, :], start=(c == 0), stop=(c == 15)),
                 reads=[('sq', i), 'onesb'], writes=[('ps', ss)])
        S.op('act', lambda e: e.activation(rs[:, :], PB[ss][:, 0:128], AF.Sqrt, bias=cst[:, C_EPS:C_EPS + 1], scale=1.0 / D),
             reads=[('ps', ss), 'cst'], writes=[('rs', i)])
        S.op('dve', lambda e: e.reciprocal(rs[:, :], rs[:, :]), reads=[('rs', i)], writes=[('rs', i)])
        S.op('dve', lambda e: e.tensor_tensor(src, src, rs[:, :].unsqueeze(1).broadcast_to([128, 16, 128]), ALU.mult),
             reads=list(src_keys) + [('rs', i)], writes=list(src_keys))
        S.op('dve', lambda e: e.tensor_tensor(dst, src, cst[:, gc:gc + 16].unsqueeze(2).broadcast_to([128, 16, 128]), ALU.mult),
             reads=list(src_keys) + ['cst'], writes=list(dst_keys))

    def norm_tc(tc, gc, D=2048.0):
        ss = bank('ss', [7, 6])
        for q in range(4):
            i = nb['i'] % 2
            nb['i'] += 1
            sq = sqr[i][:, :, :].rearrange("p c t -> p (c t)").rearrange("p (a t) -> p a t", a=4)
            S.op('act', lambda e, sq=sq, q=q: e.activation(sq, xT[:, 4 * q:4 * q + 4, tc * 512:(tc + 1) * 512], AF.Square, bias=zero_c, scale=1.0),
                 reads=[('xT', c, tc) for c in range(4 * q, 4 * q + 4)] + ['cst'], writes=[('sq', i)])
            for a in range(4):
                S.op('pe', lambda e, sq=sq, a=a, q=q: e.matmul(PB[ss][:, :], lhsT=onesb[:, :], rhs=sq[:, a, :], start=(q == 0 and a == 0), stop=(q == 3 and a == 3)),
                     reads=[('sq', i), 'onesb'], writes=[('ps', ss)])
        S.op('act', lambda e: e.activation(rs512[:, :], PB[ss][:, :], AF.Sqrt, bias=cst[:, C_EPS:C_EPS + 1], scale=1.0 / D),
             reads=[('ps', ss), 'cst'], writes=['rs512'])
        S.op('dve', lambda e: e.reciprocal(rs512[:, :], rs512[:, :]), reads=['rs512'], writes=['rs512'])
        for c in range(16):
            S.op('dve', lambda e, c=c: e.scalar_tensor_tensor(Y[:, c, tc * 512:(tc + 1) * 512], xT[:, c, tc * 512:(tc + 1) * 512], cst[:, gc + c:gc + c + 1], rs512[:, :],
                                                              ALU.mult, ALU.mult),
                 reads=[('xT', c, tc), 'rs512', 'cst'], writes=[('Y', c, tc)])

    def tm_norm_T(src_d, stg, stg_key, k, gT, gT_key, dst, dst_keys, ldsem):
        sgb = sqr[k][:, :, :].rearrange("p c t -> p (c t)")
        col = sm[:, 12 + k:13 + k]
        S.dma('sp', stg, src_d, writes=[stg_key], sem=ldsem)
        S.op('act', lambda e: e.activation(sgb, stg, AF.Square, bias=zero_c, scale=1.0, accum_out=col),
             reads=[stg_key, 'cst'], writes=[('sq', k), ('ssq', k)])
        S.op('act', lambda e: e.activation(col, col, AF.Sqrt, bias=cst[:, C_EPS:C_EPS + 1], scale=1.0 / 2048.0),
             reads=[('ssq', k), 'cst'], writes=[('ssq', k)])
        S.op('dve', lambda e: e.reciprocal(col, col), reads=[('ssq', k)], writes=[('ssq', k)])
        S.op('dve', lambda e: e.scalar_tensor_tensor(sgb, stg, col, gT, ALU.mult, ALU.mult),
             reads=[stg_key, ('ssq', k), gT_key], writes=[('sq', k)])
        for g in range(4):
            b = bank('tp', [0, 1])
            pbv = PB[b][:, 0:256].bitcast(BF16)
            for j in range(4):
                c = 4 * g + j
                S.op('pe', lambda e, c=c, j=j, pbv=pbv: e.transpose(pbv[:, j * 128:(j + 1) * 128], sgb[:, c * 128:(c + 1) * 128], identb[:, :]),
                     reads=[('sq', k), 'identb'], writes=[('ps', b)])
            evac_copy(dst[:, 4 * g:4 * g + 4, :], pbv.rearrange("p (a t) -> p a t", a=4), reads=[('ps', b)], writes=list(dst_keys))

    def load_T(src_d, stg, stg_key, dst, dst_keys):
        S.dma('sp', stg, src_d, writes=[stg_key], sem=f"ld_{stg_key[1]}")
        for g in range(4):
            b = bank('tp', [0, 1])
            for j in range(4):
                c = 4 * g + j
                S.op('pe', lambda e, c=c, j=j, b=b: e.transpose(PB[b][:, j * 128:(j + 1) * 128], stg[:, c * 128:(c + 1) * 128], ident),
                     reads=[stg_key, 'cst'], writes=[('ps', b)])
            evac_copy(dst[:, 4 * g:4 * g + 4, :], PB[b][:, :].rearrange("p (a t) -> p a t", a=4), reads=[('ps', b)], writes=list(dst_keys))

    def dump(name, ap, reads):
        if name in dbg_d:
            S.dma('sp', dbg_d[name], ap, reads=reads, sem='dbg_' + name, is_output=True)

    stg = [zf(0, 8), zf(8, 16)]
    gmixT = zf(16, 24)
    S.dma('sp', gmixT, gmixb_d, writes=['gmixT'], sem='gmixb')
    for blk in range(16):
        i = blk % 2
        tm_norm_T(xall[blk], stg[i], ('stg', i), i, gmixT, 'gmixT', hA[:, :, blk * 128:(blk + 1) * 128], [('hA', blk)], f"ld_{i}")
    if 'hA' in dbg_d:
        S.barrier()
        S.op('dve', lambda e: e.tensor_copy(Z[:, 0:4096].rearrange("p (c t) -> p c t", c=2), hA[:, 0:2, :]), reads=[('hA', b) for b in range(16)], writes=['zdbg'])
        dump('hA', Z[:, 0:4096], ['zdbg'])
    S.barrier()
    if stage < 1:
        S.emit()
        return nc, wplan

    B1 = zb(0, 4).rearrange("p (m t) -> p m t", m=2)
    B2 = zb(4, 12)
    B3 = zb(12, 20).rearrange("p (b e) -> p b e", b=16)
    hkeys_tc = lambda tc: [('hA', 4 * tc + q) for q in range(4)]
    SC128 = 128.0 ** -0.5

    def proj_fm(wv_, tcs, evac):
        for tc in tcs:
            b = bank('pj', [0, 1])
            for kc in range(16):
                S.op('pe', lambda e, wv_=wv_, kc=kc, tc=tc, b=b: e.matmul(PB[b][:, :], lhsT=wv_[0][:, kc, :], rhs=hA[:, kc, tc * 512:(tc + 1) * 512],
                                                                start=(kc == 0), stop=(kc == 15)),
                     reads=[wv_[1]] + hkeys_tc(tc), writes=[('ps', b)])
            evac(tc, b)

    def proj_v(s):
        wv_ = w_tm(s)
        for bp in range(8):
            b = bank('pj', [0, 1])
            for i in range(2):
                blk = 2 * bp + i
                for kc in range(16):
                    S.op('pe', lambda e, kc=kc, blk=blk, b=b, i=i: e.matmul(PB[b][:, i * 256:(i + 1) * 256], lhsT=hA[:, kc, blk * 128:(blk + 1) * 128],
                                                                        rhs=wv_[:, kc, :], start=(kc == 0), stop=(kc == 15)),
                         reads=[('w', s), ('hA', blk)], writes=[('ps', b)])
            evac_copy(B3[:, 2 * bp:2 * bp + 2, :], PB[b][:, :].rearrange("p (a t) -> p a t", a=2), reads=[('ps', b)],
                      writes=[('B3', 2 * bp), ('B3', 2 * bp + 1)])

    def run_attention(groups):
        units = [(g, j) for g in groups for j in range(8 + 4 * g['c'] + 4)]
        state = {}

        def qk(g, j):
            c = g['c']
            r = max(0, j - (8 + 4 * c))
            n0 = 128 * r
            N = 512 - n0
            sb_ = bank('st', [2, 3])
            kap, kkey = g['kT_of'](j)
            qap, qkey = g['q_of'](c, n0)
            S.op('pe', lambda e, sb_=sb_, kap=kap, qap=qap, N=N: e.matmul(PB[sb_][:, 0:N], lhsT=kap, rhs=qap, start=True, stop=True),
                 reads=[kkey, qkey], writes=[('ps', sb_)])
            pi = bank('pt', [0, 1, 2, 3])
            pt = ptr[pi]
            g['evac_pt'](c, j, r, n0, N, sb_, pt, pi)
            if j >= 8 + 4 * c:
                S.op('dve', lambda e, pt=pt, n0=n0: e.tensor_tensor(pt[:, n0:n0 + 128], pt[:, n0:n0 + 128], maskb[:, 0:128], ALU.mult),
                     reads=[('pt', pi), 'maskb'], writes=[('pt', pi)])
            state[(id(g), j)] = (pt, pi, n0)

        def pv(g, j):
            pt, pi, n0 = state.pop((id(g), j))
            last = 8 + 4 * g['c'] + 3
            o0, o1, smb = g['banks']
            for e_ in range(2):
                ob = (o0, o1)[e_]
                S.op('pe', lambda e, e_=e_, j=j, pt=pt, n0=n0, ob=ob: e.matmul(PB[ob][:, n0:512], lhsT=B3[:, j, e_ * 128:(e_ + 1) * 128], rhs=pt[:, n0:512],
                                                                           start=(j == 0), stop=(j == last)),
                     reads=[('B3', j), ('pt', pi)], writes=[('ps', ob)])
            if g['nsum']:
                S.op('pe', lambda e, pt=pt, n0=n0, j=j, smb=smb: e.matmul(PB[smb][:, n0:512], lhsT=onesb[:, :], rhs=pt[:, n0:512], start=(j == 0), stop=(j == last)),
                     reads=['onesb', ('pt', pi)], writes=[('ps', smb)])

        deferred = []
        qk(*units[0])
        for i, (g, j) in enumerate(units):
            if i + 1 < len(units):
                qk(*units[i + 1])
            pv(g, j)
            for d in deferred:
                d[0] -= 1
            while deferred and deferred[0][0] <= 0:
                deferred.pop(0)[1]()
            if j == 8 + 4 * g['c'] + 3:
                g['after']()
                if g.get('deferred') is not None:
                    deferred.append([3, g['deferred']])
        while deferred:
            deferred.pop(0)[1]()

    def subln_stats(ab, D):
        for e_ in range(2):
            S.op('dve', lambda e, e_=e_: e.tensor_tensor(sq512[:, e_, :], At[:, ab + e_, :], At[:, ab + e_, :], ALU.mult), reads=[('At', ab + e_)], writes=[('sq512', e_)])
        sb_ = bank('st', [2, 3])
        for e_ in range(2):
            S.op('pe', lambda e, e_=e_, sb_=sb_: e.matmul(PB[sb_][:, :], lhsT=onesb[:, :], rhs=sq512[:, e_, :], start=(e_ == 0), stop=(e_ == 1)),
                 reads=['onesb', ('sq512', e_)], writes=[('ps', sb_)])
        S.op('act', lambda e, sb_=sb_: e.activation(rs512[:, :], PB[sb_][:, :], AF.Sqrt, bias=cst[:, C_EPS:C_EPS + 1], scale=1.0 / D),
             reads=[('ps', sb_), 'cst'], writes=['rs512'])
        S.op('dve', lambda e: e.reciprocal(rs512[:, :], rs512[:, :]), reads=['rs512'], writes=['rs512'])

    PSETS = [(4, 5, 6), (0, 1, 7)]

    for h in range(4):
        sq_, sk_, sv_ = None, None, None
        sq_ = wload(('fm2', ('w_in', h * 256), ('w_in', h * 256 + 128)))
        wq = w_fm(sq_)
        for m in range(2):
            proj_fm((wq[:, m], ('w', sq_)), [2, 3],
                    lambda tc, b, m=m: evac_copy(B1[:, m, (tc - 2) * 512:(tc - 1) * 512], PB[b][:, :], reads=[('ps', b)], writes=[('B1', m * 2 + tc - 2)]))
        sk_ = wload(('fm2', ('w_in', 1024 + h * 256), ('w_in', 1024 + h * 256 + 128)))
        wk = w_fm(sk_)
        for m in range(2):
            proj_fm((wk[:, m], ('w', sk_)), [0, 1, 2, 3],
                    lambda tc, b, m=m: evac_copy(B2[:, m * 2048 + tc * 512:m * 2048 + (tc + 1) * 512], PB[b][:, :], reads=[('ps', b)], writes=[('B2', m * 4 + tc)]))
        sv_ = wload(('tm', 'w_in', 2048 + h * 256))
        proj_v(sv_)
        split = (2.0 ** (-2.0 * (h + 1))) * 511 > 60.0
        groups = []
        for c in range(2):
            for m in range(2):
                def evac_exp(c, j, r, n0, N, sb_, pt, pi, m=m, h=h, split=split):
                    D = 8 + 4 * c - j
                    tab = C_ABC if j < 8 else C_ABO
                    if split:
                        for rr in range(r, 4):
                            col = tab + h * 19 + (D + rr) + 3
                            S.op('act', lambda e, rr=rr, col=col: e.activation(pt[:, rr * 128:(rr + 1) * 128], PB[sb_][:, (rr - r) * 128:(rr - r + 1) * 128],
                                                                               AF.Exp, bias=cst[:, col:col + 1], scale=SC128),
                                 reads=[('ps', sb_), 'cst'], writes=[('pt', pi)])
                    else:
                        col = tab + h * 19 + D + 3
                        S.op('act', lambda e, col=col: e.activation(pt[:, n0:512], PB[sb_][:, 0:N], AF.Exp, bias=cst[:, col:col + 1], scale=SC128),
                             reads=[('ps', sb_), 'cst'], writes=[('pt', pi)])

                pset = PSETS[(c * 2 + m) % 2]
                ab = 2 * (c % 2)

                def after(c=c, m=m, pset=pset, ab=ab):
                    rc = rcp[m]
                    S.op('dve', lambda e: e.reciprocal(rc[:, :], PB[pset[2]][:, :]), reads=[('ps', pset[2])], writes=[('rcp', m)])
                    for e_ in range(2):
                        if m == 0:
                            S.op('dve', lambda e, e_=e_: e.tensor_tensor(At[:, ab + e_, :], PB[pset[e_]][:, :], rc[:, :], ALU.mult),
                                 reads=[('ps', pset[e_]), ('rcp', m)], writes=[('At', ab + e_)])
                        else:
                            S.op('dve', lambda e, e_=e_: e.tensor_tensor(T2[e_][:, :], PB[pset[e_]][:, :], rc[:, :], ALU.mult),
                                 reads=[('ps', pset[e_]), ('rcp', m)], writes=[('T2', e_)])
                            S.op('dve', lambda e, e_=e_: e.scalar_tensor_tensor(At[:, ab + e_, :], T2[e_][:, :], neglam, At[:, ab + e_, :], ALU.mult, ALU.add),
                                 reads=[('T2', e_), ('At', ab + e_), 'neglam'], writes=[('At', ab + e_)])

                def fin(c=c, h=h, ab=ab):
                    subln_stats(ab, 256.0)
                    for e_ in range(2):
                        S.op('dve', lambda e, e_=e_: e.scalar_tensor_tensor(Y[:, h * 2 + e_, c * 512:(c + 1) * 512], At[:, ab + e_, :], sm[:, 6 + e_:7 + e_], rs512[:, :],
                                                                            ALU.mult, ALU.mult),
                             reads=[('At', ab + e_), 'gsubs', 'rs512'], writes=[('Y', h * 2 + e_, c)])

                groups.append(dict(c=c, nsum=True, banks=pset, evac_pt=evac_exp, after=after, deferred=(fin if m == 1 else None),
                                   kT_of=(lambda j, m=m: (B2[:, m * 2048 + j * 128:m * 2048 + (j + 1) * 128], ('B2', m * 4 + j // 4))),
                                   q_of=(lambda c, n0, m=m: (B1[:, m, c * 512 + n0:(c + 1) * 512], ('B1', m * 2 + c)))))
        run_attention(groups)

        sqk = wload(('fm2', ('w_in', 3072 + h * 128), ('w_in', 3584 + h * 128)))
        wqk = w_fm(sqk)
        qd = cst[:, C_QDEC + h * 512:C_QDEC + (h + 1) * 512]
        proj_fm((wqk[:, 0], ('w', sqk)), [2, 3],
                lambda tc, b, qd=qd: S.op('dve', lambda e: e.tensor_tensor(B2[:, 2048 + (tc - 2) * 512:2048 + (tc - 1) * 512], PB[b][:, :], qd, ALU.mult),
                                   reads=[('ps', b), 'cst'], writes=[('B2', 4 + tc - 2)]))
        proj_fm((wqk[:, 1], ('w', sqk)), [0, 1, 2, 3],
                lambda tc, b: evac_copy(B2[:, tc * 512:(tc + 1) * 512], PB[b][:, :], reads=[('ps', b)], writes=[('B2', tc)]))
        sg_ = wload(('fm2', ('w_in', 5120 + h * 256), ('w_in', 5120 + h * 256 + 128)))
        wg = w_fm(sg_)
        for e_ in range(2):
            proj_fm((wg[:, e_], ('w', sg_)), [2, 3],
                    lambda tc, b, e_=e_: S.op('act', lambda e: e.activation(B1[:, e_, (tc - 2) * 512:(tc - 1) * 512], PB[b][:, :], AF.Silu, bias=zero_c, scale=1.0),
                                              reads=[('ps', b), 'cst'], writes=[('B1', e_ * 2 + tc - 2)]))
        srv = wload(('tm', 'w_in', 4096 + h * 256))
        proj_v(srv)
        groups = []
        for c in range(2):
            def evac_ret(c, j, r, n0, N, sb_, pt, pi, h=h):
                D = 8 + 4 * c - j
                col = C_KFAC + h * 16 + D + 3
                S.op('dve', lambda e: e.tensor_scalar(pt[:, n0:512], PB[sb_][:, 0:N], cst[:, col:col + 1], None, ALU.mult),
                     reads=[('ps', sb_), 'cst'], writes=[('pt', pi)])

            pset = PSETS[c % 2]
            ab = 2 * (c % 2)

            def after(pset=pset, ab=ab):
                for e_ in range(2):
                    evac_copy(At[:, ab + e_, :], PB[pset[e_]][:, :], reads=[('ps', pset[e_])], writes=[('At', ab + e_)])

            def fin(c=c, h=h, ab=ab):
                subln_stats(ab, 256.0)
                for e_ in range(2):
                    S.op('dve', lambda e, e_=e_: e.tensor_tensor(T2[e_][:, :], At[:, ab + e_, :], rs512[:, :], ALU.mult),
                         reads=[('At', ab + e_), 'rs512'], writes=[('T2', e_)])
                    S.op('dve', lambda e, e_=e_: e.tensor_tensor(Y[:, 8 + h * 2 + e_, c * 512:(c + 1) * 512], T2[e_][:, :], B1[:, e_, c * 512:(c + 1) * 512], ALU.mult),
                         reads=[('T2', e_), ('B1', e_ * 2 + c)], writes=[('Y', 8 + h * 2 + e_, c)])

            groups.append(dict(c=c, nsum=False, banks=pset, evac_pt=evac_ret, after=after, deferred=fin,
                               kT_of=(lambda j: (B2[:, j * 128:(j + 1) * 128], ('B2', j // 4))),
                               q_of=(lambda c, n0: (B2[:, 2048 + c * 512 + n0:2048 + (c + 1) * 512], ('B2', 4 + c)))))
        run_attention(groups)
    if 'Y' in dbg_d:
        S.barrier()
        S.op('dve', lambda e: e.tensor_copy(Z[:, :].rearrange("p (c t) -> p c t", c=16), Y[:, :, 0:512]), reads=[('Y', c, 0) for c in range(16)], writes=['zdbg'])
        dump('Y', Z[:, :], ['zdbg'])
    S.barrier()
    if stage < 2:
        S.emit()
        return nc, wplan

    xkeys_blk = lambda blk: [('xT', c, blk // 4) for c in range(16)]
    for blk in range(8):
        i = blk % 2
        load_T(xall[8 + blk], stg[i], ('stg', i), xT[:, :, blk * 128:(blk + 1) * 128], xkeys_blk(blk))
    hmT = zb(16, 24).rearrange("p (c t) -> p c t", c=16)
    stg_m = zf(24, 32)
    gmemT = At[:, :, :].rearrange("p c t -> p (c t)")
    S.dma('sp', gmemT, gmemb_d, writes=[('At', q) for q in range(4)], sem='gmemb')
    for mb in range(2):
        tm_norm_T(memb[mb], stg_m, ('stgm', 0), mb, gmemT, ('At', 0), hmT[:, :, mb * 128:(mb + 1) * 128], [('hm', mb)], "ld_m")

    def add_evac(c, tc, b):
        S.op('dve', lambda e: e.tensor_tensor(xT[:, c, tc * 512:(tc + 1) * 512], PB[b][:, :], xT[:, c, tc * 512:(tc + 1) * 512], ALU.add),
             reads=[('ps', b), ('xT', c, tc)], writes=[('xT', c, tc)])

    for sp_ in range(8):
        s = wload(('fm2', ('w_o', sp_ * 256), ('w_o', sp_ * 256 + 128)))
        wv_ = w_fm(s)
        for i in range(2):
            c = 2 * sp_ + i
            for tc in range(2):
                b = bank('pj', [0, 1])
                for kc in range(16):
                    S.op('pe', lambda e, wv_=wv_, kc=kc, tc=tc, b=b, i=i: e.matmul(PB[b][:, :], lhsT=wv_[:, i, kc, :], rhs=Y[:, kc, tc * 512:(tc + 1) * 512],
                                                                        start=(kc == 0), stop=(kc == 15)),
                         reads=[('w', s), ('Y', kc, tc)], writes=[('ps', b)])
                add_evac(c, tc, b)

    def norm_x_to_Y(gc):
        for tc in range(2):
            norm_tc(tc, gc)

    if 'x1' in dbg_d:
        S.barrier()
        dump('x1', X[:, 0:8192], [('xT', c, tc) for c in range(16) for tc in range(2)])
    if stage < 3:
        S.barrier()
        S.emit()
        return nc, wplan

    norm_x_to_Y(C_GX)
    S.barrier()
    xq = zb(0, 8).rearrange("p (c t) -> p c t", c=4)
    xo = zb(8, 16).rearrange("p (c t) -> p c t", c=4)
    xk = zb(24, 26).rearrange("p (c t) -> p c t", c=4)
    xv = zb(26, 28).rearrange("p (b e) -> p b e", b=2)
    SC512 = 512.0 ** -0.5
    for h in range(4):
        for half in range(2):
            s = wload(('fm2', ('w_xk', h * 512 + half * 256), ('w_xk', h * 512 + half * 256 + 128)))
            wv_ = w_fm(s)
            for i in range(2):
                ch = half * 2 + i
                b = bank('pj', [0, 1])
                for kc in range(16):
                    S.op('pe', lambda e, wv_=wv_, kc=kc, b=b, i=i: e.matmul(PB[b][:, 0:256], lhsT=wv_[:, i, kc, :], rhs=hmT[:, kc, :], start=(kc == 0), stop=(kc == 15)),
                         reads=[('w', s), ('hm', 0), ('hm', 1)], writes=[('ps', b)])
                evac_copy(xk[:, ch, :], PB[b][:, 0:256], reads=[('ps', b)], writes=[('xk', ch)])
        for half in range(2):
            s = wload(('tm', 'w_xv', h * 512 + half * 256))
            wv_ = w_tm(s)
            for mb in range(2):
                b = bank('pj', [0, 1])
                for kc in range(16):
                    S.op('pe', lambda e, wv_=wv_, kc=kc, b=b, mb=mb: e.matmul(PB[b][:, 0:256], lhsT=hmT[:, kc, mb * 128:(mb + 1) * 128], rhs=wv_[:, kc, :], start=(kc == 0), stop=(kc == 15)),
                         reads=[('w', s), ('hm', mb)], writes=[('ps', b)])
                evac_copy(xv[:, mb, half * 256:(half + 1) * 256], PB[b][:, 0:256], reads=[('ps', b)], writes=[('xv', mb)])
        for half in range(2):
            s = wload(('fm2', ('w_xq', h * 512 + half * 256), ('w_xq', h * 512 + half * 256 + 128)))
            wv_ = w_fm(s)
            for i in range(2):
                ch = half * 2 + i
                for tc in range(2):
                    b = bank('pj', [0, 1])
                    for kc in range(16):
                        S.op('pe', lambda e, wv_=wv_, kc=kc, b=b, i=i, tc=tc: e.matmul(PB[b][:, :], lhsT=wv_[:, i, kc, :], rhs=Y[:, kc, tc * 512:(tc + 1) * 512], start=(kc == 0), stop=(kc == 15)),
                             reads=[('w', s), ('Y', kc, tc)], writes=[('ps', b)])
                    evac_copy(xq[:, ch, tc * 512:(tc + 1) * 512], PB[b][:, :], reads=[('ps', b)], writes=[('xq', ch, tc)])
        for tc in range(2):
            for mb in range(2):
                sb_ = bank('st', [2, 3])
                for ch in range(4):
                    S.op('pe', lambda e, ch=ch, sb_=sb_, mb=mb, tc=tc: e.matmul(PB[sb_][:, :], lhsT=xk[:, ch, mb * 128:(mb + 1) * 128], rhs=xq[:, ch, tc * 512:(tc + 1) * 512],
                                                                          start=(ch == 0), stop=(ch == 3)),
                         reads=[('xk', ch), ('xq', ch, tc)], writes=[('ps', sb_)])
                pi = bank('pt', [0, 1, 2, 3])
                pt = ptr[pi]
                S.op('act', lambda e, pt=pt, sb_=sb_: e.activation(pt[:, :], PB[sb_][:, :], AF.Exp, bias=zero_c, scale=SC512),
                     reads=[('ps', sb_), 'cst'], writes=[('pt', pi)])
                for e_ in range(4):
                    S.op('pe', lambda e, e_=e_, pt=pt, mb=mb: e.matmul(PB[4 + e_][:, :], lhsT=xv[:, mb, e_ * 128:(e_ + 1) * 128], rhs=pt[:, :], start=(mb == 0), stop=(mb == 1)),
                         reads=[('xv', mb), ('pt', pi)], writes=[('ps', 4 + e_)])
                S.op('pe', lambda e, pt=pt, mb=mb: e.matmul(PB[1][:, :], lhsT=onesb[:, :], rhs=pt[:, :], start=(mb == 0), stop=(mb == 1)),
                     reads=['onesb', ('pt', pi)], writes=[('ps', 1)])
            rc = rcp[tc]
            S.op('dve', lambda e, rc=rc: e.reciprocal(rc[:, :], PB[1][:, :]), reads=[('ps', 1)], writes=[('rcp', tc)])
            for e_ in range(4):
                S.op('dve', lambda e, e_=e_, rc=rc, tc=tc: e.tensor_tensor(xo[:, e_, tc * 512:(tc + 1) * 512], PB[4 + e_][:, :], rc[:, :], ALU.mult),
                     reads=[('ps', 4 + e_), ('rcp', tc)], writes=[('xo', e_, tc)])
        so = [wload(('rows2', 'w_xo', 4 * h + 2 * q)) for q in range(2)]
        wo = [w_rows(s) for s in so]
        for c in range(16):
            for tc in range(2):
                b = bank('pj', [0, 1])
                for e_ in range(4):
                    S.op('pe', lambda e, wo=wo, e_=e_, c=c, tc=tc, b=b: e.matmul(PB[b][:, :], lhsT=wo[e_ // 2][:, e_ % 2, c * 128:(c + 1) * 128], rhs=xo[:, e_, tc * 512:(tc + 1) * 512],
                                                                        start=(e_ == 0), stop=(e_ == 3)),
                         reads=[('w', so[e_ // 2]), ('xo', e_, tc)], writes=[('ps', b)])
                add_evac(c, tc, b)
    if 'x2' in dbg_d:
        S.barrier()
        dump('x2', X[:, 0:8192], [('xT', c, tc) for c in range(16) for tc in range(2)])
    S.barrier()
    if stage < 4:
        S.emit()
        return nc, wplan

    norm_x_to_Y(C_GFFN)
    act = [zb(0, 8).rearrange("p (f t) -> p f t", f=4), zb(8, 16).rearrange("p (f t) -> p f t", f=4)]
    for g in range(11):
        ab = act[g % 2]
        for fi in range(4):
            f = 4 * g + fi
            s = wload(('fm2', ('w_gate', f * 128), ('w_up', f * 128)))
            wv_ = w_fm(s)
            for tc in range(2):
                bg = bank('gu', [0, 1, 2, 3])
                bu = bank('gu', [0, 1, 2, 3])
                for (i, b) in ((0, bg), (1, bu)):
                    for kc in range(16):
                        S.op('pe', lambda e, wv_=wv_, kc=kc, b=b, i=i, tc=tc: e.matmul(PB[b][:, :], lhsT=wv_[:, i, kc, :], rhs=Y[:, kc, tc * 512:(tc + 1) * 512], start=(kc == 0), stop=(kc == 15)),
                             reads=[('w', s), ('Y', kc, tc)], writes=[('ps', b)])
                ti = bank('t2', [0, 1])
                S.op('act', lambda e, ti=ti, bg=bg: e.activation(T2[ti][:, :], PB[bg][:, :], AF.Silu, bias=zero_c, scale=1.0),
                     reads=[('ps', bg), 'cst'], writes=[('T2', ti)])
                S.op('dve', lambda e, ti=ti, bu=bu, fi=fi, tc=tc, ab=ab: e.tensor_tensor(ab[:, fi, tc * 512:(tc + 1) * 512], PB[bu][:, :], T2[ti][:, :], ALU.mult),
                     reads=[('ps', bu), ('T2', ti)], writes=[('act', g % 2, fi, tc)])
        sd = [wload(('rows2', 'w_down', 4 * g + 2 * q)) for q in range(2)]
        wd = [w_rows(s) for s in sd]
        for c in range(16):
            for tc in range(2):
                b = bank('dn', [4, 5, 6, 7])
                for fi in range(4):
                    S.op('pe', lambda e, wd=wd, fi=fi, c=c, tc=tc, b=b, ab=ab: e.matmul(PB[b][:, :], lhsT=wd[fi // 2][:, fi % 2, c * 128:(c + 1) * 128], rhs=ab[:, fi, tc * 512:(tc + 1) * 512],
                                                                              start=(fi == 0), stop=(fi == 3)),
                         reads=[('w', sd[fi // 2]), ('act', g % 2, fi, tc)], writes=[('ps', b)])
                add_evac(c, tc, b)
    S.barrier()

    ostg = [zf(0, 8), zf(8, 16)]
    gfT = zf(16, 24)
    S.dma('sp', gfT, gfb_d, writes=['gfT'], sem='gfb')
    for blk in range(8):
        i = blk % 2
        for g in range(4):
            b = bank('tp', [0, 1])
            for j in range(4):
                c = 4 * g + j
                S.op('pe', lambda e, c=c, j=j, b=b, blk=blk: e.transpose(PB[b][:, j * 128:(j + 1) * 128], xT[:, c, blk * 128:(blk + 1) * 128], ident),
                     reads=[('xT', c, blk // 4), 'cst'], writes=[('ps', b)])
            evac_copy(ostg[i][:, g * 512:(g + 1) * 512], PB[b][:, :], reads=[('ps', b)], writes=[('ostg', i)])
        junk = sqr[i][:, :, :].rearrange("p c t -> p (c t)")
        S.op('act', lambda e, i=i, junk=junk: e.activation(junk, ostg[i], AF.Square, bias=zero_c, scale=1.0, accum_out=sm[:, 12 + i:13 + i]),
             reads=[('ostg', i), 'cst'], writes=[('sq', i), ('ssq', i)])
        S.op('act', lambda e, i=i: e.activation(sm[:, 12 + i:13 + i], sm[:, 12 + i:13 + i], AF.Sqrt, bias=cst[:, C_EPS:C_EPS + 1], scale=1.0 / 2048.0),
             reads=[('ssq', i), 'cst'], writes=[('ssq', i)])
        S.op('dve', lambda e, i=i: e.reciprocal(sm[:, 12 + i:13 + i], sm[:, 12 + i:13 + i]), reads=[('ssq', i)], writes=[('ssq', i)])
        S.op('dve', lambda e, i=i: e.scalar_tensor_tensor(ostg[i], ostg[i], sm[:, 12 + i:13 + i], gfT, ALU.mult, ALU.mult),
             reads=[('ostg', i), ('ssq', i), 'gfT'], writes=[('ostg', i)])
        S.dma('sp', out_d[blk], ostg[i], reads=[('ostg', i)], sem=f"st{i}", is_output=True)
    S.emit()
    assert wstate['n'] == NSLOT, wstate['n']
    return nc, wplan


def _pack_weights(wplan, W):
    arr = np.zeros((NSLOT, 128, 4096), np.float32)
    for n, d in enumerate(wplan):
        if d[0] == 'fm2':
            v = arr[n].reshape(128, 2, 16, 128)
            for i in range(2):
                name, c0 = d[1 + i]
                v[:, i] = W[name][:, c0:c0 + 128].reshape(16, 128, 128).transpose(1, 0, 2)
        elif d[0] == 'tm':
            _, name, c0 = d
            arr[n].reshape(128, 16, 256)[:] = W[name][:, c0:c0 + 256].reshape(16, 128, 256).transpose(1, 0, 2)
        elif d[0] == 'rows2':
            _, name, rc = d
            arr[n].reshape(128, 2, 2048)[:] = W[name][rc * 128:(rc + 2) * 128, :].reshape(2, 128, 2048).transpose(1, 0, 2)
        else:
            raise ValueError(d)
    return arr


def _col16(g):
    return np.ascontiguousarray(np.asarray(g, np.float32).reshape(-1, 128).T)


def _consts(inp, s):
    c = np.zeros((128, NCST), np.float32)
    p = np.arange(128, dtype=np.float64)
    c[:, C_ID:C_ID + 128] = np.eye(128, dtype=np.float32)
    c[:, C_ONE:C_ONE + 128] = 1.0
    c[:, C_GMIX:C_GMIX + 16] = _col16(inp['norm_mix_g'][0])
    c[:, C_GX:C_GX + 16] = _col16(inp['norm_x_g'][0])
    c[:, C_GMEM:C_GMEM + 16] = _col16(inp['norm_mem_g'][0])
    c[:, C_GFFN:C_GFFN + 16] = _col16(inp['norm_ffn_g'][0])
    c[:, C_GF:C_GF + 16] = _col16(inp['norm_f_g'])
    c[:, C_GSUB:C_GSUB + 2] = _col16(inp['da_subln_g'][0])
    for i, k in enumerate(('lambda_q1', 'lambda_k1', 'lambda_q2', 'lambda_k2')):
        c[:, C_LAM + i] = np.asarray(inp[k][0], np.float32)
    c[:, C_EPS] = EPS
    for h in range(4):
        slope = 2.0 ** (-8.0 * (h + 1) / 4)
        lg = math.log(1.0 - 2.0 ** (-5.0 - h))
        for idx in range(19):
            dd = idx - 3
            v = slope * (p - 128.0 * dd)
            c[:, C_ABO + h * 19 + idx] = v
            c[:, C_ABC + h * 19 + idx] = v + (0.0 if s == 1 else -30000.0)
        for idx in range(16):
            dd = idx - 3
            c[:, C_KFAC + h * 16 + idx] = np.exp(lg * (128.0 * dd - p))
        c[:, C_QDEC + h * 512:C_QDEC + (h + 1) * 512] = (np.exp(lg * np.arange(512, dtype=np.float64)) * 128.0 ** -0.5)[None, :]
    m = np.ones((128, 512), np.float32)
    m[:, 0:128] = (np.arange(128)[None, :] >= np.arange(128)[:, None]).astype(np.float32)
    c[:, C_MASK:C_MASK + 512] = m
    return c


_CACHE = {}


def _get_program():
    if 'nc' not in _CACHE:
        _CACHE['nc'], _CACHE['wplan'] = build()
    return _CACHE['nc'], _CACHE['wplan']


def make_in_maps(inp, wplan, cores=range(8)):
    W = {k: np.asarray(inp[k][0], np.float32) for k in ('w_in', 'w_o', 'w_xq', 'w_xk', 'w_xv', 'w_xo', 'w_gate', 'w_up', 'w_down')}
    wst = _pack_weights(wplan, W)
    x = np.asarray(inp['x'], np.float32)
    mem = np.asarray(inp['mem'], np.float32)
    gfb = np.ascontiguousarray(np.broadcast_to(np.asarray(inp['norm_f_g'], np.float32)[None, :], (128, 2048)))
    gmixb = np.ascontiguousarray(np.broadcast_to(np.asarray(inp['norm_mix_g'][0], np.float32)[None, :], (128, 2048)))
    gmemb = np.ascontiguousarray(np.broadcast_to(np.asarray(inp['norm_mem_g'][0], np.float32)[None, :], (128, 2048)))
    maps = []
    for core in cores:
        b, s = core // 2, core % 2
        xa = np.zeros((16, 128, 2048), np.float32)
        if s == 1:
            xa[:] = x[b].reshape(16, 128, 2048)
        else:
            xa[8:] = x[b, :1024].reshape(8, 128, 2048)
        maps.append({"gfb": gfb, "gmixb": gmixb, "gmemb": gmemb, "xall": xa, "memb": np.ascontiguousarray(mem[b].reshape(2, 128, 2048)), "cst": _consts(inp, s), "wst": wst})
    return maps


def kernel(**inputs):
    nc, wplan = _get_program()
    in_maps = make_in_maps(inputs, wplan)
    res = run_bass_kernel_spmd(nc, in_maps, core_ids=list(range(8)))
    out = np.empty((4, 2048, 2048), np.float32)
    for core in range(8):
        b, s = core // 2, core % 2
        out[b, s * 1024:(s + 1) * 1024] = np.asarray(res.results[core]["out"], np.float32).reshape(1024, 2048)
    return out
```

```python
import math
from contextlib import ExitStack
import numpy as np
import concourse.bass as bass
import concourse.mybir as mybir
from concourse.bass_utils import run_bass_kernel_spmd

F32 = mybir.dt.float32
BF16 = mybir.dt.bfloat16
AF = mybir.ActivationFunctionType
ALU = mybir.AluOpType
ENGS = ('pe', 'act', 'dve', 'pool', 'sp')

EPS = 1e-6
NSLOT = 130
RING = 4
LAM_INIT = 0.8 - 0.6 * math.exp(0.0)

_c = 0
def _col(n):
    global _c
    s = _c
    _c += n
    return s
C_ID = _col(128)
C_ONE = _col(128)
C_GMIX = _col(16)
C_GX = _col(16)
C_GMEM = _col(16)
C_GFFN = _col(16)
C_GF = _col(16)
C_GSUB = _col(2)
C_LAM = _col(4)
C_ZERO = _col(1)
C_EPS = _col(1)
C_ABO = _col(4 * 19)
C_ABC = _col(4 * 19)
C_KFAC = _col(4 * 16)
C_QDEC = _col(4 * 512)
C_MASK = _col(512)
NCST = _c


class Op:
    __slots__ = ('eng', 'fn', 'deps', 'dma_key', 'dma_val', 'signal', 'sig_ord', 'is_output')

    def __init__(self, eng, fn):
        self.eng = eng
        self.fn = fn
        self.deps = []
        self.dma_key = None
        self.dma_val = 0
        self.signal = False
        self.sig_ord = 0
        self.is_output = False


class Sched:
    def __init__(self, nc):
        self.nc = nc
        self.ops = {e: [] for e in ENGS}
        self.bufs = {}
        self.dma_cnt = {}
        self.last_dma = {}
        self.pending = {e: [] for e in ENGS}
        self.stack = ExitStack()

    def sb(self, name, shape, dtype):
        return self.stack.enter_context(self.nc.sbuf_tensor(name, shape, dtype))

    def ps(self, name, shape, dtype):
        return self.stack.enter_context(self.nc.psum_tensor(name, shape, dtype))

    def _track(self, op, reads, writes):
        deps = list(self.pending[op.eng])
        self.pending[op.eng] = []
        for r in reads:
            b = self.bufs.get(r)
            if b is not None and b['w'] is not None:
                deps.append(b['w'])
        for w in writes:
            b = self.bufs.get(w)
            if b is not None:
                if b['w'] is not None:
                    deps.append(b['w'])
                deps.extend(b['r'].values())
        rk = op.eng if op.dma_key is None else ('dma', op.dma_key)
        for r in reads:
            b = self.bufs.setdefault(r, {'w': None, 'r': {}})
            b['r'][rk] = op
        for w in writes:
            b = self.bufs.setdefault(w, {'w': None, 'r': {}})
            b['w'] = op
            b['r'] = {}
        seen = set()
        for d in deps:
            if d is op or id(d) in seen:
                continue
            seen.add(id(d))
            if d.dma_key is None:
                if d.eng == 'pe' and op.eng == 'pe' and op.dma_key is None:
                    continue
                d.signal = True
            op.deps.append(d)

    def op(self, eng, fn, reads=(), writes=()):
        o = Op(eng, fn)
        self._track(o, reads, writes)
        self.ops[eng].append(o)
        return o

    def dma(self, eng, out, in_, reads=(), writes=(), sem=None, is_output=False, **kw):
        o = Op(eng, lambda e: e.dma_start(out=out, in_=in_, **kw))
        o.dma_key = sem
        self.dma_cnt[sem] = self.dma_cnt.get(sem, 0) + 16
        o.dma_val = self.dma_cnt[sem]
        o.is_output = is_output
        self._track(o, reads, writes)
        self.ops[eng].append(o)
        self.last_dma[sem] = o
        return o

    def barrier(self):
        deps = []
        for e in ENGS:
            for o in reversed(self.ops[e]):
                if o.dma_key is None:
                    deps.append(o)
                    break
        deps.extend(self.last_dma.values())
        for e in ENGS:
            self.pending[e] = list(deps)

    def emit(self):
        nc = self.nc
        for e in ENGS:
            k = 0
            for o in self.ops[e]:
                if o.dma_key is None and o.signal:
                    k += 1
                    o.sig_ord = k
        st = self.stack
        esem = {e: st.enter_context(nc.semaphore(f"s_{e}")) for e in ('pe', 'act', 'dve', 'pool')}
        dsem = {k: st.enter_context(nc.semaphore(f"d_{k}")) for k in self.dma_cnt}
        out_keys = set()
        for e in ENGS:
            for o in self.ops[e]:
                if o.is_output:
                    out_keys.add(o.dma_key)

        def run(eng_name, eng):
            waited = {}
            for o in self.ops[eng_name]:
                for d in o.deps:
                    if d.dma_key is not None:
                        key, sem, val = ('d', d.dma_key), dsem[d.dma_key], d.dma_val
                    else:
                        key, sem, val = ('e', d.eng), esem[d.eng], d.sig_ord
                    if waited.get(key, 0) < val:
                        eng.wait_ge(sem, val)
                        waited[key] = val
                inst = o.fn(eng)
                if o.dma_key is not None:
                    inst.then_inc(dsem[o.dma_key], 16)
                elif o.signal:
                    inst.then_inc(esem[eng_name], 1)
            if eng_name == 'sp':
                for k in sorted(out_keys):
                    eng.wait_ge(dsem[k], self.dma_cnt[k])

        with nc.Block() as block:
            @block.sync
            def _(e):
                run('sp', e)

            @block.tensor
            def _(e):
                run('pe', e)

            @block.scalar
            def _(e):
                run('act', e)

            @block.vector
            def _(e):
                run('dve', e)

            @block.gpsimd
            def _(e):
                run('pool', e)
        st.close()


def build(stage=99, dbg=None):
    nc = bass.Bass("TRN2", target_bir_lowering=False)
    xall = nc.dram_tensor("xall", [16, 128, 2048], F32, kind="ExternalInput").ap()
    memb = nc.dram_tensor("memb", [2, 128, 2048], F32, kind="ExternalInput").ap()
    cst_d = nc.dram_tensor("cst", [128, NCST], F32, kind="ExternalInput").ap()
    wst = nc.dram_tensor("wst", [NSLOT, 128, 4096], F32, kind="ExternalInput").ap()
    gfb_d = nc.dram_tensor("gfb", [128, 2048], F32, kind="ExternalInput").ap()
    gmixb_d = nc.dram_tensor("gmixb", [128, 2048], F32, kind="ExternalInput").ap()
    gmemb_d = nc.dram_tensor("gmemb", [128, 2048], F32, kind="ExternalInput").ap()
    out_d = nc.dram_tensor("out", [8, 128, 2048], F32, kind="ExternalOutput").ap()
    dbg_d = {}
    if dbg:
        for name, shp in dbg.items():
            dbg_d[name] = nc.dram_tensor("dbg_" + name, list(shp), F32, kind="ExternalOutput").ap()

    S = Sched(nc)
    cst = S.sb("cst_sb", [128, NCST], F32)
    onesb = S.sb("onesb", [128, 128], BF16)
    maskb = S.sb("maskb", [128, 512], BF16)
    identb = S.sb("identb", [128, 128], BF16)
    ring = [S.sb(f"ring{i}", [128, 4096], BF16) for i in range(RING)]
    X = S.sb("X", [128, 16384], F32)
    Y = S.sb("Y", [128, 16, 1024], BF16)
    Z = S.sb("Z", [128, 8192], F32)
    sqr = [S.sb(f"sq{i}", [128, 16, 128], BF16) for i in range(2)]
    rsb = [S.sb(f"rs{i}", [128, 128], F32) for i in range(2)]
    ptr = [S.sb(f"pt{i}", [128, 512], BF16) for i in range(4)]
    rcp = [S.sb(f"rcp{i}", [128, 512], F32) for i in range(2)]
    At = S.sb("At", [128, 4, 512], F32)
    T2 = [S.sb(f"t2{i}", [128, 512], F32) for i in range(2)]
    rs512 = S.sb("rs512", [128, 512], F32)
    sq512 = S.sb("sq512", [128, 2, 512], BF16)
    sm = S.sb("sm", [128, 16], F32)
    PB = [S.ps(f"pb{i}", [128, 512], F32) for i in range(8)]

    def zf(a, b):
        return Z[:, a * 256:b * 256]

    def zb(a, b):
        return Z[:, a * 256:b * 256].bitcast(BF16)

    ident = cst[:, C_ID:C_ID + 128]
    onesf = cst[:, C_ONE:C_ONE + 128]
    zero_c = cst[:, C_ZERO:C_ZERO + 1]

    hA = X[:, :].bitcast(BF16).rearrange("p (c t) -> p c t", c=16)
    xT = X[:, :].rearrange("p (c t) -> p c t", c=16)

    wplan = []
    wstate = {'n': 0}

    def wload(desc):
        n = wstate['n']
        wstate['n'] += 1
        wplan.append(desc)
        s = n % RING
        S.dma('pool', ring[s][:, :], wst[n], writes=[('w', s)], sem=f"w{s}", max_dma_last_dim=2048)
        return s

    def w_fm(s):
        return ring[s][:, :].rearrange("p (i k n) -> p i k n", i=2, k=16)

    def w_tm(s):
        return ring[s][:, :].rearrange("p (k n) -> p k n", k=16)

    def w_rows(s):
        return ring[s][:, :].rearrange("p (i n) -> p i n", i=2)

    rot = {}

    def bank(group, banks):
        i = rot.get(group, 0)
        rot[group] = i + 1
        return banks[i % len(banks)]

    evac_flip = {'i': 0}

    def evac_copy(out_ap, in_ap, reads, writes, eng=None):
        if eng is None:
            evac_flip['i'] += 1
            eng = 'dve' if evac_flip['i'] % 3 else 'act'
        if eng == 'dve':
            S.op('dve', lambda e: e.tensor_copy(out_ap, in_ap), reads=reads, writes=writes)
        else:
            S.op('act', lambda e: e.activation(out_ap, in_ap, AF.Copy, scale=1.0), reads=reads, writes=writes)

    S.dma('sp', cst[:, :], cst_d, writes=['cst'], sem='cst')
    S.op('dve', lambda e: e.tensor_copy(onesb[:, :], cst[:, C_ONE:C_ONE + 128]), reads=['cst'], writes=['onesb'])
    S.op('dve', lambda e: e.tensor_copy(maskb[:, :], cst[:, C_MASK:C_MASK + 512]), reads=['cst'], writes=['maskb'])
    S.op('dve', lambda e: e.tensor_copy(identb[:, :], cst[:, C_ID:C_ID + 128]), reads=['cst'], writes=['identb'])
    S.op('dve', lambda e: e.tensor_tensor(sm[:, 0:1], cst[:, C_LAM:C_LAM + 1], cst[:, C_LAM + 1:C_LAM + 2], ALU.mult),
         reads=['cst'], writes=['sm01'])
    S.op('dve', lambda e: e.tensor_tensor(sm[:, 1:2], cst[:, C_LAM + 2:C_LAM + 3], cst[:, C_LAM + 3:C_LAM + 4], ALU.mult),
         reads=['cst'], writes=['sm01'])
    sp3 = S.sb("sp3", [128, 3, 2], BF16)
    S.op('dve', lambda e: e.tensor_copy(sp3[:, 0, :], sm[:, 0:2]), reads=['sm01'], writes=['sp3a'])
    S.op('dve', lambda e: e.tensor_tensor(sm[:, 8:10], sm[:, 0:2], sp3[:, 0, :], ALU.subtract), reads=['sm01', 'sp3a'], writes=['smr1'])
    S.op('dve', lambda e: e.tensor_copy(sp3[:, 1, :], sm[:, 8:10]), reads=['smr1'], writes=['sp3b'])
    S.op('dve', lambda e: e.tensor_tensor(sm[:, 10:12], sm[:, 8:10], sp3[:, 1, :], ALU.subtract), reads=['smr1', 'sp3b'], writes=['smr2'])
    S.op('dve', lambda e: e.tensor_copy(sp3[:, 2, :], sm[:, 10:12]), reads=['smr2'], writes=['sp3c'])
    for q in range(3):
        S.op('pe', lambda e, q=q: e.matmul(PB[7][:, 0:2], lhsT=onesb[:, :], rhs=sp3[:, q, :], start=(q == 0), stop=(q == 2)),
             reads=['onesb', 'sp3a', 'sp3b', 'sp3c'], writes=[('ps', 7)])
    S.op('act', lambda e: e.activation(sm[:, 2:4], PB[7][:, 0:2], AF.Exp, bias=zero_c, scale=1.0),
         reads=[('ps', 7), 'cst'], writes=['sm23'])
    S.op('dve', lambda e: e.tensor_tensor(sm[:, 4:5], sm[:, 2:3], sm[:, 3:4], ALU.subtract), reads=['sm23'], writes=['sm4'])
    S.op('dve', lambda e: e.tensor_scalar(sm[:, 5:6], sm[:, 4:5], -1.0, -LAM_INIT, ALU.mult, ALU.add), reads=['sm4'], writes=['neglam'])
    S.op('dve', lambda e: e.tensor_scalar(sm[:, 6:8], cst[:, C_GSUB:C_GSUB + 2], 1.0 - LAM_INIT, None, ALU.mult), reads=['cst'], writes=['gsubs'])
    neglam = sm[:, 5:6]

    nb = {'i': 0}

    def norm_block(src, src_keys, gc, dst, dst_keys, D=2048.0):
        i = nb['i'] % 2
        nb['i'] += 1
        sq, rs = sqr[i], rsb[i]
        ss = bank('ss', [7, 6])
        S.op('act', lambda e: e.activation(sq[:, :, :], src, AF.Square, bias=zero_c, scale=1.0),
             reads=list(src_keys) + ['cst'], writes=[('sq', i)])
        for c in range(16):
            S.op('pe', lambda e, c=c: e.matmul(PB[ss][:, 0:128], lhsT=onesb[:, :], rhs=sq[:, c, :], start=(c == 0), stop=(c == 15)),
                 reads=[('sq', i), 'onesb'], writes=[('ps', ss)])
        S.op('act', lambda e: e.activation(rs[:, :], PB[ss][:, 0:128], AF.Sqrt, bias=cst[:, C_EPS:C_EPS + 1], scale=1.0 / D),
             reads=[('ps', ss), 'cst'], writes=[('rs', i)])
        S.op('dve', lambda e: e.reciprocal(rs[:, :], rs[:, :]), reads=[('rs', i)], writes=[('rs', i)])
        S.op('dve', lambda e: e.tensor_tensor(src, src, rs[:, :].unsqueeze(1).broadcast_to([128, 16, 128]), ALU.mult),
             reads=list(src_keys) + [('rs', i)], writes=list(src_keys))
        S.op('dve', lambda e: e.tensor_tensor(dst, src, cst[:, gc:gc + 16].unsqueeze(2).broadcast_to([128, 16, 128]), ALU.mult),
             reads=list(src_keys) + ['cst'], writes=list(dst_keys))

    def norm_tc(tc, gc, D=2048.0):
        ss = bank('ss', [7, 6])
        for q in range(4):
            i = nb['i'] % 2
            nb['i'] += 1
            sq = sqr[i][:, :, :].rearrange("p c t -> p (c t)").rearrange("p (a t) -> p a t", a=4)
            S.op('act', lambda e, sq=sq, q=q: e.activation(sq, xT[:, 4 * q:4 * q + 4, tc * 512:(tc + 1) * 512], AF.Square, bias=zero_c, scale=1.0),
                 reads=[('xT', c, tc) for c in range(4 * q, 4 * q + 4)] + ['cst'], writes=[('sq', i)])
            for a in range(4):
                S.op('pe', lambda e, sq=sq, a=a, q=q: e.matmul(PB[ss][:, :], lhsT=onesb[:, :], rhs=sq[:, a, :], start=(q == 0 and a == 0), stop=(q == 3 and a == 3)),
                     reads=[('sq', i), 'onesb'], writes=[('ps', ss)])
        S.op('act', lambda e: e.activation(rs512[:, :], PB[ss][:, :], AF.Sqrt, bias=cst[:, C_EPS:C_EPS + 1], scale=1.0 / D),
             reads=[('ps', ss), 'cst'], writes=['rs512'])
        S.op('dve', lambda e: e.reciprocal(rs512[:, :], rs512[:, :]), reads=['rs512'], writes=['rs512'])
        for c in range(16):
            S.op('dve', lambda e, c=c: e.scalar_tensor_tensor(Y[:, c, tc * 512:(tc + 1) * 512], xT[:, c, tc * 512:(tc + 1) * 512], cst[:, gc + c:gc + c + 1], rs512[:, :],
                                                              ALU.mult, ALU.mult),
                 reads=[('xT', c, tc), 'rs512', 'cst'], writes=[('Y', c, tc)])

    def tm_norm_T(src_d, stg, stg_key, k, gT, gT_key, dst, dst_keys, ldsem):
        sgb = sqr[k][:, :, :].rearrange("p c t -> p (c t)")
        col = sm[:, 12 + k:13 + k]
        S.dma('sp', stg, src_d, writes=[stg_key], sem=ldsem)
        S.op('act', lambda e: e.activation(sgb, stg, AF.Square, bias=zero_c, scale=1.0, accum_out=col),
             reads=[stg_key, 'cst'], writes=[('sq', k), ('ssq', k)])
        S.op('act', lambda e: e.activation(col, col, AF.Sqrt, bias=cst[:, C_EPS:C_EPS + 1], scale=1.0 / 2048.0),
             reads=[('ssq', k), 'cst'], writes=[('ssq', k)])
        S.op('dve', lambda e: e.reciprocal(col, col), reads=[('ssq', k)], writes=[('ssq', k)])
        S.op('dve', lambda e: e.scalar_tensor_tensor(sgb, stg, col, gT, ALU.mult, ALU.mult),
             reads=[stg_key, ('ssq', k), gT_key], writes=[('sq', k)])
        for g in range(4):
            b = bank('tp', [0, 1])
            pbv = PB[b][:, 0:256].bitcast(BF16)
            for j in range(4):
                c = 4 * g + j
                S.op('pe', lambda e, c=c, j=j, pbv=pbv: e.transpose(pbv[:, j * 128:(j + 1) * 128], sgb[:, c * 128:(c + 1) * 128], identb[:, :]),
                     reads=[('sq', k), 'identb'], writes=[('ps', b)])
            evac_copy(dst[:, 4 * g:4 * g + 4, :], pbv.rearrange("p (a t) -> p a t", a=4), reads=[('ps', b)], writes=list(dst_keys))

    def load_T(src_d, stg, stg_key, dst, dst_keys):
        S.dma('sp', stg, src_d, writes=[stg_key], sem=f"ld_{stg_key[1]}")
        for g in range(4):
            b = bank('tp', [0, 1])
            for j in range(4):
                c = 4 * g + j
                S.op('pe', lambda e, c=c, j=j, b=b: e.transpose(PB[b][:, j * 128:(j + 1) * 128], stg[:, c * 128:(c + 1) * 128], ident),
                     reads=[stg_key, 'cst'], writes=[('ps', b)])
            evac_copy(dst[:, 4 * g:4 * g + 4, :], PB[b][:, :].rearrange("p (a t) -> p a t", a=4), reads=[('ps', b)], writes=list(dst_keys))

    def dump(name, ap, reads):
        if name in dbg_d:
            S.dma('sp', dbg_d[name], ap, reads=reads, sem='dbg_' + name, is_output=True)

    stg = [zf(0, 8), zf(8, 16)]
    stg3 = stg + [zf(24, 32)]
    gmixT = zf(16, 24)
    S.dma('sp', gmixT, gmixb_d, writes=['gmixT'], sem='gmixb')
    for blk in range(16):
        i = blk % 3
        tm_norm_T(xall[blk], stg3[i], ('stg', i), blk % 2, gmixT, 'gmixT', hA[:, :, blk * 128:(blk + 1) * 128], [('hA', blk)], f"ld_{i}")
    if 'hA' in dbg_d:
        S.barrier()
        S.op('dve', lambda e: e.tensor_copy(Z[:, 0:4096].rearrange("p (c t) -> p c t", c=2), hA[:, 0:2, :]), reads=[('hA', b) for b in range(16)], writes=['zdbg'])
        dump('hA', Z[:, 0:4096], ['zdbg'])
    S.barrier()
    if stage < 1:
        S.emit()
        return nc, wplan

    B1 = zb(0, 4).rearrange("p (m t) -> p m t", m=2)
    B2 = zb(4, 12)
    B3 = zb(12, 20).rearrange("p (b e) -> p b e", b=16)
    hkeys_tc = lambda tc: [('hA', 4 * tc + q) for q in range(4)]
    SC128 = 128.0 ** -0.5

    def proj_fm(wv_, tcs, evac):
        for tc in tcs:
            b = bank('pj', [0, 1])
            for kc in range(16):
                S.op('pe', lambda e, wv_=wv_, kc=kc, tc=tc, b=b: e.matmul(PB[b][:, :], lhsT=wv_[0][:, kc, :], rhs=hA[:, kc, tc * 512:(tc + 1) * 512],
                                                                start=(kc == 0), stop=(kc == 15)),
                     reads=[wv_[1]] + hkeys_tc(tc), writes=[('ps', b)])
            evac(tc, b)

    def proj_v(s):
        wv_ = w_tm(s)
        for bp in range(8):
            b = bank('pj', [0, 1])
            for i in range(2):
                blk = 2 * bp + i
                for kc in range(16):
                    S.op('pe', lambda e, kc=kc, blk=blk, b=b, i=i: e.matmul(PB[b][:, i * 256:(i + 1) * 256], lhsT=hA[:, kc, blk * 128:(blk + 1) * 128],
                                                                        rhs=wv_[:, kc, :], start=(kc == 0), stop=(kc == 15)),
                         reads=[('w', s), ('hA', blk)], writes=[('ps', b)])
            evac_copy(B3[:, 2 * bp:2 * bp + 2, :], PB[b][:, :].rearrange("p (a t) -> p a t", a=2), reads=[('ps', b)],
                      writes=[('B3', 2 * bp), ('B3', 2 * bp + 1)])

    def run_attention(groups):
        units = [(g, j) for g in groups for j in range(8 + 4 * g['c'] + 4)]
        state = {}

        def qk(g, j):
            c = g['c']
            r = max(0, j - (8 + 4 * c))
            n0 = 128 * r
            N = 512 - n0
            sb_ = bank('st', [2, 3])
            kap, kkey = g['kT_of'](j)
            qap, qkey = g['q_of'](c, n0)
            S.op('pe', lambda e, sb_=sb_, kap=kap, qap=qap, N=N: e.matmul(PB[sb_][:, 0:N], lhsT=kap, rhs=qap, start=True, stop=True),
                 reads=[kkey, qkey], writes=[('ps', sb_)])
            pi = bank('pt', [0, 1, 2, 3])
            pt = ptr[pi]
            g['evac_pt'](c, j, r, n0, N, sb_, pt, pi)
            if j >= 8 + 4 * c:
                S.op('dve', lambda e, pt=pt, n0=n0: e.tensor_tensor(pt[:, n0:n0 + 128], pt[:, n0:n0 + 128], maskb[:, 0:128], ALU.mult),
                     reads=[('pt', pi), 'maskb'], writes=[('pt', pi)])
            state[(id(g), j)] = (pt, pi, n0)

        def pv(g, j):
            pt, pi, n0 = state.pop((id(g), j))
            last = 8 + 4 * g['c'] + 3
            o0, o1, smb = g['banks']
            for e_ in range(2):
                ob = (o0, o1)[e_]
                S.op('pe', lambda e, e_=e_, j=j, pt=pt, n0=n0, ob=ob: e.matmul(PB[ob][:, n0:512], lhsT=B3[:, j, e_ * 128:(e_ + 1) * 128], rhs=pt[:, n0:512],
                                                                           start=(j == 0), stop=(j == last)),
                     reads=[('B3', j), ('pt', pi)], writes=[('ps', ob)])
            if g['nsum']:
                S.op('pe', lambda e, pt=pt, n0=n0, j=j, smb=smb: e.matmul(PB[smb][:, n0:512], lhsT=onesb[:, :], rhs=pt[:, n0:512], start=(j == 0), stop=(j == last)),
                     reads=['onesb', ('pt', pi)], writes=[('ps', smb)])

        deferred = []
        qk(*units[0])
        for i, (g, j) in enumerate(units):
            if i + 1 < len(units):
                qk(*units[i + 1])
            pv(g, j)
            for d in deferred:
                d[0] -= 1
            while deferred and deferred[0][0] <= 0:
                deferred.pop(0)[1]()
            if j == 8 + 4 * g['c'] + 3:
                g['after']()
                if g.get('deferred') is not None:
                    deferred.append([3, g['deferred']])
        while deferred:
            deferred.pop(0)[1]()

    def subln_stats(ab, D):
        for e_ in range(2):
            S.op('dve', lambda e, e_=e_: e.tensor_tensor(sq512[:, e_, :], At[:, ab + e_, :], At[:, ab + e_, :], ALU.mult), reads=[('At', ab + e_)], writes=[('sq512', e_)])
        sb_ = bank('st', [2, 3])
        for e_ in range(2):
            S.op('pe', lambda e, e_=e_, sb_=sb_: e.matmul(PB[sb_][:, :], lhsT=onesb[:, :], rhs=sq512[:, e_, :], start=(e_ == 0), stop=(e_ == 1)),
                 reads=['onesb', ('sq512', e_)], writes=[('ps', sb_)])
        S.op('act', lambda e, sb_=sb_: e.activation(rs512[:, :], PB[sb_][:, :], AF.Sqrt, bias=cst[:, C_EPS:C_EPS + 1], scale=1.0 / D),
             reads=[('ps', sb_), 'cst'], writes=['rs512'])
        S.op('dve', lambda e: e.reciprocal(rs512[:, :], rs512[:, :]), reads=['rs512'], writes=['rs512'])

    PSETS = [(4, 5, 6), (0, 1, 7)]

    for h in range(4):
        sq_, sk_, sv_ = None, None, None
        sq_ = wload(('fm2', ('w_in', h * 256), ('w_in', h * 256 + 128)))
        wq = w_fm(sq_)
        for m in range(2):
            proj_fm((wq[:, m], ('w', sq_)), [2, 3],
                    lambda tc, b, m=m: evac_copy(B1[:, m, (tc - 2) * 512:(tc - 1) * 512], PB[b][:, :], reads=[('ps', b)], writes=[('B1', m * 2 + tc - 2)]))
        sk_ = wload(('fm2', ('w_in', 1024 + h * 256), ('w_in', 1024 + h * 256 + 128)))
        wk = w_fm(sk_)
        for m in range(2):
            proj_fm((wk[:, m], ('w', sk_)), [0, 1, 2, 3],
                    lambda tc, b, m=m: evac_copy(B2[:, m * 2048 + tc * 512:m * 2048 + (tc + 1) * 512], PB[b][:, :], reads=[('ps', b)], writes=[('B2', m * 4 + tc)]))
        sv_ = wload(('tm', 'w_in', 2048 + h * 256))
        proj_v(sv_)
        split = (2.0 ** (-2.0 * (h + 1))) * 511 > 60.0
        groups = []
        for c in range(2):
            for m in range(2):
                def evac_exp(c, j, r, n0, N, sb_, pt, pi, m=m, h=h, split=split):
                    D = 8 + 4 * c - j
                    tab = C_ABC if j < 8 else C_ABO
                    if split:
                        for rr in range(r, 4):
                            col = tab + h * 19 + (D + rr) + 3
                            S.op('act', lambda e, rr=rr, col=col: e.activation(pt[:, rr * 128:(rr + 1) * 128], PB[sb_][:, (rr - r) * 128:(rr - r + 1) * 128],
                                                                               AF.Exp, bias=cst[:, col:col + 1], scale=SC128),
                                 reads=[('ps', sb_), 'cst'], writes=[('pt', pi)])
                    else:
                        col = tab + h * 19 + D + 3
                        S.op('act', lambda e, col=col: e.activation(pt[:, n0:512], PB[sb_][:, 0:N], AF.Exp, bias=cst[:, col:col + 1], scale=SC128),
                             reads=[('ps', sb_), 'cst'], writes=[('pt', pi)])

                pset = PSETS[(c * 2 + m) % 2]
                ab = 2 * (c % 2)

                def after(c=c, m=m, pset=pset, ab=ab):
                    rc = rcp[m]
                    S.op('dve', lambda e: e.reciprocal(rc[:, :], PB[pset[2]][:, :]), reads=[('ps', pset[2])], writes=[('rcp', m)])
                    for e_ in range(2):
                        if m == 0:
                            S.op('dve', lambda e, e_=e_: e.tensor_tensor(At[:, ab + e_, :], PB[pset[e_]][:, :], rc[:, :], ALU.mult),
                                 reads=[('ps', pset[e_]), ('rcp', m)], writes=[('At', ab + e_)])
                        else:
                            S.op('dve', lambda e, e_=e_: e.tensor_tensor(T2[e_][:, :], PB[pset[e_]][:, :], rc[:, :], ALU.mult),
                                 reads=[('ps', pset[e_]), ('rcp', m)], writes=[('T2', e_)])
                            S.op('dve', lambda e, e_=e_: e.scalar_tensor_tensor(At[:, ab + e_, :], T2[e_][:, :], neglam, At[:, ab + e_, :], ALU.mult, ALU.add),
                                 reads=[('T2', e_), ('At', ab + e_), 'neglam'], writes=[('At', ab + e_)])

                def fin(c=c, h=h, ab=ab):
                    subln_stats(ab, 256.0)
                    for e_ in range(2):
                        S.op('dve', lambda e, e_=e_: e.scalar_tensor_tensor(Y[:, h * 2 + e_, c * 512:(c + 1) * 512], At[:, ab + e_, :], sm[:, 6 + e_:7 + e_], rs512[:, :],
                                                                            ALU.mult, ALU.mult),
                             reads=[('At', ab + e_), 'gsubs', 'rs512'], writes=[('Y', h * 2 + e_, c)])

                groups.append(dict(c=c, nsum=True, banks=pset, evac_pt=evac_exp, after=after, deferred=(fin if m == 1 else None),
                                   kT_of=(lambda j, m=m: (B2[:, m * 2048 + j * 128:m * 2048 + (j + 1) * 128], ('B2', m * 4 + j // 4))),
                                   q_of=(lambda c, n0, m=m: (B1[:, m, c * 512 + n0:(c + 1) * 512], ('B1', m * 2 + c)))))
        run_attention(groups)

        sqk = wload(('fm2', ('w_in', 3072 + h * 128), ('w_in', 3584 + h * 128)))
        wqk = w_fm(sqk)
        qd = cst[:, C_QDEC + h * 512:C_QDEC + (h + 1) * 512]
        proj_fm((wqk[:, 0], ('w', sqk)), [2, 3],
                lambda tc, b, qd=qd: S.op('dve', lambda e: e.tensor_tensor(B2[:, 2048 + (tc - 2) * 512:2048 + (tc - 1) * 512], PB[b][:, :], qd, ALU.mult),
                                   reads=[('ps', b), 'cst'], writes=[('B2', 4 + tc - 2)]))
        proj_fm((wqk[:, 1], ('w', sqk)), [0, 1, 2, 3],
                lambda tc, b: evac_copy(B2[:, tc * 512:(tc + 1) * 512], PB[b][:, :], reads=[('ps', b)], writes=[('B2', tc)]))
        sg_ = wload(('fm2', ('w_in', 5120 + h * 256), ('w_in', 5120 + h * 256 + 128)))
        wg = w_fm(sg_)
        for e_ in range(2):
            proj_fm((wg[:, e_], ('w', sg_)), [2, 3],
                    lambda tc, b, e_=e_: S.op('act', lambda e: e.activation(B1[:, e_, (tc - 2) * 512:(tc - 1) * 512], PB[b][:, :], AF.Silu, bias=zero_c, scale=1.0),
                                              reads=[('ps', b), 'cst'], writes=[('B1', e_ * 2 + tc - 2)]))
        srv = wload(('tm', 'w_in', 4096 + h * 256))
        proj_v(srv)
        groups = []
        for c in range(2):
            def evac_ret(c, j, r, n0, N, sb_, pt, pi, h=h):
                D = 8 + 4 * c - j
                col = C_KFAC + h * 16 + D + 3
                S.op('dve', lambda e: e.tensor_scalar(pt[:, n0:512], PB[sb_][:, 0:N], cst[:, col:col + 1], None, ALU.mult),
                     reads=[('ps', sb_), 'cst'], writes=[('pt', pi)])

            pset = PSETS[c % 2]
            ab = 2 * (c % 2)

            def after(pset=pset, ab=ab):
                for e_ in range(2):
                    evac_copy(At[:, ab + e_, :], PB[pset[e_]][:, :], reads=[('ps', pset[e_])], writes=[('At', ab + e_)])

            def fin(c=c, h=h, ab=ab):
                subln_stats(ab, 256.0)
                for e_ in range(2):
                    S.op('dve', lambda e, e_=e_: e.tensor_tensor(T2[e_][:, :], At[:, ab + e_, :], rs512[:, :], ALU.mult),
                         reads=[('At', ab + e_), 'rs512'], writes=[('T2', e_)])
                    S.op('dve', lambda e, e_=e_: e.tensor_tensor(Y[:, 8 + h * 2 + e_, c * 512:(c + 1) * 512], T2[e_][:, :], B1[:, e_, c * 512:(c + 1) * 512], ALU.mult),
                         reads=[('T2', e_), ('B1', e_ * 2 + c)], writes=[('Y', 8 + h * 2 + e_, c)])

            groups.append(dict(c=c, nsum=False, banks=pset, evac_pt=evac_ret, after=after, deferred=fin,
                               kT_of=(lambda j: (B2[:, j * 128:(j + 1) * 128], ('B2', j // 4))),
                               q_of=(lambda c, n0: (B2[:, 2048 + c * 512 + n0:2048 + (c + 1) * 512], ('B2', 4 + c)))))
        run_attention(groups)
    if 'Y' in dbg_d:
        S.barrier()
        S.op('dve', lambda e: e.tensor_copy(Z[:, :].rearrange("p (c t) -> p c t", c=16), Y[:, :, 0:512]), reads=[('Y', c, 0) for c in range(16)], writes=['zdbg'])
        dump('Y', Z[:, :], ['zdbg'])
    S.barrier()
    if stage < 2:
        S.emit()
        return nc, wplan

    xkeys_blk = lambda blk: [('xT', c, blk // 4) for c in range(16)]
    for blk in range(8):
        i = blk % 2
        load_T(xall[8 + blk], stg[i], ('stg', i), xT[:, :, blk * 128:(blk + 1) * 128], xkeys_blk(blk))
    hmT = zb(16, 24).rearrange("p (c t) -> p c t", c=16)
    stg_m = zf(24, 32)
    gmemT = At[:, :, :].rearrange("p c t -> p (c t)")
    S.dma('sp', gmemT, gmemb_d, writes=[('At', q) for q in range(4)], sem='gmemb')
    for mb in range(2):
        tm_norm_T(memb[mb], stg_m, ('stgm', 0), mb, gmemT, ('At', 0), hmT[:, :, mb * 128:(mb + 1) * 128], [('hm', mb)], "ld_m")

    def add_evac(c, tc, b):
        S.op('dve', lambda e: e.tensor_tensor(xT[:, c, tc * 512:(tc + 1) * 512], PB[b][:, :], xT[:, c, tc * 512:(tc + 1) * 512], ALU.add),
             reads=[('ps', b), ('xT', c, tc)], writes=[('xT', c, tc)])

    for sp_ in range(8):
        s = wload(('fm2', ('w_o', sp_ * 256), ('w_o', sp_ * 256 + 128)))
        wv_ = w_fm(s)
        for i in range(2):
            c = 2 * sp_ + i
            for tc in range(2):
                b = bank('pj', [0, 1])
                for kc in range(16):
                    S.op('pe', lambda e, wv_=wv_, kc=kc, tc=tc, b=b, i=i: e.matmul(PB[b][:, :], lhsT=wv_[:, i, kc, :], rhs=Y[:, kc, tc * 512:(tc + 1) * 512],
                                                                        start=(kc == 0), stop=(kc == 15)),
                         reads=[('w', s), ('Y', kc, tc)], writes=[('ps', b)])
                add_evac(c, tc, b)

    def norm_x_to_Y(gc):
        for tc in range(2):
            norm_tc(tc, gc)

    if 'x1' in dbg_d:
        S.barrier()
        dump('x1', X[:, 0:8192], [('xT', c, tc) for c in range(16) for tc in range(2)])
    if stage < 3:
        S.barrier()
        S.emit()
        return nc, wplan

    norm_x_to_Y(C_GX)
    S.barrier()
    xq = zb(0, 8).rearrange("p (c t) -> p c t", c=4)
    xo = zb(8, 16).rearrange("p (c t) -> p c t", c=4)
    xk = zb(24, 26).rearrange("p (c t) -> p c t", c=4)
    xv = zb(26, 28).rearrange("p (b e) -> p b e", b=2)
    SC512 = 512.0 ** -0.5
    for h in range(4):
        for half in range(2):
            s = wload(('fm2', ('w_xk', h * 512 + half * 256), ('w_xk', h * 512 + half * 256 + 128)))
            wv_ = w_fm(s)
            for i in range(2):
                ch = half * 2 + i
                b = bank('pj', [0, 1])
                for kc in range(16):
                    S.op('pe', lambda e, wv_=wv_, kc=kc, b=b, i=i: e.matmul(PB[b][:, 0:256], lhsT=wv_[:, i, kc, :], rhs=hmT[:, kc, :], start=(kc == 0), stop=(kc == 15)),
                         reads=[('w', s), ('hm', 0), ('hm', 1)], writes=[('ps', b)])
                evac_copy(xk[:, ch, :], PB[b][:, 0:256], reads=[('ps', b)], writes=[('xk', ch)])
        for half in range(2):
            s = wload(('tm', 'w_xv', h * 512 + half * 256))
            wv_ = w_tm(s)
            for mb in range(2):
                b = bank('pj', [0, 1])
                for kc in range(16):
                    S.op('pe', lambda e, wv_=wv_, kc=kc, b=b, mb=mb: e.matmul(PB[b][:, 0:256], lhsT=hmT[:, kc, mb * 128:(mb + 1) * 128], rhs=wv_[:, kc, :], start=(kc == 0), stop=(kc == 15)),
                         reads=[('w', s), ('hm', mb)], writes=[('ps', b)])
                evac_copy(xv[:, mb, half * 256:(half + 1) * 256], PB[b][:, 0:256], reads=[('ps', b)], writes=[('xv', mb)])
        for half in range(2):
            s = wload(('fm2', ('w_xq', h * 512 + half * 256), ('w_xq', h * 512 + half * 256 + 128)))
            wv_ = w_fm(s)
            for i in range(2):
                ch = half * 2 + i
                for tc in range(2):
                    b = bank('pj', [0, 1])
                    for kc in range(16):
                        S.op('pe', lambda e, wv_=wv_, kc=kc, b=b, i=i, tc=tc: e.matmul(PB[b][:, :], lhsT=wv_[:, i, kc, :], rhs=Y[:, kc, tc * 512:(tc + 1) * 512], start=(kc == 0), stop=(kc == 15)),
                             reads=[('w', s), ('Y', kc, tc)], writes=[('ps', b)])
                    evac_copy(xq[:, ch, tc * 512:(tc + 1) * 512], PB[b][:, :], reads=[('ps', b)], writes=[('xq', ch, tc)])
        for tc in range(2):
            for mb in range(2):
                sb_ = bank('st', [2, 3])
                for ch in range(4):
                    S.op('pe', lambda e, ch=ch, sb_=sb_, mb=mb, tc=tc: e.matmul(PB[sb_][:, :], lhsT=xk[:, ch, mb * 128:(mb + 1) * 128], rhs=xq[:, ch, tc * 512:(tc + 1) * 512],
                                                                          start=(ch == 0), stop=(ch == 3)),
                         reads=[('xk', ch), ('xq', ch, tc)], writes=[('ps', sb_)])
                pi = bank('pt', [0, 1, 2, 3])
                pt = ptr[pi]
                S.op('act', lambda e, pt=pt, sb_=sb_: e.activation(pt[:, :], PB[sb_][:, :], AF.Exp, bias=zero_c, scale=SC512),
                     reads=[('ps', sb_), 'cst'], writes=[('pt', pi)])
                for e_ in range(4):
                    S.op('pe', lambda e, e_=e_, pt=pt, mb=mb: e.matmul(PB[4 + e_][:, :], lhsT=xv[:, mb, e_ * 128:(e_ + 1) * 128], rhs=pt[:, :], start=(mb == 0), stop=(mb == 1)),
                         reads=[('xv', mb), ('pt', pi)], writes=[('ps', 4 + e_)])
                S.op('pe', lambda e, pt=pt, mb=mb: e.matmul(PB[1][:, :], lhsT=onesb[:, :], rhs=pt[:, :], start=(mb == 0), stop=(mb == 1)),
                     reads=['onesb', ('pt', pi)], writes=[('ps', 1)])
            rc = rcp[tc]
            S.op('dve', lambda e, rc=rc: e.reciprocal(rc[:, :], PB[1][:, :]), reads=[('ps', 1)], writes=[('rcp', tc)])
            for e_ in range(4):
                S.op('dve', lambda e, e_=e_, rc=rc, tc=tc: e.tensor_tensor(xo[:, e_, tc * 512:(tc + 1) * 512], PB[4 + e_][:, :], rc[:, :], ALU.mult),
                     reads=[('ps', 4 + e_), ('rcp', tc)], writes=[('xo', e_, tc)])
        so = [wload(('rows2', 'w_xo', 4 * h + 2 * q)) for q in range(2)]
        wo = [w_rows(s) for s in so]
        for c in range(16):
            for tc in range(2):
                b = bank('pj', [0, 1])
                for e_ in range(4):
                    S.op('pe', lambda e, wo=wo, e_=e_, c=c, tc=tc, b=b: e.matmul(PB[b][:, :], lhsT=wo[e_ // 2][:, e_ % 2, c * 128:(c + 1) * 128], rhs=xo[:, e_, tc * 512:(tc + 1) * 512],
                                                                        start=(e_ == 0), stop=(e_ == 3)),
                         reads=[('w', so[e_ // 2]), ('xo', e_, tc)], writes=[('ps', b)])
                add_evac(c, tc, b)
    if 'x2' in dbg_d:
        S.barrier()
        dump('x2', X[:, 0:8192], [('xT', c, tc) for c in range(16) for tc in range(2)])
    S.barrier()
    if stage < 4:
        S.emit()
        return nc, wplan

    norm_x_to_Y(C_GFFN)
    act = [zb(0, 8).rearrange("p (f t) -> p f t", f=4), zb(8, 16).rearrange("p (f t) -> p f t", f=4)]
    for g in range(11):
        ab = act[g % 2]
        for fi in range(4):
            f = 4 * g + fi
            s = wload(('fm2', ('w_gate', f * 128), ('w_up', f * 128)))
            wv_ = w_fm(s)
            for tc in range(2):
                bg = bank('gu', [0, 1, 2, 3])
                bu = bank('gu', [0, 1, 2, 3])
                for (i, b) in ((0, bg), (1, bu)):
                    for kc in range(16):
                        S.op('pe', lambda e, wv_=wv_, kc=kc, b=b, i=i, tc=tc: e.matmul(PB[b][:, :], lhsT=wv_[:, i, kc, :], rhs=Y[:, kc, tc * 512:(tc + 1) * 512], start=(kc == 0), stop=(kc == 15)),
                             reads=[('w', s), ('Y', kc, tc)], writes=[('ps', b)])
                ti = bank('t2', [0, 1])
                S.op('act', lambda e, ti=ti, bg=bg: e.activation(T2[ti][:, :], PB[bg][:, :], AF.Silu, bias=zero_c, scale=1.0),
                     reads=[('ps', bg), 'cst'], writes=[('T2', ti)])
                S.op('dve', lambda e, ti=ti, bu=bu, fi=fi, tc=tc, ab=ab: e.tensor_tensor(ab[:, fi, tc * 512:(tc + 1) * 512], PB[bu][:, :], T2[ti][:, :], ALU.mult),
                     reads=[('ps', bu), ('T2', ti)], writes=[('act', g % 2, fi, tc)])
        sd = [wload(('rows2', 'w_down', 4 * g + 2 * q)) for q in range(2)]
        wd = [w_rows(s) for s in sd]
        for c in range(16):
            for tc in range(2):
                b = bank('dn', [4, 5, 6, 7])
                for fi in range(4):
                    S.op('pe', lambda e, wd=wd, fi=fi, c=c, tc=tc, b=b, ab=ab: e.matmul(PB[b][:, :], lhsT=wd[fi // 2][:, fi % 2, c * 128:(c + 1) * 128], rhs=ab[:, fi, tc * 512:(tc + 1) * 512],
                                                                              start=(fi == 0), stop=(fi == 3)),
                         reads=[('w', sd[fi // 2]), ('act', g % 2, fi, tc)], writes=[('ps', b)])
                add_evac(c, tc, b)
    S.barrier()

    ostg = [zf(0, 8), zf(8, 16), zf(24, 32)]
    gfT = zf(16, 24)
    S.dma('sp', gfT, gfb_d, writes=['gfT'], sem='gfb')
    for blk in range(8):
        i = blk % 3
        for g in range(4):
            b = bank('tp', [0, 1])
            for j in range(4):
                c = 4 * g + j
                S.op('pe', lambda e, c=c, j=j, b=b, blk=blk: e.transpose(PB[b][:, j * 128:(j + 1) * 128], xT[:, c, blk * 128:(blk + 1) * 128], ident),
                     reads=[('xT', c, blk // 4), 'cst'], writes=[('ps', b)])
            evac_copy(ostg[i][:, g * 512:(g + 1) * 512], PB[b][:, :], reads=[('ps', b)], writes=[('ostg', i)])
        k = blk % 2
        junk = sqr[k][:, :, :].rearrange("p c t -> p (c t)")
        S.op('act', lambda e, i=i, k=k, junk=junk: e.activation(junk, ostg[i], AF.Square, bias=zero_c, scale=1.0, accum_out=sm[:, 12 + k:13 + k]),
             reads=[('ostg', i), 'cst'], writes=[('sq', k), ('ssq', k)])
        S.op('act', lambda e, k=k: e.activation(sm[:, 12 + k:13 + k], sm[:, 12 + k:13 + k], AF.Sqrt, bias=cst[:, C_EPS:C_EPS + 1], scale=1.0 / 2048.0),
             reads=[('ssq', k), 'cst'], writes=[('ssq', k)])
        S.op('dve', lambda e, k=k: e.reciprocal(sm[:, 12 + k:13 + k], sm[:, 12 + k:13 + k]), reads=[('ssq', k)], writes=[('ssq', k)])
        S.op('dve', lambda e, i=i, k=k: e.scalar_tensor_tensor(ostg[i], ostg[i], sm[:, 12 + k:13 + k], gfT, ALU.mult, ALU.mult),
             reads=[('ostg', i), ('ssq', k), 'gfT'], writes=[('ostg', i)])
        S.dma('sp', out_d[blk], ostg[i], reads=[('ostg', i)], sem=f"st{i}", is_output=True)
    S.emit()
    assert wstate['n'] == NSLOT, wstate['n']
    return nc, wplan


def _pack_weights(wplan, W):
    arr = np.zeros((NSLOT, 128, 4096), np.float32)
    for n, d in enumerate(wplan):
        if d[0] == 'fm2':
            v = arr[n].reshape(128, 2, 16, 128)
            for i in range(2):
                name, c0 = d[1 + i]
                v[:, i] = W[name][:, c0:c0 + 128].reshape(16, 128, 128).transpose(1, 0, 2)
        elif d[0] == 'tm':
            _, name, c0 = d
            arr[n].reshape(128, 16, 256)[:] = W[name][:, c0:c0 + 256].reshape(16, 128, 256).transpose(1, 0, 2)
        elif d[0] == 'rows2':
            _, name, rc = d
            arr[n].reshape(128, 2, 2048)[:] = W[name][rc * 128:(rc + 2) * 128, :].reshape(2, 128, 2048).transpose(1, 0, 2)
        else:
            raise ValueError(d)
    return arr


def _col16(g):
    return np.ascontiguousarray(np.asarray(g, np.float32).reshape(-1, 128).T)


def _consts(inp, s):
    c = np.zeros((128, NCST), np.float32)
    p = np.arange(128, dtype=np.float64)
    c[:, C_ID:C_ID + 128] = np.eye(128, dtype=np.float32)
    c[:, C_ONE:C_ONE + 128] = 1.0
    c[:, C_GMIX:C_GMIX + 16] = _col16(inp['norm_mix_g'][0])
    c[:, C_GX:C_GX + 16] = _col16(inp['norm_x_g'][0])
    c[:, C_GMEM:C_GMEM + 16] = _col16(inp['norm_mem_g'][0])
    c[:, C_GFFN:C_GFFN + 16] = _col16(inp['norm_ffn_g'][0])
    c[:, C_GF:C_GF + 16] = _col16(inp['norm_f_g'])
    c[:, C_GSUB:C_GSUB + 2] = _col16(inp['da_subln_g'][0])
    for i, k in enumerate(('lambda_q1', 'lambda_k1', 'lambda_q2', 'lambda_k2')):
        c[:, C_LAM + i] = np.asarray(inp[k][0], np.float32)
    c[:, C_EPS] = EPS
    for h in range(4):
        slope = 2.0 ** (-8.0 * (h + 1) / 4)
        lg = math.log(1.0 - 2.0 ** (-5.0 - h))
        for idx in range(19):
            dd = idx - 3
            v = slope * (p - 128.0 * dd)
            c[:, C_ABO + h * 19 + idx] = v
            c[:, C_ABC + h * 19 + idx] = v + (0.0 if s == 1 else -30000.0)
        for idx in range(16):
            dd = idx - 3
            c[:, C_KFAC + h * 16 + idx] = np.exp(lg * (128.0 * dd - p))
        c[:, C_QDEC + h * 512:C_QDEC + (h + 1) * 512] = (np.exp(lg * np.arange(512, dtype=np.float64)) * 128.0 ** -0.5)[None, :]
    m = np.ones((128, 512), np.float32)
    m[:, 0:128] = (np.arange(128)[None, :] >= np.arange(128)[:, None]).astype(np.float32)
    c[:, C_MASK:C_MASK + 512] = m
    return c


_CACHE = {}


def _get_program():
    if 'nc' not in _CACHE:
        _CACHE['nc'], _CACHE['wplan'] = build()
    return _CACHE['nc'], _CACHE['wplan']


def make_in_maps(inp, wplan, cores=range(8)):
    W = {k: np.asarray(inp[k][0], np.float32) for k in ('w_in', 'w_o', 'w_xq', 'w_xk', 'w_xv', 'w_xo', 'w_gate', 'w_up', 'w_down')}
    wst = _pack_weights(wplan, W)
    x = np.asarray(inp['x'], np.float32)
    mem = np.asarray(inp['mem'], np.float32)
    gfb = np.ascontiguousarray(np.broadcast_to(np.asarray(inp['norm_f_g'], np.float32)[None, :], (128, 2048)))
    gmixb = np.ascontiguousarray(np.broadcast_to(np.asarray(inp['norm_mix_g'][0], np.float32)[None, :], (128, 2048)))
    gmemb = np.ascontiguousarray(np.broadcast_to(np.asarray(inp['norm_mem_g'][0], np.float32)[None, :], (128, 2048)))
    maps = []
    for core in cores:
        b, s = core // 2, core % 2
        xa = np.zeros((16, 128, 2048), np.float32)
        if s == 1:
            xa[:] = x[b].reshape(16, 128, 2048)
        else:
            xa[8:] = x[b, :1024].reshape(8, 128, 2048)
        maps.append({"gfb": gfb, "gmixb": gmixb, "gmemb": gmemb, "xall": xa, "memb": np.ascontiguousarray(mem[b].reshape(2, 128, 2048)), "cst": _consts(inp, s), "wst": wst})
    return maps


def kernel(**inputs):
    nc, wplan = _get_program()
    in_maps = make_in_maps(inputs, wplan)
    res = run_bass_kernel_spmd(nc, in_maps, core_ids=list(range(8)))
    out = np.empty((4, 2048, 2048), np.float32)
    for core in range(8):
        b, s = core // 2, core % 2
        out[b, s * 1024:(s + 1) * 1024] = np.asarray(res.results[core]["out"], np.float32).reshape(1024, 2048)
    return out
```
